# Optimizing a Trainium2 kernel written in Bass

```python
import math
import jax
import jax.numpy as jnp
from jax import lax
import numpy as np

D_MODEL = 2048
BATCH = 8
SEQ = 4096
DEPTH = 4

CTX_LEN = 256
GRID_W = 64
EPS = 1e-6
N_MOD = 6
N_BRANCH = 3

DA_HEADS = 4
DA_DIM = 128
DA_WIDTH = DA_HEADS * 2 * DA_DIM
DA_SCALE = DA_DIM ** -0.5
Q_BLOCK = 128
ROPE_BASE = 10000.0

RNN_WIDTH = 1024
RNN_BLOCKS = 8
RNN_BLOCK = RNN_WIDTH // RNN_BLOCKS
LRU_C = 8.0
CONV_W = 4

ML_HEADS = 4
ML_DIM = 256
ML_WIDTH = ML_HEADS * ML_DIM
ML_GATES = 2 * 2 * ML_HEADS
ML_CHUNK = 128

D_FF = ((8 * D_MODEL + 3 * 256 - 1) // (3 * 256)) * 256

IN_SPLITS = (DA_WIDTH, DA_WIDTH, DA_WIDTH, RNN_WIDTH, RNN_WIDTH,
             ML_WIDTH, ML_WIDTH, ML_WIDTH, ML_WIDTH, ML_GATES, N_BRANCH * D_MODEL)
N_IN = sum(IN_SPLITS)

kernel_name = 'hybrid_diffattn_rglru_mlstm_prefix_dit'


def rms_norm(x, g):
    xf = x.astype(jnp.float32)
    y = xf * lax.rsqrt(jnp.mean(jnp.square(xf), axis=-1, keepdims=True) + EPS)
    return (y * g.astype(jnp.float32)).astype(x.dtype)


def modulate(x, g, shift, scale):
    return rms_norm(x, g) * (1 + scale) + shift


def split_inputs(u):
    cuts = []
    acc = 0
    for w in IN_SPLITS[:-1]:
        acc += w
        cuts.append(acc)
    return jnp.split(u, cuts, axis=-1)


def dwconv(x, w, b):
    ch = x.shape[-1]
    lo = (CONV_W - 1) // 2
    hi = CONV_W - 1 - lo
    y = lax.conv_general_dilated(x, w.astype(x.dtype)[:, None, :], window_strides=(1,),
                                 padding=[(lo, hi)], dimension_numbers=('NWC', 'WIO', 'NWC'),
                                 feature_group_count=ch)
    return y + b.astype(x.dtype)


def axial_rope_tables(n):
    rows = n // GRID_W
    row = jnp.repeat(jnp.arange(rows, dtype=jnp.float32), GRID_W)
    col = jnp.tile(jnp.arange(GRID_W, dtype=jnp.float32), rows)
    quarter = DA_DIM // 4
    inv = ROPE_BASE ** (-jnp.arange(quarter, dtype=jnp.float32) / quarter)
    ar = row[:, None] * inv
    ac = col[:, None] * inv
    ang = jnp.concatenate([ar, ar, ac, ac], axis=-1)
    return jnp.cos(ang), jnp.sin(ang)


def apply_axial_rope(x, cos, sin):
    cs = cos[:, None, None, :].astype(x.dtype)
    sn = sin[:, None, None, :].astype(x.dtype)
    p0, p1, p2, p3 = jnp.split(x, 4, axis=-1)
    return x * cs + jnp.concatenate([-p1, p0, -p3, p2], axis=-1) * sn


def da_heads(u, g):
    return rms_norm(u.reshape(*u.shape[:-1], DA_HEADS, 2, DA_DIM), g)


def to_heads(t):
    return t.transpose(0, 2, 1, 3, 4)


def da_values(u):
    b, n, _ = u.shape
    return u.reshape(b, n, DA_HEADS, 2 * DA_DIM).transpose(0, 2, 1, 3)


def diff_attention_scores(q, k, v, lam):
    s = jnp.einsum('bhqmd,bhkmd->bhmqk', q, k).astype(jnp.float32) * DA_SCALE
    p = jax.nn.softmax(s, axis=-1)
    w = p[:, :, 0] - lam * p[:, :, 1]
    return jnp.einsum('bhqk,bhkv->bhqv', w.astype(v.dtype), v)


def latent_diff_attention(q, k, v, lam):
    b, h, n = q.shape[:3]
    nb = n // Q_BLOCK
    qb = jnp.moveaxis(q.reshape(b, h, nb, Q_BLOCK, 2, DA_DIM), 2, 0)
    o = lax.map(lambda qq: diff_attention_scores(qq, k, v, lam), qb)
    return jnp.moveaxis(o, 0, 2).reshape(b, h, n, 2 * DA_DIM)


def da_output(o, g, lam_init):
    b, h, n, _ = o.shape
    o = rms_norm(o, g) * (1.0 - lam_init)
    return o.transpose(0, 2, 1, 3).reshape(b, n, DA_WIDTH)


def linear_scan(a, b, h0, reverse):
    if reverse:
        a = jnp.flip(a, axis=1)
        b = jnp.flip(b, axis=1)

    def combine(e1, e2):
        return e1[0] * e2[0], e2[0] * e1[1] + e2[1]

    a_cum, b_cum = lax.associative_scan(combine, (a, b), axis=1)
    h = a_cum * h0[:, None, :] + b_cum
    return jnp.flip(h, axis=1) if reverse else h


def rglru_coeffs(x, wa, ba, wx, bx, lam):
    xf = x.astype(jnp.float32)
    xb = xf.reshape(*xf.shape[:-1], RNN_BLOCKS, RNN_BLOCK)
    r = jax.nn.sigmoid(jnp.einsum('bnkc,kcd->bnkd', xb, wa.astype(jnp.float32)).reshape(xf.shape)
                       + ba.astype(jnp.float32))
    i = jax.nn.sigmoid(jnp.einsum('bnkc,kcd->bnkd', xb, wx.astype(jnp.float32)).reshape(xf.shape)
                       + bx.astype(jnp.float32))
    log_a = -LRU_C * r * jax.nn.softplus(-lam.astype(jnp.float32))
    return jnp.exp(log_a), jnp.sqrt(-jnp.expm1(2.0 * log_a)) * (i * xf)


def rglru_mixer(x_c, x_l, gate_c, gate_l, conv_w, conv_b, wa, ba, wx, bx, lam, need_ctx):
    xc = dwconv(x_c, conv_w, conv_b)
    xl = dwconv(x_l, conv_w, conv_b)
    bsz = xc.shape[0]
    hs_l, hs_c = [], []
    for d, rev in ((0, False), (1, True)):
        ac, bc = rglru_coeffs(xc, wa[d], ba[d], wx[d], bx[d], lam[d])
        hc = linear_scan(ac, bc, jnp.zeros((bsz, RNN_WIDTH), jnp.float32), rev)
        h0 = hc[:, 0] if rev else hc[:, -1]
        al, bl = rglru_coeffs(xl, wa[d], ba[d], wx[d], bx[d], lam[d])
        hs_l.append(linear_scan(al, bl, h0, rev))
        hs_c.append(hc)
    y_l = ((hs_l[0] + hs_l[1]) * jax.nn.gelu(gate_l.astype(jnp.float32))).astype(x_l.dtype)
    y_c = None
    if need_ctx:
        y_c = ((hs_c[0] + hs_c[1]) * jax.nn.gelu(gate_c.astype(jnp.float32))).astype(x_c.dtype)
    return y_l, y_c


def mlstm_scan(q, k, v, ig, lf, state):
    b, h, n, _ = q.shape
    nc = n // ML_CHUNK

    def chunks(t):
        return jnp.moveaxis(t.reshape(b, h, nc, ML_CHUNK, *t.shape[3:]), 2, 0)

    tril = jnp.tril(jnp.ones((ML_CHUNK, ML_CHUNK), dtype=bool))

    def step(carry, xs):
        cmat, nvec, m = carry
        qc, kc, vc, ic, fc = xs
        bcum = jnp.cumsum(fc, axis=-1)
        g_inter = bcum + m[..., None]
        dmat = jnp.where(tril, bcum[..., :, None] - bcum[..., None, :] + ic[..., None, :], -jnp.inf)
        m_t = jnp.maximum(g_inter, jnp.max(dmat, axis=-1))
        w_inter = jnp.exp(g_inter - m_t)
        s = jnp.einsum('bhtd,bhsd->bhts', qc, kc) * jnp.exp(dmat - m_t[..., None])
        num = jnp.einsum('bhts,bhsv->bhtv', s, vc) + w_inter[..., None] * jnp.einsum('bhvd,bhtd->bhtv', cmat, qc)
        den = jnp.sum(s, axis=-1) + w_inter * jnp.einsum('bhd,bhtd->bht', nvec, qc)
        hout = num / jnp.maximum(jnp.abs(den), jnp.exp(-m_t))[..., None]
        decay = bcum[..., -1] + m
        w_s = bcum[..., -1:] - bcum + ic
        m_new = jnp.maximum(decay, jnp.max(w_s, axis=-1))
        ws = jnp.exp(w_s - m_new[..., None])
        sc = jnp.exp(decay - m_new)
        c_new = sc[..., None, None] * cmat + jnp.einsum('bhsv,bhsd->bhvd', ws[..., None] * vc, kc)
        n_new = sc[..., None] * nvec + jnp.einsum('bhs,bhsd->bhd', ws, kc)
        return (c_new, n_new, m_new), hout

    state, hseq = lax.scan(step, state, (chunks(q), chunks(k), chunks(v), chunks(ig), chunks(lf)))
    return jnp.moveaxis(hseq, 0, 2).reshape(b, h, n, -1), state


def mlstm_prep(q, k, v, gp, conv_w, conv_b, gate_b):
    b, n, _ = q.shape
    qk = jax.nn.silu(dwconv(jnp.concatenate([q, k], axis=-1), conv_w, conv_b))
    q, k = jnp.split(qk, 2, axis=-1)

    def heads(t):
        return t.reshape(b, n, ML_HEADS, ML_DIM).transpose(0, 2, 1, 3).astype(jnp.float32)

    g = (gp.astype(jnp.float32) + gate_b.astype(jnp.float32)).reshape(b, n, 2, 2, ML_HEADS)
    g = g.transpose(2, 3, 0, 4, 1)
    return heads(q), heads(k) * (ML_DIM ** -0.5), heads(v), g[:, 0], jax.nn.log_sigmoid(g[:, 1])


def mlstm_readout(hsum, o, norm_g):
    b, h, n, d = hsum.shape
    hn = rms_norm(hsum, norm_g.reshape(ML_HEADS, 1, ML_DIM))
    hn = hn.transpose(0, 2, 1, 3).reshape(b, n, ML_WIDTH)
    return (hn * jax.nn.sigmoid(o.astype(jnp.float32))).astype(o.dtype)


def mlstm_mixer(q_c, k_c, v_c, o_c, gp_c, q_l, k_l, v_l, o_l, gp_l, conv_w, conv_b, gate_b, norm_g, need_ctx):
    qc, kc, vc, ic, fc = mlstm_prep(q_c, k_c, v_c, gp_c, conv_w, conv_b, gate_b)
    ql, kl, vl, il, fl = mlstm_prep(q_l, k_l, v_l, gp_l, conv_w, conv_b, gate_b)
    bsz = qc.shape[0]
    init = (jnp.zeros((bsz, ML_HEADS, ML_DIM, ML_DIM), jnp.float32),
            jnp.zeros((bsz, ML_HEADS, ML_DIM), jnp.float32),
            jnp.zeros((bsz, ML_HEADS), jnp.float32))
    hs_l, hs_c = [], []
    for d, rev in ((0, False), (1, True)):
        def flip(t):
            return jnp.flip(t, axis=2) if rev else t
        hc, st = mlstm_scan(flip(qc), flip(kc), flip(vc), flip(ic[d]), flip(fc[d]), init)
        hl, _ = mlstm_scan(flip(ql), flip(kl), flip(vl), flip(il[d]), flip(fl[d]), st)
        hs_l.append(flip(hl))
        hs_c.append(flip(hc))
    y_l = mlstm_readout(hs_l[0] + hs_l[1], o_l, norm_g)
    y_c = mlstm_readout(hs_c[0] + hs_c[1], o_c, norm_g) if need_ctx else None
    return y_l, y_c


def merge_branches(gates, a, r, m, wba, wbr, wbm, wo):
    ga, gr, gm = jnp.split(gates, N_BRANCH, axis=-1)
    z = jax.nn.sigmoid(ga) * (a @ wba) + jax.nn.sigmoid(gr) * (r @ wbr) + jax.nn.sigmoid(gm) * (m @ wbm)
    return z @ wo


def swiglu_ffn(h, w1, w3, w2):
    return (jax.nn.silu(h @ w1) * (h @ w3)) @ w2


def setup_inputs(seed: int = 0) -> dict:
    key = jax.random.key(seed)
    ks = jax.random.split(key, 40)
    f32 = jnp.float32

    def nrm(k, shape, scale):
        return jax.random.normal(k, shape, f32) * scale

    def gain(k, shape):
        return 1.0 + 0.05 * jax.random.normal(k, shape, f32)

    a0 = jax.random.uniform(ks[19], (DEPTH, 2, RNN_WIDTH), f32, minval=0.9, maxval=0.999)
    p = a0 ** (1.0 / LRU_C)
    rnn_lambda = jnp.log(p) - jnp.log1p(-p)
    ib = 0.1 * jax.random.normal(ks[22], (DEPTH, 2, 1, ML_HEADS), f32)
    fb = jnp.linspace(3.0, 6.0, ML_HEADS, dtype=f32) + 0.1 * jax.random.normal(ks[32], (DEPTH, 2, 1, ML_HEADS), f32)
    ml_gate_b = jnp.concatenate([ib, fb], axis=2).reshape(DEPTH, ML_GATES)
    return {
        'x': nrm(ks[0], (BATCH, SEQ, D_MODEL), 1.0),
        'c': nrm(ks[1], (BATCH, D_MODEL), 1.0),
        'ctx': nrm(ks[2], (BATCH, CTX_LEN, D_MODEL), 1.0),
        'c_ctx': nrm(ks[3], (D_MODEL,), 1.0),
        'w_mod': nrm(ks[4], (DEPTH, D_MODEL, N_MOD * D_MODEL), 0.5 * D_MODEL ** -0.5),
        'b_mod': nrm(ks[5], (DEPTH, N_MOD * D_MODEL), 0.02),
        'norm1_g': gain(ks[6], (DEPTH, D_MODEL)),
        'norm2_g': gain(ks[7], (DEPTH, D_MODEL)),
        'w_in': nrm(ks[8], (DEPTH, D_MODEL, N_IN), D_MODEL ** -0.5),
        'attn_qnorm_g': gain(ks[9], (DEPTH, DA_DIM)),
        'attn_knorm_g': gain(ks[10], (DEPTH, DA_DIM)),
        'attn_lambda': nrm(ks[11], (DEPTH, 4, DA_DIM), 0.1),
        'attn_subln_g': gain(ks[12], (DEPTH, 2 * DA_DIM)),
        'rnn_conv_w': nrm(ks[13], (DEPTH, CONV_W, RNN_WIDTH), CONV_W ** -0.5),
        'rnn_conv_b': nrm(ks[14], (DEPTH, RNN_WIDTH), 0.02),
        'rnn_wa': nrm(ks[15], (DEPTH, 2, RNN_BLOCKS, RNN_BLOCK, RNN_BLOCK), RNN_BLOCK ** -0.5),
        'rnn_ba': nrm(ks[16], (DEPTH, 2, RNN_WIDTH), 0.02),
        'rnn_wx': nrm(ks[17], (DEPTH, 2, RNN_BLOCKS, RNN_BLOCK, RNN_BLOCK), RNN_BLOCK ** -0.5),
        'rnn_bx': nrm(ks[18], (DEPTH, 2, RNN_WIDTH), 0.02),
        'rnn_lambda': rnn_lambda,
        'ml_conv_w': nrm(ks[20], (DEPTH, CONV_W, 2 * ML_WIDTH), CONV_W ** -0.5),
        'ml_conv_b': nrm(ks[21], (DEPTH, 2 * ML_WIDTH), 0.02),
        'ml_gate_b': ml_gate_b,
        'ml_norm_g': gain(ks[23], (DEPTH, ML_WIDTH)),
        'w_branch_attn': nrm(ks[24], (DEPTH, DA_WIDTH, D_MODEL), DA_WIDTH ** -0.5),
        'w_branch_rnn': nrm(ks[25], (DEPTH, RNN_WIDTH, D_MODEL), RNN_WIDTH ** -0.5),
        'w_branch_ml': nrm(ks[26], (DEPTH, ML_WIDTH, D_MODEL), ML_WIDTH ** -0.5),
        'w_out': nrm(ks[27], (DEPTH, D_MODEL, D_MODEL), D_MODEL ** -0.5),
        'w_ffn1': nrm(ks[28], (DEPTH, D_MODEL, D_FF), D_MODEL ** -0.5),
        'w_ffn3': nrm(ks[29], (DEPTH, D_MODEL, D_FF), D_MODEL ** -0.5),
        'w_ffn2': nrm(ks[30], (DEPTH, D_FF, D_MODEL), D_FF ** -0.5),
    }


def reference(x, c, ctx, c_ctx, w_mod, b_mod, norm1_g, norm2_g, w_in, attn_qnorm_g, attn_knorm_g,
              attn_lambda, attn_subln_g, rnn_conv_w, rnn_conv_b, rnn_wa, rnn_ba, rnn_wx, rnn_bx,
              rnn_lambda, ml_conv_w, ml_conv_b, ml_gate_b, ml_norm_g, w_branch_attn, w_branch_rnn,
              w_branch_ml, w_out, w_ffn1, w_ffn3, w_ffn2):
    n = x.shape[1]
    cos, sin = axial_rope_tables(n)
    s_lat = jax.nn.silu(c)
    s_ctx = jax.nn.silu(c_ctx)[None, :]
    h, hc = x, ctx
    for l in range(DEPTH):
        need_ctx = l < DEPTH - 1
        lam_init = 0.8 - 0.6 * math.exp(-0.3 * l)
        sh1, sc1, gt1, sh2, sc2, gt2 = jnp.split((s_lat @ w_mod[l] + b_mod[l])[:, None, :], N_MOD, axis=-1)
        csh1, csc1, cgt1, csh2, csc2, cgt2 = jnp.split((s_ctx @ w_mod[l] + b_mod[l])[:, None, :], N_MOD, axis=-1)

        (aq_l, ak_l, av_l, rx_l, rg_l, mq_l, mk_l, mv_l, mo_l, mg_l, bg_l) = split_inputs(
            modulate(h, norm1_g[l], sh1, sc1) @ w_in[l])
        (aq_c, ak_c, av_c, rx_c, rg_c, mq_c, mk_c, mv_c, mo_c, mg_c, bg_c) = split_inputs(
            modulate(hc, norm1_g[l], csh1, csc1) @ w_in[l])

        lv = attn_lambda[l].astype(jnp.float32)
        lam = jnp.exp(jnp.sum(lv[0] * lv[1])) - jnp.exp(jnp.sum(lv[2] * lv[3])) + lam_init
        q_lat = to_heads(apply_axial_rope(da_heads(aq_l, attn_qnorm_g[l]), cos, sin))
        k_lat = to_heads(apply_axial_rope(da_heads(ak_l, attn_knorm_g[l]), cos, sin))
        k_ctx = to_heads(da_heads(ak_c, attn_knorm_g[l]))
        v_ctx = da_values(av_c)
        k_all = jnp.concatenate([k_lat, k_ctx], axis=2)
        v_all = jnp.concatenate([da_values(av_l), v_ctx], axis=2)
        a_l = da_output(latent_diff_attention(q_lat, k_all, v_all, lam), attn_subln_g[l], lam_init)

        r_l, r_c = rglru_mixer(rx_c, rx_l, rg_c, rg_l, rnn_conv_w[l], rnn_conv_b[l], rnn_wa[l], rnn_ba[l],
                               rnn_wx[l], rnn_bx[l], rnn_lambda[l], need_ctx)

        m_l, m_c = mlstm_mixer(mq_c, mk_c, mv_c, mo_c, mg_c, mq_l, mk_l, mv_l, mo_l, mg_l,
                               ml_conv_w[l], ml_conv_b[l], ml_gate_b[l], ml_norm_g[l], need_ctx)

        h = h + gt1 * merge_branches(bg_l, a_l, r_l, m_l, w_branch_attn[l], w_branch_rnn[l], w_branch_ml[l], w_out[l])
        h = h + gt2 * swiglu_ffn(modulate(h, norm2_g[l], sh2, sc2), w_ffn1[l], w_ffn3[l], w_ffn2[l])

        if need_ctx:
            q_ctx = to_heads(da_heads(aq_c, attn_qnorm_g[l]))
            a_c = da_output(diff_attention_scores(q_ctx, k_ctx, v_ctx, lam), attn_subln_g[l], lam_init)
            hc = hc + cgt1 * merge_branches(bg_c, a_c, r_c, m_c, w_branch_attn[l], w_branch_rnn[l], w_branch_ml[l], w_out[l])
            hc = hc + cgt2 * swiglu_ffn(modulate(hc, norm2_g[l], csh2, csc2), w_ffn1[l], w_ffn3[l], w_ffn2[l])
    return h
```

```python
import math
from contextlib import ExitStack
import numpy as np
import concourse.bass as bass
import concourse.mybir as mybir
from concourse.bass_utils import run_bass_kernel_spmd

F32 = mybir.dt.float32
BF16 = mybir.dt.bfloat16
ALU = mybir.AluOpType
AF = mybir.ActivationFunctionType
AX = mybir.AxisListType
ENGS = ("pe", "act", "dve", "pool", "sp")

D = 2048
KT = 16
NCTX = 256
DFF = 5632
N_IN = 15376
OFF = dict(aq=0, ak=1024, av=2048, rx=3072, rg=4096, mq=5120, mk=6144, mv=7168, mo=8192, mg=9216, bg=9232)
EPS = 1e-6
NEG = -1.0e30


class Buf:
    __slots__ = ("name", "lastw", "readers", "dsem")

    def __init__(self, name):
        self.name = name
        self.lastw = None
        self.readers = {}
        self.dsem = None


class DmaSem:
    __slots__ = ("h", "count")

    def __init__(self, h):
        self.h = h
        self.count = 0


class Op:
    __slots__ = ("eng", "fn", "deps", "signal", "is_dma", "dsem", "dcount", "cnt")

    def __init__(self, eng, fn):
        self.eng = eng
        self.fn = fn
        self.deps = []
        self.signal = False
        self.is_dma = False
        self.dsem = None
        self.dcount = 0
        self.cnt = 0


class Sched:
    def __init__(self, nc, stack):
        self.nc = nc
        self.stack = stack
        self.ops = {e: [] for e in ENGS}
        self.esem = {e: stack.enter_context(nc.semaphore("es_" + e)) for e in ("pe", "act", "dve", "pool")}
        self.ecnt = {e: 0 for e in ("pe", "act", "dve", "pool")}
        self.free_dsems = []
        self.all_dsems = []
        self.last_sig = {}
        self.nops = 0

    def get_dsem(self):
        if self.free_dsems:
            return self.free_dsems.pop()
        ds = DmaSem(self.stack.enter_context(self.nc.semaphore("ds%d" % len(self.all_dsems))))
        self.all_dsems.append(ds)
        return ds

    def release(self, bufs):
        for b in bufs:
            if b.dsem is not None:
                self.free_dsems.append(b.dsem)
                b.dsem = None

    def _dep(self, op, prev):
        if prev is None or prev is op:
            return
        if prev.eng == "pe" and op.eng == "pe" and not prev.is_dma:
            return
        if not prev.is_dma:
            prev.signal = True
        op.deps.append(prev)

    def _track(self, o, r, w, rkey):
        for b in r:
            self._dep(o, b.lastw)
        for b in w:
            self._dep(o, b.lastw)
            for rd in b.readers.values():
                self._dep(o, rd)
        for b in r:
            b.readers[rkey] = o
        for b in w:
            b.lastw = o
            b.readers = {}

    def op(self, eng, fn, r=(), w=()):
        o = Op(eng, fn)
        self._track(o, r, w, eng)
        self.ops[eng].append(o)
        self.nops += 1
        return o

    def dma(self, eng, out_ap, in_ap, sb, r=(), w=()):
        if sb.dsem is None:
            sb.dsem = self.get_dsem()
        ds = sb.dsem
        o = Op(eng, lambda e: e.dma_start(out=out_ap, in_=in_ap))
        o.is_dma = True
        o.dsem = ds
        ds.count += 16
        o.dcount = ds.count
        self._track(o, r, w, ("dma", id(ds)))
        self.ops[eng].append(o)
        self.nops += 1
        return o

    def barrier(self):
        lasts = []
        for e in ("pe", "act", "dve", "pool"):
            found = None
            for o in reversed(self.ops[e]):
                if not o.is_dma and o.fn is not None:
                    found = o
                    break
            if found is not None:
                found.signal = True
                self.last_sig[e] = found
            if e in self.last_sig:
                lasts.append(self.last_sig[e])
        dl = []
        for ds in self.all_dsems:
            if ds.count > 0:
                d = Op("sp", None)
                d.is_dma = True
                d.dsem = ds
                d.dcount = ds.count
                dl.append(d)
        for e in ENGS:
            b = Op(e, None)
            b.deps = list(lasts) + dl
            self.ops[e].append(b)

    def emit(self):
        nc = self.nc
        for e in ("pe", "act", "dve", "pool"):
            for o in self.ops[e]:
                if o.signal and not o.is_dma and o.fn is not None:
                    self.ecnt[e] += 1
                    o.cnt = self.ecnt[e]
        esem = self.esem
        oplists = self.ops
        self.ops = {e: [] for e in ENGS}

        def run(e, engobj):
            seen = {}
            for o in oplists[e]:
                for d in o.deps:
                    if d.is_dma:
                        key = id(d.dsem)
                        if seen.get(key, 0) < d.dcount:
                            seen[key] = d.dcount
                            engobj.wait_ge(d.dsem.h, d.dcount)
                    else:
                        if d.eng == e and e == "pe":
                            continue
                        if seen.get(d.eng, 0) < d.cnt:
                            seen[d.eng] = d.cnt
                            engobj.wait_ge(esem[d.eng], d.cnt)
                if o.fn is None:
                    continue
                ins = o.fn(engobj)
                if o.is_dma:
                    ins.then_inc(o.dsem.h, 16)
                elif o.signal:
                    ins.then_inc(esem[e], 1)

        with nc.Block() as block:
            @block.tensor
            def _(eng):
                run("pe", eng)

            @block.scalar
            def _(eng):
                run("act", eng)

            @block.vector
            def _(eng):
                run("dve", eng)

            @block.gpsimd
            def _(eng):
                run("pool", eng)

            @block.sync
            def _(eng):
                run("sp", eng)


def split_sizes(total, maxsz, mult=128):
    n = -(-total // maxsz)
    base = -(-(total // mult) // n) * mult
    out = []
    t = 0
    while t < total:
        s = min(base, total - t)
        out.append((t, s))
        t += s
    return out


def groups512(t0, n):
    out = []
    t = 0
    while t < n:
        s = min(512, n - t)
        out.append((t0 + t, s))
        t += s
    return out


def segs(t0, n):
    out = []
    if t0 < NCTX:
        m = min(n, NCTX - t0)
        out.append((t0, m, 1))
        if n > m:
            out.append((t0 + m, n - m, 0))
    else:
        out.append((t0, n, 0))
    return out


def build(NL, DEPTH, dbg=()):
    NT = NCTX + NL
    TT = NT // 128
    nc = bass.Bass("TRN2", target_bir_lowering=False)

    def din(name, shape):
        return nc.dram_tensor(name, list(shape), F32, kind="ExternalInput").ap()

    x_in = din("x", [NL, D])
    ctx_in = din("ctx", [NCTX, D])
    c_in = din("c2", [2, D])
    W = {}
    for name, shape in [("w_mod", [DEPTH, D, 6 * D]), ("b_mod", [DEPTH, 6 * D]), ("norm1_g", [DEPTH, D]),
                        ("norm2_g", [DEPTH, D]), ("w_in", [DEPTH, D, N_IN]), ("attn_qnorm_g", [DEPTH, 128]),
                        ("attn_knorm_g", [DEPTH, 128]), ("attn_lambda", [DEPTH, 4, 128]),
                        ("attn_subln_g", [DEPTH, 256]), ("rnn_conv_w", [DEPTH, 4, 1024]),
                        ("rnn_conv_b", [DEPTH, 1024]), ("rnn_wa", [DEPTH, 2, 8, 128, 128]),
                        ("rnn_ba", [DEPTH, 2, 1024]), ("rnn_wx", [DEPTH, 2, 8, 128, 128]),
                        ("rnn_bx", [DEPTH, 2, 1024]), ("rnn_lambda", [DEPTH, 2, 1024]),
                        ("ml_conv_w", [DEPTH, 4, 2048]), ("ml_conv_b", [DEPTH, 2048]), ("ml_gate_b", [DEPTH, 16]),
                        ("ml_norm_g", [DEPTH, 1024]), ("w_branch_attn", [DEPTH, 1024, D]),
                        ("w_branch_rnn", [DEPTH, 1024, D]), ("w_branch_ml", [DEPTH, 1024, D]),
                        ("w_out", [DEPTH, D, D]), ("w_ffn1", [DEPTH, D, DFF]), ("w_ffn3", [DEPTH, D, DFF]),
                        ("w_ffn2", [DEPTH, DFF, D])]:
        W[name] = din(name, shape)
    ident_in = din("k_ident", [128, 128])
    rot_in = din("k_rot", [128, 128])
    cos_in = din("k_cos", [128, NL])
    sin_in = din("k_sin", [128, NL])
    mask_in = din("k_mask", [2, 128, 128])
    sel_in = din("k_sel", [4, 512])
    out_d = nc.dram_tensor("out", [NL, D], F32, kind="ExternalOutput").ap()

    def dscr(name, shape, dt):
        kind = "ExternalOutput" if name in dbg else "Internal"
        return nc.dram_tensor(name, list(shape), dt, kind=kind).ap()

    hT = dscr("hT", [D, NT], F32)
    xmT = dscr("xmT", [D, NT], BF16)
    uT = dscr("uT", [N_IN, NT], F32)
    vtok = dscr("vtok", [NT, 3072], F32)
    qkT = dscr("qkT", [2048, NT], BF16)
    qkm = dscr("qkm", [2048, NT], BF16)
    aT = dscr("aT", [1024, NT], BF16)
    rT = dscr("rT", [1024, NT], BF16)
    mT = dscr("mT", [1024, NT], BF16)
    zT = dscr("zT", [D, NT], BF16)
    actT = dscr("actT", [DFF, NT], BF16)

    with ExitStack() as gst:
        S = Sched(nc, gst)

        uid = [0]

        def sbt(st, name, shape, dt=F32):
            uid[0] += 1
            name = "%s_u%d" % (name, uid[0])
            t = st.enter_context(nc.sbuf_tensor(name, list(shape), dt))
            return t, Buf(name)

        PS = []
        for i in range(8):
            t = gst.enter_context(nc.psum_tensor("ps%d" % i, [128, 512], F32))
            PS.append((t, Buf("ps%d" % i)))

        ident, Bident = sbt(gst, "ident", [128, 128])
        rotb, Brotb = sbt(gst, "rotb", [128, 128], BF16)
        onesb, Bonesb = sbt(gst, "onesb", [128, 128], BF16)
        ones32, Bones32 = sbt(gst, "ones32", [128, 128])
        maskt, Bmask = sbt(gst, "maskt", [128, 2, 128])
        sel, Bsel = sbt(gst, "sel", [4, 512])
        sT, BsT = sbt(gst, "sT", [128, KT, 2])
        epsc, Bepsc = sbt(gst, "epsc", [128, 1])
        modT, BmodT = sbt(gst, "modT", [128, 96, 2])
        G1, BG1 = sbt(gst, "G1", [128, KT, 2])
        G2, BG2 = sbt(gst, "G2", [128, KT, 2])
        gqk, Bgqk = sbt(gst, "gqk", [128, 2])
        nlam, Bnlam = sbt(gst, "nlam", [128, 1])
        gsub, Bgsub = sbt(gst, "gsub", [128, 256])
        rcw, Brcw = sbt(gst, "rcw", [128, 32])
        rcb, Brcb = sbt(gst, "rcb", [128, 8])
        rba, Brba = sbt(gst, "rba", [128, 16])
        rbx, Brbx = sbt(gst, "rbx", [128, 16])
        rnsp, Brnsp = sbt(gst, "rnsp", [128, 16])
        mcw, Bmcw = sbt(gst, "mcw", [128, 64])
        mcb, Bmcb = sbt(gst, "mcb", [128, 16])
        mgb, Bmgb = sbt(gst, "mgb", [4, 4])
        mng, Bmng = sbt(gst, "mng", [128, 1024])
        Bscr = Buf("dram_misc")

        with ExitStack() as st:
            rot32, Brot32 = sbt(st, "rot32", [128, 128])
            c2, Bc2 = sbt(st, "c2sb", [2, D])
            S.dma("sp", ident[:], ident_in, Bident, w=[Bident])
            S.dma("sp", rot32[:], rot_in, Brot32, w=[Brot32])
            S.dma("sp", maskt[:], mask_in.rearrange("a p t -> p a t"), Bmask, w=[Bmask])
            S.dma("sp", sel[:], sel_in, Bsel, w=[Bsel])
            S.dma("sp", c2[:], c_in, Bc2, w=[Bc2])
            S.op("dve", lambda e: e.tensor_copy(rotb[:], rot32[:]), r=[Brot32], w=[Brotb])
            S.op("dve", lambda e: e.memset(onesb[:], 1.0), w=[Bonesb])
            S.op("dve", lambda e: e.memset(ones32[:], 1.0), w=[Bones32])
            S.op("dve", lambda e: e.memset(epsc[:], EPS), w=[Bepsc])
            S.op("act", lambda e: e.activation(c2[:], c2[:], AF.Silu), r=[Bc2], w=[Bc2])
            pt, Bpt = PS[0]
            for kt in range(KT):
                S.op("pe", lambda e, kt=kt: e.transpose(pt[:, 2 * kt:2 * kt + 2], c2[0:2, kt * 128:(kt + 1) * 128],
                                                        ident[0:2, 0:2]), r=[Bc2, Bident], w=[Bpt])
            S.op("dve", lambda e: e.tensor_copy(sT[:].rearrange("p k r -> p (k r)"), pt[:, 0:2 * KT]), r=[Bpt], w=[BsT])
            xin = [sbt(st, "xin%d" % i, [128, D]) for i in range(2)]
            hst = [sbt(st, "hst%d" % i, [128, KT, 128]) for i in range(2)]
            for tt in range(TT):
                xt, Bxt = xin[tt % 2]
                ht, Bht = hst[tt % 2]
                src = ctx_in[tt * 128:(tt + 1) * 128, :] if tt < 2 else x_in[(tt - 2) * 128:(tt - 1) * 128, :]
                S.dma("sp", xt[:], src, Bxt, w=[Bxt])
                for q in range(4):
                    p, Bp = PS[1 + (tt * 4 + q) % 4]
                    for j in range(4):
                        kt = q * 4 + j
                        S.op("pe", lambda e, p=p, j=j, kt=kt, xt=xt: e.transpose(
                            p[:, j * 128:(j + 1) * 128], xt[:, kt * 128:(kt + 1) * 128], ident[:]),
                            r=[Bxt, Bident], w=[Bp])
                    eng = "dve" if q % 2 == 0 else "act"
                    if eng == "dve":
                        S.op("dve", lambda e, p=p, q=q, ht=ht: e.tensor_copy(
                            ht[:, q * 4:(q + 1) * 4, :], p[:].rearrange("p (j t) -> p j t", j=4)), r=[Bp], w=[Bht])
                    else:
                        S.op("act", lambda e, p=p, q=q, ht=ht: e.copy(
                            ht[:, q * 4:(q + 1) * 4, :], p[:].rearrange("p (j t) -> p j t", j=4)), r=[Bp], w=[Bht])
                S.dma("sp", hT[:, tt * 128:(tt + 1) * 128].rearrange("(k p) t -> p k t", p=128), ht[:], Bht, r=[Bht])
            S.barrier()
            S.emit()
            S.release([Bident, Brot32, Bmask, Bsel, Bc2] + [b for _, b in xin] + [b for _, b in hst])

        def load_T(st, dst_ap, Bdst, src_ap, R, tmpname):
            t, Bt = sbt(st, tmpname, [R, 128])
            S.dma("sp", t[:], src_ap, Bt, w=[Bt])
            p, Bp = PS[0]
            S.op("pe", lambda e: e.transpose(p[:, 0:R], t[:], ident[0:R, 0:R]), r=[Bt, Bident], w=[Bp])
            S.op("dve", lambda e: e.tensor_copy(dst_ap, p[:, 0:R]), r=[Bp], w=[Bdst])
            return Bt

        def load_rowbc(st, dst_ap, Bdst, src_ap, Fdim, tmpname):
            t, Bt = sbt(st, tmpname, [1, Fdim])
            S.dma("sp", t[:], src_ap, Bt, w=[Bt])
            for f0 in range(0, Fdim, 512):
                fn = min(512, Fdim - f0)
                p, Bp = PS[0]
                S.op("pe", lambda e, f0=f0, fn=fn: e.matmul(p[:, 0:fn], ones32[0:1, :], t[0:1, f0:f0 + fn], start=True,
                                                           stop=True), r=[Bt, Bones32], w=[Bp])
                S.op("dve", lambda e, f0=f0, fn=fn: e.tensor_copy(dst_ap[:, f0:f0 + fn], p[:, 0:fn]), r=[Bp], w=[Bdst])
            return Bt

        def phase_params(l):
            lam_init = 0.8 - 0.6 * math.exp(-0.3 * l)
            with ExitStack() as st:
                rel = []
                bm, Bbm = sbt(st, "bm", [128, 96])
                g1, Bg1 = sbt(st, "g1", [128, KT])
                g2, Bg2 = sbt(st, "g2", [128, KT])
                rlam, Brlam = sbt(st, "rlam", [128, 16])
                rel.append(load_T(st, bm[:], Bbm, W["b_mod"][l].rearrange("(m p) -> m p", p=128), 96, "t_bm"))
                rel.append(load_T(st, g1[:], Bg1, W["norm1_g"][l].rearrange("(m p) -> m p", p=128), KT, "t_g1"))
                rel.append(load_T(st, g2[:], Bg2, W["norm2_g"][l].rearrange("(m p) -> m p", p=128), KT, "t_g2"))
                rel.append(load_T(st, gqk[:, 0:1], Bgqk, W["attn_qnorm_g"][l].rearrange("(m p) -> m p", p=128), 1, "t_gq"))
                rel.append(load_T(st, gqk[:, 1:2], Bgqk, W["attn_knorm_g"][l].rearrange("(m p) -> m p", p=128), 1, "t_gk"))
                rel.append(load_T(st, rcw[:], Brcw, W["rnn_conv_w"][l].rearrange("a (k p) -> (a k) p", p=128), 32, "t_rcw"))
                rel.append(load_T(st, rcb[:], Brcb, W["rnn_conv_b"][l].rearrange("(k p) -> k p", p=128), 8, "t_rcb"))
                rel.append(load_T(st, rba[:], Brba, W["rnn_ba"][l].rearrange("a (k p) -> (a k) p", p=128), 16, "t_rba"))
                rel.append(load_T(st, rbx[:], Brbx, W["rnn_bx"][l].rearrange("a (k p) -> (a k) p", p=128), 16, "t_rbx"))
                rel.append(load_T(st, rlam[:], Brlam, W["rnn_lambda"][l].rearrange("a (k p) -> (a k) p", p=128), 16, "t_rl"))
                rel.append(load_T(st, mcw[:], Bmcw, W["ml_conv_w"][l].rearrange("a (k p) -> (a k) p", p=128), 64, "t_mcw"))
                rel.append(load_T(st, mcb[:], Bmcb, W["ml_conv_b"][l].rearrange("(k p) -> k p", p=128), 16, "t_mcb"))
                rel.append(load_rowbc(st, gsub[:], Bgsub, W["attn_subln_g"][l].rearrange("(o f) -> o f", o=1), 256, "t_gs"))
                rel.append(load_rowbc(st, mng[:], Bmng, W["ml_norm_g"][l].rearrange("(o f) -> o f", o=1), 1024, "t_mn"))
                S.op("dve", lambda e: e.tensor_scalar_mul(gsub[:], gsub[:], 1.0 - lam_init), r=[Bgsub], w=[Bgsub])
                for j in range(4):
                    S.dma("sp", mgb[:, j:j + 1], W["ml_gate_b"][l, j * 4:(j + 1) * 4].rearrange("(h o) -> h o", o=1),
                          Bmgb, w=[Bmgb])
                S.op("act", lambda e: e.activation(rlam[:], rlam[:], AF.Exp, scale=-1.0), r=[Brlam], w=[Brlam])
                S.op("act", lambda e: e.activation(rlam[:], rlam[:], AF.Ln, bias=1.0), r=[Brlam], w=[Brlam])
                S.op("dve", lambda e: e.tensor_scalar_mul(rnsp[:], rlam[:], -8.0), r=[Brlam], w=[Brnsp])
                lv, Blv = sbt(st, "lv", [1, 512])
                lw, Blw = sbt(st, "lw", [1, 8])
                S.dma("sp", lv[:], W["attn_lambda"][l].rearrange("(o a) f -> o (a f)", o=1), Blv, w=[Blv])
                S.op("dve", lambda e: e.tensor_tensor(lv[0:1, 0:128], lv[0:1, 0:128], lv[0:1, 128:256], ALU.mult), r=[Blv], w=[Blv])
                S.op("dve", lambda e: e.tensor_tensor(lv[0:1, 256:384], lv[0:1, 256:384], lv[0:1, 384:512], ALU.mult), r=[Blv], w=[Blv])
                S.op("dve", lambda e: e.tensor_reduce(lw[0:1, 0:1], lv[0:1, 0:128], AX.X, ALU.add), r=[Blv], w=[Blw])
                S.op("dve", lambda e: e.tensor_reduce(lw[0:1, 1:2], lv[0:1, 256:384], AX.X, ALU.add), r=[Blv], w=[Blw])
                S.op("act", lambda e: e.activation(lw[0:1, 2:4], lw[0:1, 0:2], AF.Exp), r=[Blw], w=[Blw])
                S.op("dve", lambda e: e.tensor_tensor(lw[0:1, 4:5], lw[0:1, 3:4], lw[0:1, 2:3], ALU.subtract), r=[Blw], w=[Blw])
                S.op("dve", lambda e: e.tensor_scalar_add(lw[0:1, 5:6], lw[0:1, 4:5], -lam_init), r=[Blw], w=[Blw])
                p, Bp = PS[0]
                S.op("pe", lambda e: e.matmul(p[:, 0:1], ones32[0:1, :], lw[0:1, 5:6], start=True, stop=True),
                     r=[Blw, Bones32], w=[Bp])
                S.op("dve", lambda e: e.tensor_copy(nlam[:], p[:, 0:1]), r=[Bp], w=[Bnlam])
                wm = [[sbt(st, "wm%d_%d" % (i, j), [128, 4, 512]) for j in range(4)] for i in range(2)]
                pm, Bpm = PS[1]
                wmod = W["w_mod"][l].rearrange("(k p) c -> p k c", p=128)
                for cb in range(24):
                    bufs = wm[cb % 2]
                    for j in range(4):
                        t, Bt = bufs[j]
                        S.dma("sp", t[:], wmod[:, j * 4:(j + 1) * 4, cb * 512:(cb + 1) * 512], Bt, w=[Bt])
                    for mi in range(4):
                        m = cb * 4 + mi
                        for kt in range(KT):
                            t, Bt = bufs[kt // 4]
                            S.op("pe", lambda e, t=t, kt=kt, mi=mi, m=m: e.matmul(
                                pm[:, 2 * m:2 * m + 2], t[:, kt % 4, mi * 128:(mi + 1) * 128], sT[:, kt, :],
                                start=(kt == 0), stop=(kt == KT - 1)), r=[Bt, BsT], w=[Bpm])
                for r_ in range(2):
                    S.op("dve", lambda e, r_=r_: e.tensor_tensor(
                        modT[:, :, r_], pm[:, 0:192].rearrange("p (m r) -> p m r", r=2)[:, :, r_], bm[:], ALU.add),
                        r=[Bpm, Bbm], w=[BmodT])
                    for (Gt, BG, gt_, Bg_, j) in ((G1, BG1, g1, Bg1, 1), (G2, BG2, g2, Bg2, 4)):
                        S.op("dve", lambda e, Gt=Gt, gt_=gt_, j=j, r_=r_: e.scalar_tensor_tensor(
                            Gt[:, :, r_], modT[:, j * 16:(j + 1) * 16, r_], 1.0, gt_[:], ALU.add, ALU.mult),
                            r=[BmodT, Bg_], w=[BG])
                S.barrier()
                S.emit()
                S.release(rel + [Blv, Bmgb] + [b for row in wm for _, b in row])

        def phase_norm(Gt, BG, shj):
            with ExitStack() as st:
                hin = [sbt(st, "n_h%d" % i, [128, KT, 512]) for i in range(2)]
                sq, Bsq = sbt(st, "n_sq", [128, KT, 512], BF16)
                xo = [sbt(st, "n_xo%d" % i, [128, KT, 512], BF16) for i in range(2)]
                rstd, Brstd = sbt(st, "n_rstd", [128, 512])
                tmp = [sbt(st, "n_tmp%d" % i, [128, 512]) for i in range(2)]
                for gi, (t0, n) in enumerate(groups512(0, NT)):
                    h, Bh = hin[gi % 2]
                    xo_, Bxo = xo[gi % 2]
                    S.dma("sp", h[:, :, 0:n], hT[:, t0:t0 + n].rearrange("(k p) t -> p k t", p=128), Bh, w=[Bh])
                    S.op("act", lambda e, h=h, n=n: e.activation(sq[:, :, 0:n], h[:, :, 0:n], AF.Square), r=[Bh], w=[Bsq])
                    p, Bp = PS[gi % 2]
                    for kt in range(KT):
                        S.op("pe", lambda e, p=p, kt=kt, n=n: e.matmul(p[:, 0:n], onesb[:], sq[:, kt, 0:n], start=(kt == 0),
                                                                         stop=(kt == KT - 1)), r=[Bsq, Bonesb], w=[Bp])
                    S.op("act", lambda e, p=p, n=n: e.activation(rstd[:, 0:n], p[:, 0:n], AF.Sqrt, bias=epsc[:], scale=1.0 / D),
                         r=[Bp, Bepsc], w=[Brstd])
                    S.op("dve", lambda e, n=n: e.reciprocal(rstd[:, 0:n], rstd[:, 0:n]), r=[Brstd], w=[Brstd])
                    for kt in range(KT):
                        tm, Btm = tmp[kt % 2]
                        for (s0, sn, r_) in segs(t0, n):
                            a0 = s0 - t0
                            S.op("dve", lambda e, h=h, kt=kt, a0=a0, sn=sn, r_=r_, tm=tm: e.scalar_tensor_tensor(
                                tm[:, a0:a0 + sn], h[:, kt, a0:a0 + sn], Gt[:, kt, r_:r_ + 1], rstd[:, a0:a0 + sn],
                                ALU.mult, ALU.mult), r=[Bh, BG, Brstd], w=[Btm])
                            eng = "act" if kt % 2 == 0 else "pool"
                            if eng == "act":
                                S.op("act", lambda e, kt=kt, a0=a0, sn=sn, r_=r_, tm=tm, xo_=xo_: e.activation(
                                    xo_[:, kt, a0:a0 + sn], tm[:, a0:a0 + sn], AF.Identity,
                                    bias=modT[:, shj * 16 + kt, r_:r_ + 1], scale=1.0), r=[Btm, BmodT], w=[Bxo])
                            else:
                                S.op("pool", lambda e, kt=kt, a0=a0, sn=sn, r_=r_, tm=tm, xo_=xo_: e.tensor_scalar(
                                    xo_[:, kt, a0:a0 + sn], tm[:, a0:a0 + sn], modT[:, shj * 16 + kt, r_:r_ + 1], None,
                                    ALU.add), r=[Btm, BmodT], w=[Bxo])
                    S.dma("sp", xmT[:, t0:t0 + n].rearrange("(k p) t -> p k t", p=128), xo_[:, :, 0:n], Bxo, r=[Bxo])
                S.barrier()
                S.emit()
                S.release([b for _, b in hin] + [b for _, b in xo])

        def gemm_fm(tag, inputs, streams, colblocks, sg_max, epilogue, extra_setup=None, wwidth=512):
            with ExitStack() as st:
                ns = len(streams)
                X = []
                for i, (scr, kti) in enumerate(inputs):
                    X.append(sbt(st, "%s_x%d" % (tag, i), [128, kti, sg_max], BF16))
                Wb = []
                for s_, (ii, wap) in enumerate(streams):
                    kti = inputs[ii][1]
                    nch = -(-kti // 8)
                    Wb.append([[sbt(st, "%s_w%d_%d_%d" % (tag, s_, par, ch), [128, min(8, kti - ch * 8), wwidth], BF16)
                                for ch in range(nch)] for par in range(2)])
                ctxobj = extra_setup(st) if extra_setup else None
                allb = [b for _, b in X] + [b for s_ in Wb for par in s_ for _, b in par]
                gidx = 0
                cbi = 0
                for sgi, (t0sg, nsg) in enumerate(split_sizes(NT, sg_max)):
                    for i, (scr, kti) in enumerate(inputs):
                        xt, Bx = X[i]
                        S.dma("sp", xt[:, :, 0:nsg], scr[0:kti * 128, t0sg:t0sg + nsg].rearrange("(k p) t -> p k t", p=128),
                              Bx, w=[Bx])
                    for (c0, cw) in colblocks:
                        par = cbi % 2
                        cbi += 1
                        for s_, (ii, wap) in enumerate(streams):
                            kti = inputs[ii][1]
                            wv = wap.rearrange("(k p) c -> p k c", p=128)
                            for ch, (wt, Bw) in enumerate(Wb[s_][par]):
                                k0 = ch * 8
                                k1 = min(kti, k0 + 8)
                                S.dma("pool", wt[:, 0:k1 - k0, 0:cw], wv[:, k0:k1, c0:c0 + cw], Bw, w=[Bw])
                        for mi in range(-(-cw // 128)):
                            mw = min(128, cw - mi * 128)
                            for (g0, n) in groups512(t0sg, nsg):
                                pss = []
                                for s_, (ii, wap) in enumerate(streams):
                                    kti = inputs[ii][1]
                                    xt, Bx = X[ii]
                                    p, Bp = PS[(gidx % 2) * ns + s_] if 2 * ns <= 6 else PS[s_]
                                    for kt in range(kti):
                                        wt, Bw = Wb[s_][par][kt // 8]
                                        S.op("pe", lambda e, p=p, wt=wt, kt=kt, mi=mi, mw=mw, xt=xt, x0=g0 - t0sg, n=n, kti=kti: e.matmul(
                                            p[0:mw, 0:n], wt[:, kt % 8, mi * 128:mi * 128 + mw],
                                            xt[:, kt, x0:x0 + n], start=(kt == 0), stop=(kt == kti - 1)),
                                            r=[Bw, Bx], w=[Bp])
                                    pss.append((p, Bp))
                                epilogue(ctxobj, sgi, t0sg, nsg, c0, mi, mw, g0, n, pss, gidx)
                                gidx += 1
                S.barrier()
                S.emit()
                S.release(allb + (ctxobj["bufs"] if ctxobj and "bufs" in ctxobj else []))

        def phase_inproj(l):
            win = W["w_in"][l]
            SG = 2176
            fm_ranges = [(OFF["aq"], 2048), (OFF["rx"], 4096), (OFF["bg"], 6144)]
            cbs = []
            for (c0, w_) in fm_ranges:
                for c in range(c0, c0 + w_, 512):
                    cbs.append((c, 512))

            def setup(st):
                stg = [sbt(st, "ip_stg%d" % i, [128, SG]) for i in range(3)]
                return dict(stg=stg, bufs=[b for _, b in stg], k=0)

            def epi(cx, sgi, t0sg, nsg, c0, mi, mw, g0, n, pss, gidx):
                p, Bp = pss[0]
                first = (g0 == t0sg)
                if first:
                    cx["k"] += 1
                stg, Bstg = cx["stg"][cx["k"] % 3]
                a0 = g0 - t0sg
                if gidx % 2 == 0:
                    S.op("act", lambda e: e.copy(stg[0:mw, a0:a0 + n], p[0:mw, 0:n]), r=[Bp], w=[Bstg])
                else:
                    S.op("dve", lambda e: e.tensor_copy(stg[0:mw, a0:a0 + n], p[0:mw, 0:n]), r=[Bp], w=[Bstg])
                if g0 + n == t0sg + nsg:
                    r0 = c0 + mi * 128
                    S.dma("sp", uT[r0:r0 + mw, t0sg:t0sg + nsg], stg[0:mw, 0:nsg], Bstg, r=[Bstg])

            gemm_fm("ip", [(xmT, KT)], [(0, win)], cbs, SG, epi, setup)

            def epi_g(cx, sgi, t0sg, nsg, c0, mi, mw, g0, n, pss, gidx):
                epi(cx, sgi, t0sg, nsg, c0, mi, mw, g0, n, pss, gidx)

            gemm_fm("ig", [(xmT, KT)], [(0, win)], [(OFF["mg"] + 4 * j, 4) for j in range(4)], SG, epi_g, setup)

            with ExitStack() as st:
                X, BX = sbt(st, "it_x", [128, KT, SG], BF16)
                Wb = [[sbt(st, "it_w%d_%d" % (par, ch), [128, 8, 512], BF16) for ch in range(2)] for par in range(2)]
                stg = [sbt(st, "it_s%d" % i, [128, 512]) for i in range(3)]
                cols = []
                for ci, name in enumerate(("av", "mv", "mo")):
                    for c in range(0, 1024, 512):
                        cols.append((OFF[name] + c, ci * 1024 + c))
                wv = win.rearrange("(k p) c -> p k c", p=128)
                k = 0
                cbi = 0
                for (t0sg, nsg) in split_sizes(NT, SG):
                    S.dma("sp", X[:, :, 0:nsg], xmT[:, t0sg:t0sg + nsg].rearrange("(k p) t -> p k t", p=128), BX, w=[BX])
                    for (c0, oc0) in cols:
                        par = cbi % 2
                        cbi += 1
                        for ch in range(2):
                            wt, Bw = Wb[par][ch]
                            S.dma("pool", wt[:], wv[:, ch * 8:(ch + 1) * 8, c0:c0 + 512], Bw, w=[Bw])
                        for ti in range(nsg // 128):
                            p, Bp = PS[k % 4]
                            for kt in range(KT):
                                wt, Bw = Wb[par][kt // 8]
                                S.op("pe", lambda e, p=p, kt=kt, ti=ti, wt=wt: e.matmul(
                                    p[:, :], X[:, kt, ti * 128:(ti + 1) * 128], wt[:, kt % 8, :], start=(kt == 0),
                                    stop=(kt == KT - 1)), r=[BX, Bw], w=[Bp])
                            sg_, Bsg = stg[k % 3]
                            if k % 2 == 0:
                                S.op("act", lambda e, sg_=sg_, p=p: e.copy(sg_[:], p[:]), r=[Bp], w=[Bsg])
                            else:
                                S.op("dve", lambda e, sg_=sg_, p=p: e.tensor_copy(sg_[:], p[:]), r=[Bp], w=[Bsg])
                            tk = t0sg + ti * 128
                            S.dma("sp", vtok[tk:tk + 128, oc0:oc0 + 512], sg_[:], Bsg, r=[Bsg])
                            k += 1
                S.barrier()
                S.emit()
                S.release([BX] + [b for par in Wb for _, b in par] + [b for _, b in stg])

        def phase_attn_prep():
            with ExitStack() as st:
                cosT, Bcos = sbt(st, "ap_cos", [128, NL])
                sinT, Bsin = sbt(st, "ap_sin", [128, NL])
                S.dma("sp", cosT[:], cos_in, Bcos, w=[Bcos])
                S.dma("sp", sinT[:], sin_in, Bsin, w=[Bsin])
                qin = [sbt(st, "ap_q%d" % i, [128, NT]) for i in range(2)]
                qo = [sbt(st, "ap_o%d" % i, [128, NT], BF16) for i in range(2)]
                sq, Bsq = sbt(st, "ap_sq", [128, 512], BF16)
                rstd, Brstd = sbt(st, "ap_rstd", [128, 512])
                qn, Bqn = sbt(st, "ap_qn", [128, 512])
                qnb, Bqnb = sbt(st, "ap_qnb", [128, 512], BF16)
                t1, Bt1 = sbt(st, "ap_t1", [128, 512])
                t2, Bt2 = sbt(st, "ap_t2", [128, 512])
                for idx in range(16):
                    which = idx // 8
                    q, Bq = qin[idx % 2]
                    o, Bo = qo[idx % 2]
                    row0 = (OFF["aq"] if which == 0 else OFF["ak"]) + (idx % 8) * 128
                    S.dma("sp", q[:], uT[row0:row0 + 128, :], Bq, w=[Bq])
                    for gi, (t0, n) in enumerate(groups512(0, NT)):
                        S.op("act", lambda e, q=q, t0=t0, n=n: e.activation(sq[:, 0:n], q[:, t0:t0 + n], AF.Square), r=[Bq], w=[Bsq])
                        p, Bp = PS[gi % 2]
                        S.op("pe", lambda e, p=p, n=n: e.matmul(p[:, 0:n], onesb[:], sq[:, 0:n], start=True, stop=True),
                             r=[Bsq, Bonesb], w=[Bp])
                        S.op("act", lambda e, p=p, n=n: e.activation(rstd[:, 0:n], p[:, 0:n], AF.Sqrt, bias=epsc[:], scale=1.0 / 128),
                             r=[Bp, Bepsc], w=[Brstd])
                        S.op("dve", lambda e, n=n: e.reciprocal(rstd[:, 0:n], rstd[:, 0:n]), r=[Brstd], w=[Brstd])
                        S.op("dve", lambda e, q=q, t0=t0, n=n, which=which: e.scalar_tensor_tensor(
                            qn[:, 0:n], q[:, t0:t0 + n], gqk[:, which:which + 1], rstd[:, 0:n], ALU.mult, ALU.mult),
                            r=[Bq, Bgqk, Brstd], w=[Bqn])
                        for (s0, sn, r_) in segs(t0, n):
                            a0 = s0 - t0
                            if r_ == 1:
                                S.op("act", lambda e, o=o, s0=s0, sn=sn, a0=a0: e.copy(o[:, s0:s0 + sn], qn[:, a0:a0 + sn]), r=[Bqn], w=[Bo])
                            else:
                                l0 = s0 - NCTX
                                S.op("act", lambda e, a0=a0, sn=sn: e.copy(qnb[:, a0:a0 + sn], qn[:, a0:a0 + sn]), r=[Bqn], w=[Bqnb])
                                pr, Bpr = PS[2 + gi % 2]
                                S.op("pe", lambda e, pr=pr, a0=a0, sn=sn: e.matmul(pr[:, 0:sn], rotb[:], qnb[:, a0:a0 + sn], start=True,
                                                                                  stop=True), r=[Brotb, Bqnb], w=[Bpr])
                                S.op("pool", lambda e, a0=a0, sn=sn, l0=l0: e.tensor_tensor(t1[:, 0:sn], qn[:, a0:a0 + sn], cosT[:, l0:l0 + sn], ALU.mult),
                                     r=[Bqn, Bcos], w=[Bt1])
                                S.op("dve", lambda e, pr=pr, sn=sn, l0=l0: e.tensor_tensor(t2[:, 0:sn], pr[:, 0:sn], sinT[:, l0:l0 + sn], ALU.mult),
                                     r=[Bpr, Bsin], w=[Bt2])
                                S.op("dve", lambda e, o=o, s0=s0, sn=sn: e.tensor_tensor(o[:, s0:s0 + sn], t1[:, 0:sn], t2[:, 0:sn], ALU.add),
                                     r=[Bt1, Bt2], w=[Bo])
                    S.dma("sp", qkT[idx * 128:(idx + 1) * 128, :], o[:], Bo, r=[Bo])
                S.barrier()
                S.emit()
                S.release([Bcos, Bsin] + [b for _, b in qin] + [b for _, b in qo])

        def phase_attn():
            scale = 128 ** -0.5
            with ExitStack() as st:
                qTt, BqT = sbt(st, "at_q", [128, 2, NT], BF16)
                kTt, BkT = sbt(st, "at_k", [128, 2, NT], BF16)
                V, BV = sbt(st, "at_v", [128, TT, 257], BF16)
                ao, Bao = sbt(st, "at_ao", [128, 2, NT], BF16)
                A2, BA2 = sbt(st, "at_A2", [128, TT, 256])
                SSQ, BSSQ = sbt(st, "at_SSQ", [128, TT])
                E = [sbt(st, "at_e%d" % i, [128, 512], BF16) for i in range(4)]
                r01s = [sbt(st, "at_r%d" % i, [128, 2]) for i in range(2)]
                a1s = [sbt(st, "at_a1_%d" % i, [128, 256]) for i in range(2)]
                junk, Bjunk = sbt(st, "at_junk", [128, 256])
                STB = [PS[0], PS[1], PS[6]]
                LA = 2
                for h in range(4):
                    for sub in range(2):
                        S.dma("sp", qTt[:, sub, :], qkT[(h * 2 + sub) * 128:(h * 2 + sub + 1) * 128, :], BqT, w=[BqT])
                        S.dma("sp", kTt[:, sub, :], qkT[1024 + (h * 2 + sub) * 128:1024 + (h * 2 + sub + 1) * 128, :], BkT, w=[BkT])
                    S.dma("pool", V[:, :, 0:256], vtok[:, h * 256:(h + 1) * 256].rearrange("(t p) c -> p t c", p=128), BV, w=[BV])
                    S.op("dve", lambda e: e.memset(V[:, :, 256:257], 1.0), w=[BV])
                    iters = []
                    for qg in range(NT // 256):
                        ktiles = list(range(2)) if qg == 0 else list(range(TT))
                        for ki, kt in enumerate(ktiles):
                            iters.append((qg, ki, kt, len(ktiles)))
                    nit = len(iters)

                    def emit_st(i):
                        qg, ki, kt, nk = iters[i]
                        q0 = qg * 256
                        pst, Bpst = STB[i % 3]
                        et, Bet = E[i % 4]
                        for sub in range(2):
                            S.op("pe", lambda e, sub=sub: e.matmul(
                                pst[:, sub * 256:(sub + 1) * 256], kTt[:, sub, kt * 128:(kt + 1) * 128],
                                qTt[:, sub, q0:q0 + 256], start=True, stop=True), r=[BkT, BqT], w=[Bpst])
                        S.op("act", lambda e: e.activation(et[:], pst[:], AF.Exp, scale=scale), r=[Bpst], w=[Bet])

                    def emit_pv(i):
                        qg, ki, kt, nk = iters[i]
                        q0 = qg * 256
                        et, Bet = E[i % 4]
                        for sub in range(2):
                            for qs in range(2):
                                pa, Bpa = PS[2 + sub * 2 + qs]
                                S.op("pe", lambda e, pa=pa, sub=sub, qs=qs: e.matmul(
                                    pa[:, 0:257], et[:, sub * 256 + qs * 128:sub * 256 + (qs + 1) * 128], V[:, kt, :],
                                    start=(ki == 0), stop=(ki == nk - 1)), r=[Bet, BV], w=[Bpa])
                        if ki != nk - 1:
                            return
                        for qs in range(2):
                            p0, Bp0 = PS[2 + qs]
                            p1, Bp1 = PS[4 + qs]
                            tt = qg * 2 + qs
                            r01, Br01 = r01s[qs]
                            a1, Ba1 = a1s[qs]
                            S.op("dve", lambda e, p0=p0, r01=r01: e.reciprocal(r01[:, 0:1], p0[:, 256:257]), r=[Bp0], w=[Br01])
                            S.op("dve", lambda e, p1=p1, r01=r01: e.reciprocal(r01[:, 1:2], p1[:, 256:257]), r=[Bp1], w=[Br01])
                            S.op("dve", lambda e, r01=r01: e.tensor_tensor(r01[:, 1:2], r01[:, 1:2], nlam[:], ALU.mult), r=[Br01, Bnlam], w=[Br01])
                            S.op("dve", lambda e, p0=p0, r01=r01, a1=a1: e.tensor_scalar(a1[:], p0[:, 0:256], r01[:, 0:1], None, ALU.mult),
                                 r=[Bp0, Br01], w=[Ba1])
                            S.op("dve", lambda e, p1=p1, r01=r01, a1=a1, tt=tt: e.scalar_tensor_tensor(
                                A2[:, tt, :], p1[:, 0:256], r01[:, 1:2], a1[:], ALU.mult, ALU.add), r=[Bp1, Br01, Ba1], w=[BA2])
                            S.op("dve", lambda e, tt=tt: e.scalar_tensor_tensor(
                                junk[:], A2[:, tt, :], 1.0, A2[:, tt, :], ALU.mult, ALU.mult, accum_out=SSQ[:, tt:tt + 1]),
                                r=[BA2], w=[Bjunk, BSSQ])

                    for i in range(min(LA, nit)):
                        emit_st(i)
                    for i in range(nit):
                        if i + LA < nit:
                            emit_st(i + LA)
                        emit_pv(i)
                    S.op("act", lambda e: e.activation(SSQ[:], SSQ[:], AF.Sqrt, bias=epsc[:], scale=1.0 / 256), r=[BSSQ, Bepsc], w=[BSSQ])
                    S.op("dve", lambda e: e.reciprocal(SSQ[:], SSQ[:]), r=[BSSQ], w=[BSSQ])
                    for tt in range(TT):
                        a1, Ba1 = a1s[tt % 2]
                        S.op("dve", lambda e, tt=tt, a1=a1: e.scalar_tensor_tensor(a1[:], A2[:, tt, :], SSQ[:, tt:tt + 1], gsub[:], ALU.mult, ALU.mult),
                             r=[BA2, BSSQ, Bgsub], w=[Ba1])
                        ptr, Bptr = PS[7] if tt % 2 == 0 else PS[6]
                        for vc in range(2):
                            S.op("pe", lambda e, ptr=ptr, vc=vc, a1=a1: e.transpose(ptr[:, vc * 128:(vc + 1) * 128], a1[:, vc * 128:(vc + 1) * 128], ident[:]),
                                 r=[Ba1, Bident], w=[Bptr])
                        S.op("act", lambda e, ptr=ptr, tt=tt: e.copy(ao[:, :, tt * 128:(tt + 1) * 128], ptr[:, 0:256].rearrange("p (v t) -> p v t", v=2)),
                             r=[Bptr], w=[Bao])
                    for vc in range(2):
                        S.dma("sp", aT[h * 256 + vc * 128:h * 256 + (vc + 1) * 128, :], ao[:, vc, :], Bao, r=[Bao])
                S.barrier()
                S.emit()
                S.release([BqT, BkT, BV, Bao])

        def conv_ops(xin, Bx, y, By, wt, Bw, wcol, bt, Bb, bcol):
            for (s0, sn) in ((0, NCTX), (NCTX, NL)):
                S.op("dve", lambda e, s0=s0, sn=sn: e.tensor_scalar(y[:, s0:s0 + sn], xin[:, s0:s0 + sn], wcol(1), bcol, ALU.mult, ALU.add),
                     r=[Bx, Bw, Bb], w=[By])
                S.op("dve", lambda e, s0=s0, sn=sn: e.scalar_tensor_tensor(y[:, s0 + 1:s0 + sn], xin[:, s0:s0 + sn - 1], wcol(0), y[:, s0 + 1:s0 + sn],
                                                                          ALU.mult, ALU.add), r=[Bx, Bw, By], w=[By])
                S.op("dve", lambda e, s0=s0, sn=sn: e.scalar_tensor_tensor(y[:, s0:s0 + sn - 1], xin[:, s0 + 1:s0 + sn], wcol(2), y[:, s0:s0 + sn - 1],
                                                                          ALU.mult, ALU.add), r=[Bx, Bw, By], w=[By])
                S.op("dve", lambda e, s0=s0, sn=sn: e.scalar_tensor_tensor(y[:, s0:s0 + sn - 2], xin[:, s0 + 2:s0 + sn], wcol(3), y[:, s0:s0 + sn - 2],
                                                                          ALU.mult, ALU.add), r=[Bx, Bw, By], w=[By])

        def rev_ap(t, p0, p1, rowlen, c0, n):
            return bass.AP(t, p0 * rowlen + c0 + n - 1, [[rowlen, p1 - p0], [-1, n]])

        def phase_rglru(l):
            with ExitStack() as st:
                xs = [sbt(st, "rg_x%d" % i, [128, NT]) for i in range(2)]
                gs = [sbt(st, "rg_g%d" % i, [128, NT]) for i in range(2)]
                xc, Bxc = sbt(st, "rg_xc", [128, NT])
                xcb, Bxcb = sbt(st, "rg_xcb", [128, NT], BF16)
                Rt, BR = sbt(st, "rg_R", [128, NT])
                It, BI = sbt(st, "rg_I", [128, NT])
                hh = [sbt(st, "rg_h%d" % d, [128, NT]) for d in range(2)]
                yo, Byo = sbt(st, "rg_yo", [128, NT], BF16)
                wg = [[[sbt(st, "rg_w%d_%d_%d" % (par, d, j), [128, 128], BF16) for j in range(2)] for d in range(2)] for par in range(2)]
                for k in range(8):
                    x, Bx = xs[k % 2]
                    g, Bg = gs[k % 2]
                    S.dma("sp", x[:], uT[OFF["rx"] + k * 128:OFF["rx"] + (k + 1) * 128, :], Bx, w=[Bx])
                    S.dma("sp", g[:], uT[OFF["rg"] + k * 128:OFF["rg"] + (k + 1) * 128, :], Bg, w=[Bg])
                    wk = wg[k % 2]
                    for d in range(2):
                        S.dma("pool", wk[d][0][0][:], W["rnn_wa"][l, d, k], wk[d][0][1], w=[wk[d][0][1]])
                        S.dma("pool", wk[d][1][0][:], W["rnn_wx"][l, d, k], wk[d][1][1], w=[wk[d][1][1]])
                    conv_ops(x, Bx, xc, Bxc, rcw, Brcw, lambda tap, k=k: rcw[:, tap * 8 + k:tap * 8 + k + 1], rcb, Brcb, rcb[:, k:k + 1])
                    S.op("act", lambda e: e.copy(xcb[:], xc[:]), r=[Bxc], w=[Bxcb])
                    S.op("act", lambda e, g=g: e.activation(g[:], g[:], AF.Gelu), r=[Bg], w=[Bg])
                    for d in range(2):
                        col = d * 8 + k
                        h, Bh = hh[d]
                        for gi, (t0, n) in enumerate(groups512(0, NT)):
                            pr, Bpr = PS[(gi % 2) * 2]
                            pi, Bpi = PS[(gi % 2) * 2 + 1]
                            S.op("pe", lambda e, pr=pr, d=d, t0=t0, n=n, wk=wk: e.matmul(pr[:, 0:n], wk[d][0][0][:], xcb[:, t0:t0 + n], start=True, stop=True),
                                 r=[wk[d][0][1], Bxcb], w=[Bpr])
                            S.op("pe", lambda e, pi=pi, d=d, t0=t0, n=n, wk=wk: e.matmul(pi[:, 0:n], wk[d][1][0][:], xcb[:, t0:t0 + n], start=True, stop=True),
                                 r=[wk[d][1][1], Bxcb], w=[Bpi])
                            S.op("act", lambda e, pr=pr, t0=t0, n=n, col=col: e.activation(Rt[:, t0:t0 + n], pr[:, 0:n], AF.Sigmoid, bias=rba[:, col:col + 1], scale=1.0),
                                 r=[Bpr, Brba], w=[BR])
                            S.op("act", lambda e, pi=pi, t0=t0, n=n, col=col: e.activation(It[:, t0:t0 + n], pi[:, 0:n], AF.Sigmoid, bias=rbx[:, col:col + 1], scale=1.0),
                                 r=[Bpi, Brbx], w=[BI])
                        S.op("act", lambda e, col=col: e.activation(Rt[:], Rt[:], AF.Exp, scale=rnsp[:, col:col + 1]), r=[BR, Brnsp], w=[BR])
                        S.op("pool", lambda e, h=h: e.tensor_tensor(h[:], Rt[:], Rt[:], ALU.mult), r=[BR], w=[Bh])
                        S.op("act", lambda e, h=h: e.activation(h[:], h[:], AF.Sqrt, bias=1.0, scale=-1.0), r=[Bh], w=[Bh])
                        S.op("dve", lambda e: e.tensor_tensor(It[:], It[:], xc[:], ALU.mult), r=[BI, Bxc], w=[BI])
                        S.op("pool", lambda e, h=h: e.tensor_tensor(It[:], It[:], h[:], ALU.mult), r=[BI, Bh], w=[BI])
                        if d == 0:
                            S.op("dve", lambda e, h=h: e.tensor_tensor_scan(h[:], Rt[:], It[:], 0.0, ALU.mult, ALU.add), r=[BR, BI], w=[Bh])
                        else:
                            S.op("dve", lambda e, h=h: e.tensor_tensor_scan(rev_ap(h, 0, 128, NT, 0, NCTX), rev_ap(Rt, 0, 128, NT, 0, NCTX),
                                                                            rev_ap(It, 0, 128, NT, 0, NCTX), 0.0, ALU.mult, ALU.add), r=[BR, BI], w=[Bh])
                            S.op("dve", lambda e, h=h: e.tensor_tensor_scan(rev_ap(h, 0, 128, NT, NCTX, NL), rev_ap(Rt, 0, 128, NT, NCTX, NL),
                                                                            rev_ap(It, 0, 128, NT, NCTX, NL), h[:, 0:1], ALU.mult, ALU.add),
                                 r=[BR, BI, Bh], w=[Bh])
                    S.op("pool", lambda e: e.tensor_tensor(hh[0][0][:], hh[0][0][:], hh[1][0][:], ALU.add), r=[hh[0][1], hh[1][1]], w=[hh[0][1]])
                    S.op("dve", lambda e, g=g: e.tensor_tensor(yo[:], hh[0][0][:], g[:], ALU.mult), r=[hh[0][1], Bg], w=[Byo])
                    S.dma("sp", rT[k * 128:(k + 1) * 128, :], yo[:], Byo, r=[Byo])
                S.barrier()
                S.emit()
                S.release([b for _, b in xs] + [b for _, b in gs] + [Byo] + [wg[p][d][j][1] for p in range(2) for d in range(2) for j in range(2)])

        def phase_mlstm_prep():
            with ExitStack() as st:
                xs = [sbt(st, "mp_x%d" % i, [128, NT]) for i in range(2)]
                ys = [sbt(st, "mp_y%d" % i, [128, NT]) for i in range(2)]
                os_ = [sbt(st, "mp_o%d" % i, [128, NT], BF16) for i in range(2)]
                for j in range(16):
                    x, Bx = xs[j % 2]
                    y, By = ys[j % 2]
                    o, Bo = os_[j % 2]
                    S.dma("sp", x[:], uT[OFF["mq"] + j * 128:OFF["mq"] + (j + 1) * 128, :], Bx, w=[Bx])
                    conv_ops(x, Bx, y, By, mcw, Bmcw, lambda tap, j=j: mcw[:, tap * 16 + j:tap * 16 + j + 1], mcb, Bmcb, mcb[:, j:j + 1])
                    if j < 8:
                        S.op("act", lambda e, o=o, y=y: e.activation(o[:], y[:], AF.Silu), r=[By], w=[Bo])
                    else:
                        S.op("act", lambda e, y=y: e.activation(y[:], y[:], AF.Silu), r=[By], w=[By])
                        S.op("pool", lambda e, o=o, y=y: e.tensor_scalar(o[:], y[:], 1.0 / 16.0, None, ALU.mult), r=[By], w=[Bo])
                    S.dma("sp", qkm[j * 128:(j + 1) * 128, :], o[:], Bo, r=[Bo])
                S.barrier()
                S.emit()
                S.release([b for _, b in xs] + [b for _, b in os_])

        def phase_mlstm(l):
            with ExitStack() as st:
                nG = [sbt(st, "ml_nG%d" % d, [4, NT]) for d in range(2)]
                COL, BCOL = sbt(st, "ml_col", [128, TT, 32])
                SCB, BSCB = sbt(st, "ml_scb", [128, 8, TT])
                rows = {("nG", 0): nG[0], ("nG", 1): nG[1]}
                relg = []
                with ExitStack() as st2:
                    onesr, Bonesr = sbt(st2, "ml_onesr", [4, NT])
                    t_i = sbt(st2, "ml_ti", [4, NT])
                    t_f = sbt(st2, "ml_tf", [4, NT])
                    t_B = sbt(st2, "ml_tB", [4, NT])
                    t_A = sbt(st2, "ml_tA", [4, NT])
                    t_ws = sbt(st2, "ml_tws", [4, NT])
                    ge, Bge = sbt(st2, "ml_gend", [4, TT])
                    gp, Bgp = sbt(st2, "ml_gprev", [4, TT])
                    sct, Bsct = sbt(st2, "ml_sc", [4, TT])
                    relg = [t_i[1], t_f[1]]
                    S.op("pool", lambda e: e.memset(onesr[:], 1.0), w=[Bonesr])
                    mg0 = OFF["mg"]
                    for d in range(2):
                        it, Bi = t_i
                        ft, Bf = t_f
                        Bt_, BB = t_B
                        At, BA = t_A
                        nGt, BnG = nG[d]
                        E1t, BE1 = t_f
                        wit, Bwi = t_B
                        wst, Bws = t_ws
                        S.dma("sp", it[:], uT[mg0 + d * 8:mg0 + d * 8 + 4, :], Bi, w=[Bi])
                        S.dma("sp", ft[:], uT[mg0 + d * 8 + 4:mg0 + d * 8 + 8, :], Bf, w=[Bf])
                        S.op("dve", lambda e, d=d: e.tensor_scalar(it[:], it[:], mgb[:, d * 2:d * 2 + 1], None, ALU.add), r=[Bi, Bmgb], w=[Bi])
                        S.op("dve", lambda e, d=d: e.tensor_scalar(ft[:], ft[:], mgb[:, d * 2 + 1:d * 2 + 2], None, ALU.add), r=[Bf, Bmgb], w=[Bf])
                        S.op("act", lambda e: e.activation(ft[:], ft[:], AF.Exp, scale=-1.0), r=[Bf], w=[Bf])
                        S.op("act", lambda e: e.activation(ft[:], ft[:], AF.Ln, bias=1.0), r=[Bf], w=[Bf])
                        S.op("dve", lambda e: e.tensor_scalar(ft[:], ft[:], -1.0, None, ALU.mult), r=[Bf], w=[Bf])
                        if d == 0:
                            S.op("dve", lambda e: e.tensor_tensor_scan(Bt_[:], onesr[:], ft[:], 0.0, ALU.mult, ALU.add), r=[Bf, Bonesr], w=[BB])
                        else:
                            S.op("dve", lambda e: e.tensor_tensor_scan(rev_ap(Bt_, 0, 4, NT, 0, NCTX), rev_ap(onesr, 0, 4, NT, 0, NCTX),
                                                                       rev_ap(ft, 0, 4, NT, 0, NCTX), 0.0, ALU.mult, ALU.add), r=[Bf, Bonesr], w=[BB])
                            S.op("dve", lambda e: e.tensor_tensor_scan(rev_ap(Bt_, 0, 4, NT, NCTX, NL), rev_ap(onesr, 0, 4, NT, NCTX, NL),
                                                                       rev_ap(ft, 0, 4, NT, NCTX, NL), Bt_[:, 0:1], ALU.mult, ALU.add),
                                 r=[Bf, Bonesr, BB], w=[BB])
                        S.op("dve", lambda e: e.tensor_tensor(At[:], it[:], Bt_[:], ALU.subtract), r=[Bi, BB], w=[BA])
                        Gt, BGt = it, Bi
                        if d == 0:
                            S.op("dve", lambda e: e.tensor_tensor_scan(Gt[:], At[:], At[:], 0.0, ALU.max, ALU.max), r=[BA], w=[BGt])
                        else:
                            S.op("dve", lambda e: e.tensor_tensor_scan(rev_ap(Gt, 0, 4, NT, 0, NCTX), rev_ap(At, 0, 4, NT, 0, NCTX),
                                                                       rev_ap(At, 0, 4, NT, 0, NCTX), 0.0, ALU.max, ALU.max), r=[BA], w=[BGt])
                            S.op("dve", lambda e: e.tensor_tensor_scan(rev_ap(Gt, 0, 4, NT, NCTX, NL), rev_ap(At, 0, 4, NT, NCTX, NL),
                                                                       rev_ap(At, 0, 4, NT, NCTX, NL), Gt[:, 0:1], ALU.max, ALU.max),
                                 r=[BA, BGt], w=[BGt])
                        S.op("dve", lambda e, nGt=nGt: e.tensor_scalar(nGt[:], Gt[:], -1.0, None, ALU.mult), r=[BGt], w=[BnG])
                        S.op("dve", lambda e, nGt=nGt: e.tensor_tensor(E1t[:], nGt[:], Bt_[:], ALU.subtract), r=[BnG, BB], w=[BE1])
                        S.op("act", lambda e: e.activation(E1t[:], E1t[:], AF.Exp), r=[BE1], w=[BE1])
                        endoff = 127 if d == 0 else 0
                        S.op("dve", lambda e, endoff=endoff: e.tensor_copy(ge[:], bass.AP(Gt, endoff, [[NT, 4], [128, TT]])), r=[BGt], w=[Bge])
                        S.op("dve", lambda e: e.memset(gp[:], 0.0), w=[Bgp])
                        if d == 0:
                            S.op("dve", lambda e: e.tensor_copy(gp[:, 1:TT], ge[:, 0:TT - 1]), r=[Bge], w=[Bgp])
                        else:
                            S.op("dve", lambda e: e.tensor_copy(gp[:, 0:1], ge[:, 1:2]), r=[Bge], w=[Bgp])
                            S.op("dve", lambda e: e.tensor_copy(gp[:, TT - 1:TT], ge[:, 0:1]), r=[Bge], w=[Bgp])
                            if TT > 3:
                                S.op("dve", lambda e: e.tensor_copy(gp[:, 2:TT - 1], ge[:, 3:TT]), r=[Bge], w=[Bgp])
                        S.op("dve", lambda e, nGt=nGt: e.tensor_tensor(
                            wit[:].rearrange("p (c t) -> p c t", t=128), nGt[:].rearrange("p (c t) -> p c t", t=128),
                            bass.AP(gp, 0, [[TT, 4], [1, TT], [0, 128]]), ALU.add), r=[BnG, Bgp, BB], w=[Bwi])
                        S.op("act", lambda e: e.activation(wit[:], wit[:], AF.Exp), r=[Bwi], w=[Bwi])
                        S.op("dve", lambda e: e.tensor_tensor(
                            wst[:].rearrange("p (c t) -> p c t", t=128), At[:].rearrange("p (c t) -> p c t", t=128),
                            bass.AP(ge, 0, [[TT, 4], [1, TT], [0, 128]]), ALU.subtract), r=[BA, Bge], w=[Bws])
                        S.op("act", lambda e: e.activation(wst[:], wst[:], AF.Exp), r=[Bws], w=[Bws])
                        S.op("dve", lambda e: e.tensor_tensor(sct[:], gp[:], ge[:], ALU.subtract), r=[Bgp, Bge], w=[Bsct])
                        S.op("act", lambda e: e.activation(sct[:], sct[:], AF.Exp), r=[Bsct], w=[Bsct])
                        for h in range(4):
                            p, Bp = PS[h % 2]
                            S.op("pe", lambda e, p=p, h=h: e.matmul(p[:, 0:TT], sel[0:4, h * 128:(h + 1) * 128], sct[:], start=True, stop=True),
                                 r=[Bsel, Bsct], w=[Bp])
                            S.op("dve", lambda e, p=p, h=h, d=d: e.tensor_copy(SCB[:, d * 4 + h, :], p[:, 0:TT]), r=[Bp], w=[BSCB])
                        for c in range(TT):
                            p, Bp = PS[2 + c % 2]
                            for qi, (t, Bt) in enumerate((t_A, t_ws, t_B, t_f)):
                                o0 = qi * 4
                                S.op("pe", lambda e, p=p, t=t, c=c, o0=o0: e.transpose(p[:, o0:o0 + 4], t[0:4, c * 128:(c + 1) * 128], ident[0:4, 0:4]),
                                     r=[Bt, Bident], w=[Bp])
                            S.op("dve", lambda e, p=p, c=c, d=d: e.tensor_copy(COL[:, c, d * 16:(d + 1) * 16], p[:, 0:16]), r=[Bp], w=[BCOL])
                    S.barrier()
                    S.emit()

                qTt, BqT = sbt(st, "ml_q", [128, 2, NT], BF16)
                kTt, BkT = sbt(st, "ml_k", [128, 2, NT], BF16)
                ktok, Bktok = sbt(st, "ml_ktok", [128, TT, 256], BF16)
                V, BV = sbt(st, "ml_v", [128, TT, 257], BF16)
                hacc, Bhacc = sbt(st, "ml_hacc", [128, TT, 256])
                mos = [sbt(st, "ml_mo%d" % i, [128, 256]) for i in range(2)]
                mout, Bmout = sbt(st, "ml_mout", [128, 2, NT], BF16)
                C32, BC32 = sbt(st, "ml_c32", [128, 2, 257])
                Cb, BCb = sbt(st, "ml_cb", [128, 2, 257], BF16)
                arg, Barg = sbt(st, "ml_arg", [128, 128])
                SD, BSD = sbt(st, "ml_sd", [128, 128], BF16)
                intras = [sbt(st, "ml_intra%d" % i, [128, 257]) for i in range(2)]
                nd, Bnd = sbt(st, "ml_nd", [128, 257])
                rden, Brden = sbt(st, "ml_rden", [128, 1])
                vw, Bvw = sbt(st, "ml_vw", [128, 257], BF16)
                ssq, Bssq = sbt(st, "ml_ssq", [128, 1])
                junk, Bjunk = sbt(st, "ml_junk", [128, 256])
                hns = [sbt(st, "ml_hn%d" % i, [128, 256]) for i in range(2)]
                sgs = [sbt(st, "ml_sg%d" % i, [128, 256]) for i in range(2)]
                SSQ, BSSQ = sbt(st, "ml_SSQ", [128, TT])
                NGB = [sbt(st, "ml_ngb%d" % i, [128, 512]) for i in range(2)]
                for h in range(4):
                    for j in range(2):
                        S.dma("sp", qTt[:, j, :], qkm[(h * 2 + j) * 128:(h * 2 + j + 1) * 128, :], BqT, w=[BqT])
                        S.dma("sp", kTt[:, j, :], qkm[1024 + (h * 2 + j) * 128:1024 + (h * 2 + j + 1) * 128, :], BkT, w=[BkT])
                    S.dma("pool", V[:, :, 0:256], vtok[:, 1024 + h * 256:1024 + (h + 1) * 256].rearrange("(t p) c -> p t c", p=128), BV, w=[BV])
                    S.op("dve", lambda e: e.memset(V[:, :, 256:257], 1.0), w=[BV])
                    for c in range(TT):
                        p, Bp = PS[c % 2]
                        pb = p[:].bitcast(BF16)
                        for j in range(2):
                            S.op("pe", lambda e, pb=pb, j=j, c=c: e.transpose(pb[:, j * 128:(j + 1) * 128], kTt[:, j, c * 128:(c + 1) * 128], identb[:]),
                                 r=[BkT, Bidentb], w=[Bp])
                        S.op("act", lambda e, pb=pb, c=c: e.copy(ktok[:, c, :], pb[:, 0:256]), r=[Bp], w=[Bktok])
                    for d in range(2):
                        order = list(range(TT)) if d == 0 else [1, 0] + list(range(TT - 1, 1, -1))
                        nGt, BnG = nG[d]
                        S.op("dve", lambda e: e.memset(C32[:], 0.0), w=[BC32])
                        S.op("dve", lambda e: e.memset(Cb[:], 0.0), w=[BCb])
                        ngb_state = {"g": None, "n": 0, "t": None}
                        cb_ = d * 16

                        def stage_a(step, d=d, order=order, nGt=nGt, BnG=BnG, ngb_state=ngb_state, cb_=cb_, h=h):
                            c = order[step]
                            par = step % 2
                            g4 = c // 4
                            if ngb_state["g"] != g4:
                                ngt, Bng = NGB[ngb_state["n"] % 2]
                                ngb_state["n"] += 1
                                ngb_state["g"] = g4
                                ngb_state["t"] = (ngt, Bng)
                                n4 = min(512, NT - g4 * 512)
                                p, Bp = PS[7]
                                S.op("pe", lambda e: e.matmul(p[:, 0:n4], sel[0:4, h * 128:(h + 1) * 128],
                                                              nGt[0:4, g4 * 512:g4 * 512 + n4], start=True, stop=True),
                                     r=[Bsel, BnG], w=[Bp])
                                S.op("act", lambda e: e.copy(ngt[:, 0:n4], p[:, 0:n4]), r=[Bp], w=[Bng])
                            ngt, Bng = ngb_state["t"]
                            o4 = (c % 4) * 128
                            t0 = c * 128
                            pst, Bpst = PS[0]
                            for j in range(2):
                                S.op("pe", lambda e, j=j: e.matmul(pst[:, 0:128], kTt[:, j, t0:t0 + 128], qTt[:, j, t0:t0 + 128],
                                                                   start=(j == 0), stop=(j == 1)), r=[BkT, BqT], w=[Bpst])
                            S.op("dve", lambda e: e.scalar_tensor_tensor(
                                arg[:], ngt[:, o4:o4 + 128], COL[:, c, cb_ + h:cb_ + h + 1], maskt[:, d, :], ALU.add, ALU.min),
                                r=[Bng, BCOL, Bmask], w=[Barg])
                            S.op("act", lambda e: e.activation(arg[:], arg[:], AF.Exp), r=[Barg], w=[Barg])
                            S.op("dve", lambda e: e.tensor_tensor(SD[:], pst[:, 0:128], arg[:], ALU.mult), r=[Bpst, Barg], w=[BSD])
                            pin, Bpin = PS[1]
                            S.op("pe", lambda e: e.matmul(pin[:, 0:257], SD[:], V[:, c, :], start=True, stop=True), r=[BSD, BV], w=[Bpin])
                            it_, Bit_ = intras[par]
                            S.op("act", lambda e: e.copy(it_[:], pin[:, 0:257]), r=[Bpin], w=[Bit_])
                            S.op("pool", lambda e: e.tensor_scalar(vw[:], V[:, c, :], COL[:, c, cb_ + 4 + h:cb_ + 4 + h + 1], None, ALU.mult),
                                 r=[BV, BCOL], w=[Bvw])
                            for j in range(2):
                                pu, Bpu = PS[3 + 2 * par + j]
                                S.op("pe", lambda e, pu=pu, j=j: e.matmul(pu[:, 0:257], ktok[:, c, j * 128:(j + 1) * 128], vw[:], start=True, stop=True),
                                     r=[Bktok, Bvw], w=[Bpu])

                        def stage_b(step, d=d, order=order, cb_=cb_, h=h):
                            c = order[step]
                            par = step % 2
                            t0 = c * 128
                            pit, Bpit = PS[2]
                            for j in range(2):
                                S.op("pe", lambda e, j=j: e.matmul(pit[:, 0:257], qTt[:, j, t0:t0 + 128], Cb[:, j, :],
                                                                   start=(j == 0), stop=(j == 1)), r=[BqT, BCb], w=[Bpit])
                            for j in range(2):
                                pu, Bpu = PS[3 + 2 * par + j]
                                S.op("dve", lambda e, pu=pu, j=j: e.scalar_tensor_tensor(
                                    C32[:, j, :], C32[:, j, :], SCB[:, d * 4 + h, c:c + 1], pu[:, 0:257], ALU.mult, ALU.add),
                                    r=[BC32, BSCB, Bpu], w=[BC32])
                            S.op("act", lambda e: e.copy(Cb[:], C32[:]), r=[BC32], w=[BCb])
                            it_, Bit_ = intras[par]
                            S.op("dve", lambda e: e.scalar_tensor_tensor(
                                nd[:], pit[:, 0:257], COL[:, c, cb_ + 8 + h:cb_ + 8 + h + 1], it_[:], ALU.mult, ALU.add),
                                r=[Bpit, BCOL, Bit_], w=[Bnd])
                            S.op("act", lambda e: e.activation(rden[:], nd[:, 256:257], AF.Abs), r=[Bnd], w=[Brden])
                            S.op("dve", lambda e: e.tensor_tensor(
                                rden[:], rden[:], COL[:, c, cb_ + 12 + h:cb_ + 12 + h + 1], ALU.max),
                                r=[Brden, BCOL], w=[Brden])
                            S.op("dve", lambda e: e.reciprocal(rden[:], rden[:]), r=[Brden], w=[Brden])
                            if d == 0:
                                S.op("dve", lambda e: e.tensor_scalar(hacc[:, c, :], nd[:, 0:256], rden[:, 0:1], None, ALU.mult),
                                     r=[Bnd, Brden], w=[Bhacc])
                            else:
                                S.op("dve", lambda e: e.scalar_tensor_tensor(hacc[:, c, :], nd[:, 0:256], rden[:, 0:1], hacc[:, c, :], ALU.mult, ALU.add),
                                     r=[Bnd, Brden, Bhacc], w=[Bhacc])

                        stage_a(0)
                        for step in range(TT):
                            if step + 1 < TT:
                                stage_a(step + 1)
                            stage_b(step)
                    for c in range(TT):
                        S.op("dve", lambda e, c=c: e.scalar_tensor_tensor(junk[:], hacc[:, c, :], 1.0, hacc[:, c, :], ALU.mult, ALU.mult,
                                                                          accum_out=SSQ[:, c:c + 1]), r=[Bhacc], w=[Bjunk, BSSQ])
                    S.op("act", lambda e: e.activation(SSQ[:], SSQ[:], AF.Sqrt, bias=epsc[:], scale=1.0 / 256), r=[BSSQ, Bepsc], w=[BSSQ])
                    S.op("dve", lambda e: e.reciprocal(SSQ[:], SSQ[:]), r=[BSSQ], w=[BSSQ])
                    for c in range(TT):
                        hn, Bhn = hns[c % 2]
                        sg_, Bsg_ = sgs[c % 2]
                        S.op("dve", lambda e, c=c, h=h, hn=hn: e.scalar_tensor_tensor(hn[:], hacc[:, c, :], SSQ[:, c:c + 1], mng[:, h * 256:(h + 1) * 256], ALU.mult, ALU.mult),
                             r=[Bhacc, BSSQ, Bmng], w=[Bhn])
                        mo, Bmo = mos[c % 2]
                        S.dma("sp", mo[:], vtok[c * 128:(c + 1) * 128, 2048 + h * 256:2048 + (h + 1) * 256], Bmo, w=[Bmo])
                        S.op("act", lambda e, mo=mo, sg_=sg_: e.activation(sg_[:], mo[:], AF.Sigmoid), r=[Bmo], w=[Bsg_])
                        S.op("pool", lambda e, hn=hn, sg_=sg_: e.tensor_tensor(hn[:], hn[:], sg_[:], ALU.mult), r=[Bhn, Bsg_], w=[Bhn])
                        ptr, Bptr = PS[6 + c % 2]
                        for vc in range(2):
                            S.op("pe", lambda e, ptr=ptr, vc=vc, hn=hn: e.transpose(ptr[:, vc * 128:(vc + 1) * 128], hn[:, vc * 128:(vc + 1) * 128], ident[:]),
                                 r=[Bhn, Bident], w=[Bptr])
                        S.op("act", lambda e, ptr=ptr, c=c: e.copy(mout[:, :, c * 128:(c + 1) * 128], ptr[:, 0:256].rearrange("p (v t) -> p v t", v=2)),
                             r=[Bptr], w=[Bmout])
                    for vc in range(2):
                        S.dma("sp", mT[h * 256 + vc * 128:h * 256 + (vc + 1) * 128, :], mout[:, vc, :], Bmout, r=[Bmout])
                S.barrier()
                S.emit()
                S.release(relg + [BqT, BkT, BV, Bmout] + [b for _, b in mos])

        def resid_epilogue(gtj):
            def setup(st):
                hb = [sbt(st, "re_h%d_%d" % (gtj, i), [128, 512]) for i in range(3)]
                return dict(hb=hb, bufs=[b for _, b in hb])

            def epi(cx, sgi, t0sg, nsg, c0, mi, mw, g0, n, pss, gidx):
                p, Bp = pss[0]
                m = (c0 // 128) + mi
                hb, Bhb = cx["hb"][gidx % 3]
                S.dma("sp", hb[:, 0:n], hT[m * 128:(m + 1) * 128, g0:g0 + n], Bhb, w=[Bhb])
                for (s0, sn, r_) in segs(g0, n):
                    a0 = s0 - g0
                    S.op("dve", lambda e, a0=a0, sn=sn, r_=r_: e.scalar_tensor_tensor(
                        hb[:, a0:a0 + sn], p[:, a0:a0 + sn], modT[:, gtj * 16 + m, r_:r_ + 1], hb[:, a0:a0 + sn], ALU.mult, ALU.add),
                        r=[Bp, BmodT, Bhb], w=[Bhb])
                S.dma("sp", hT[m * 128:(m + 1) * 128, g0:g0 + n], hb[:, 0:n], Bhb, r=[Bhb])
            return setup, epi

        def phase_merge(l):
            SG = 1536
            bg0 = OFF["bg"]

            def setup(st):
                gts = [[sbt(st, "mg_g%d_%d" % (i, j), [128, 512]) for j in range(3)] for i in range(2)]
                t1 = sbt(st, "mg_t1", [128, 512])
                t2 = sbt(st, "mg_t2", [128, 512])
                zo = [sbt(st, "mg_zo%d" % i, [128, SG], BF16) for i in range(2)]
                return dict(gts=gts, t1=t1, t2=t2, zo=zo, k=0,
                            bufs=[b for row in gts for _, b in row] + [b for _, b in zo])

            def epi(cx, sgi, t0sg, nsg, c0, mi, mw, g0, n, pss, gidx):
                m = (c0 // 128) + mi
                if g0 == t0sg:
                    cx["k"] += 1
                zo, Bzo = cx["zo"][cx["k"] % 2]
                gts = cx["gts"][gidx % 2]
                t1, Bt1 = cx["t1"]
                t2, Bt2 = cx["t2"]
                for j in range(3):
                    gt_, Bgt = gts[j]
                    r0 = bg0 + j * 2048 + m * 128
                    S.dma("sp", gt_[:, 0:n], uT[r0:r0 + 128, g0:g0 + n], Bgt, w=[Bgt])
                    S.op("act", lambda e, gt_=gt_: e.activation(gt_[:, 0:n], gt_[:, 0:n], AF.Sigmoid), r=[Bgt], w=[Bgt])
                a0 = g0 - t0sg
                S.op("dve", lambda e: e.tensor_tensor(t1[:, 0:n], pss[0][0][:, 0:n], gts[0][0][:, 0:n], ALU.mult), r=[pss[0][1], gts[0][1]], w=[Bt1])
                S.op("dve", lambda e: e.tensor_tensor(t2[:, 0:n], pss[1][0][:, 0:n], gts[1][0][:, 0:n], ALU.mult), r=[pss[1][1], gts[1][1]], w=[Bt2])
                S.op("pool", lambda e: e.tensor_tensor(t1[:, 0:n], t1[:, 0:n], t2[:, 0:n], ALU.add), r=[Bt1, Bt2], w=[Bt1])
                S.op("dve", lambda e: e.tensor_tensor(t2[:, 0:n], pss[2][0][:, 0:n], gts[2][0][:, 0:n], ALU.mult), r=[pss[2][1], gts[2][1]], w=[Bt2])
                S.op("dve", lambda e: e.tensor_tensor(zo[:, a0:a0 + n], t1[:, 0:n], t2[:, 0:n], ALU.add), r=[Bt1, Bt2], w=[Bzo])
                if g0 + n == t0sg + nsg:
                    S.dma("sp", zT[m * 128:(m + 1) * 128, t0sg:t0sg + nsg], zo[:, 0:nsg], Bzo, r=[Bzo])

            gemm_fm("mg", [(aT, 8), (rT, 8), (mT, 8)],
                    [(0, W["w_branch_attn"][l]), (1, W["w_branch_rnn"][l]), (2, W["w_branch_ml"][l])],
                    [(c, 512) for c in range(0, D, 512)], SG, epi, setup)
            setup2, epi2 = resid_epilogue(2)
            gemm_fm("op", [(zT, KT)], [(0, W["w_out"][l])], [(c, 512) for c in range(0, D, 512)], 2176, epi2, setup2)

        def phase_ffn(l):
            SG = 2176

            def setup(st):
                s1 = [sbt(st, "ff_s%d" % i, [128, 512]) for i in range(2)]
                ao = [sbt(st, "ff_ao%d" % i, [128, SG], BF16) for i in range(2)]
                return dict(s1=s1, ao=ao, k=0, bufs=[b for _, b in ao])

            def epi(cx, sgi, t0sg, nsg, c0, mi, mw, g0, n, pss, gidx):
                m = (c0 // 128) + mi
                if g0 == t0sg:
                    cx["k"] += 1
                ao, Bao = cx["ao"][cx["k"] % 2]
                s1, Bs1 = cx["s1"][gidx % 2]
                a0 = g0 - t0sg
                S.op("act", lambda e: e.activation(s1[:, 0:n], pss[0][0][:, 0:n], AF.Silu), r=[pss[0][1]], w=[Bs1])
                S.op("dve", lambda e: e.tensor_tensor(ao[:, a0:a0 + n], pss[1][0][:, 0:n], s1[:, 0:n], ALU.mult), r=[pss[1][1], Bs1], w=[Bao])
                if g0 + n == t0sg + nsg:
                    S.dma("sp", actT[m * 128:(m + 1) * 128, t0sg:t0sg + nsg], ao[:, 0:nsg], Bao, r=[Bao])

            gemm_fm("f1", [(xmT, KT)], [(0, W["w_ffn1"][l]), (0, W["w_ffn3"][l])], [(c, 512) for c in range(0, DFF, 512)], SG, epi, setup)
            setup2, epi2 = resid_epilogue(5)
            gemm_fm("f2", [(actT, 44)], [(0, W["w_ffn2"][l])], [(c, 256) for c in range(0, D, 256)], 1152, epi2, setup2, wwidth=256)

        def phase_out():
            with ExitStack() as st:
                hin = [sbt(st, "po_h%d" % i, [128, KT, 128]) for i in range(2)]
                ot = [sbt(st, "po_o%d" % i, [128, D]) for i in range(2)]
                for ti in range(NL // 128):
                    h, Bh = hin[ti % 2]
                    o, Bo = ot[ti % 2]
                    tk = NCTX + ti * 128
                    S.dma("sp", h[:], hT[:, tk:tk + 128].rearrange("(k p) t -> p k t", p=128), Bh, w=[Bh])
                    for q in range(4):
                        p, Bp = PS[(ti * 4 + q) % 4]
                        for j in range(4):
                            kt = q * 4 + j
                            S.op("pe", lambda e, p=p, j=j, kt=kt, h=h: e.transpose(p[:, j * 128:(j + 1) * 128], h[:, kt, :], ident[:]),
                                 r=[Bh, Bident], w=[Bp])
                        if q % 2 == 0:
                            S.op("dve", lambda e, p=p, q=q, o=o: e.tensor_copy(o[:, q * 512:(q + 1) * 512], p[:]), r=[Bp], w=[Bo])
                        else:
                            S.op("act", lambda e, p=p, q=q, o=o: e.copy(o[:, q * 512:(q + 1) * 512], p[:]), r=[Bp], w=[Bo])
                    S.dma("sp", out_d[ti * 128:(ti + 1) * 128, :], o[:], Bo, r=[Bo])
                S.barrier()
                S.emit()

        identb, Bidentb = sbt(gst, "identb", [128, 128], BF16)
        S.op("dve", lambda e: e.tensor_copy(identb[:], ident[:]), r=[Bident], w=[Bidentb])

        for l in range(DEPTH):
            phase_params(l)
            phase_norm(G1, BG1, 0)
            phase_inproj(l)
            phase_attn_prep()
            phase_attn()
            phase_rglru(l)
            phase_mlstm_prep()
            phase_mlstm(l)
            phase_merge(l)
            phase_norm(G2, BG2, 3)
            phase_ffn(l)
        phase_out()
        print("total ops recorded:", S.nops, "dma sems:", len(S.all_dsems))
    return nc


def host_tables(NL):
    ident = np.eye(128, dtype=np.float32)
    R = np.zeros((128, 128), np.float32)
    for j in range(32):
        R[j, j + 32] = -1.0
        R[j + 32, j] = 1.0
        R[j + 64, j + 96] = -1.0
        R[j + 96, j + 64] = 1.0
    rotT = np.ascontiguousarray(R.T)
    rows = NL // 64
    row = np.repeat(np.arange(rows, dtype=np.float32), 64)
    col = np.tile(np.arange(64, dtype=np.float32), rows)
    inv = (np.float32(10000.0) ** (-np.arange(32, dtype=np.float32) / np.float32(32))).astype(np.float32)
    ar = row[:, None] * inv
    ac = col[:, None] * inv
    ang = np.concatenate([ar, ar, ac, ac], axis=-1).astype(np.float32)
    cosT = np.ascontiguousarray(np.cos(ang).T.astype(np.float32))
    sinT = np.ascontiguousarray(np.sin(ang).T.astype(np.float32))
    s = np.arange(128)[:, None]
    t = np.arange(128)[None, :]
    mask = np.stack([np.where(s <= t, 0.0, NEG), np.where(s >= t, 0.0, NEG)]).astype(np.float32)
    sel = np.zeros((4, 512), np.float32)
    for j in range(4):
        sel[j, j * 128:(j + 1) * 128] = 1.0
    return dict(k_ident=ident, k_rot=rotT, k_cos=cosT, k_sin=sinT, k_mask=mask, k_sel=sel)


WNAMES = ["w_mod", "b_mod", "norm1_g", "norm2_g", "w_in", "attn_qnorm_g", "attn_knorm_g", "attn_lambda", "attn_subln_g",
          "rnn_conv_w", "rnn_conv_b", "rnn_wa", "rnn_ba", "rnn_wx", "rnn_bx", "rnn_lambda", "ml_conv_w", "ml_conv_b",
          "ml_gate_b", "ml_norm_g", "w_branch_attn", "w_branch_rnn", "w_branch_ml", "w_out", "w_ffn1", "w_ffn3", "w_ffn2"]


def run(inputs, dbg=(), n_cores=None):
    x = np.asarray(inputs["x"], np.float32)
    B, NL, _ = x.shape
    DEPTH = inputs["w_mod"].shape[0]
    nc = build(NL, DEPTH, dbg)
    tabs = host_tables(NL)
    wd = {k: np.ascontiguousarray(np.asarray(inputs[k], np.float32)) for k in WNAMES}
    in_maps = []
    ncores = B if n_cores is None else n_cores
    for b in range(ncores):
        m = dict(wd)
        m.update(tabs)
        m["x"] = np.ascontiguousarray(x[b])
        m["ctx"] = np.ascontiguousarray(np.asarray(inputs["ctx"], np.float32)[b])
        m["c2"] = np.ascontiguousarray(np.stack([np.asarray(inputs["c"], np.float32)[b], np.asarray(inputs["c_ctx"], np.float32)]))
        in_maps.append(m)
    res = run_bass_kernel_spmd(nc, in_maps, core_ids=list(range(ncores)))
    return res


def kernel(**inputs):
    res = run(inputs)
    return np.stack([r["out"] for r in res.results], axis=0).astype(np.float32)
```

```python
import math
from contextlib import ExitStack
import numpy as np
import concourse.bass as bass
import concourse.mybir as mybir
from concourse.bass_utils import run_bass_kernel_spmd

F32 = mybir.dt.float32
BF16 = mybir.dt.bfloat16
ALU = mybir.AluOpType
AF = mybir.ActivationFunctionType
AX = mybir.AxisListType
ENGS = ("pe", "act", "dve", "pool", "sp")

D = 2048
KT = 16
NCTX = 256
DFF = 5632
N_IN = 15376
OFF = dict(aq=0, ak=1024, av=2048, rx=3072, rg=4096, mq=5120, mk=6144, mv=7168, mo=8192, mg=9216, bg=9232)
EPS = 1e-6
NEG = -1.0e30


class Buf:
    __slots__ = ("name", "lastw", "readers", "dsem")

    def __init__(self, name):
        self.name = name
        self.lastw = None
        self.readers = {}
        self.dsem = None


class DmaSem:
    __slots__ = ("h", "count")

    def __init__(self, h):
        self.h = h
        self.count = 0


class Op:
    __slots__ = ("eng", "fn", "deps", "signal", "is_dma", "dsem", "dcount", "cnt")

    def __init__(self, eng, fn):
        self.eng = eng
        self.fn = fn
        self.deps = []
        self.signal = False
        self.is_dma = False
        self.dsem = None
        self.dcount = 0
        self.cnt = 0


class Sched:
    def __init__(self, nc, stack):
        self.nc = nc
        self.stack = stack
        self.ops = {e: [] for e in ENGS}
        self.esem = {e: stack.enter_context(nc.semaphore("es_" + e)) for e in ("pe", "act", "dve", "pool")}
        self.ecnt = {e: 0 for e in ("pe", "act", "dve", "pool")}
        self.free_dsems = []
        self.all_dsems = []
        self.last_sig = {}
        self.nops = 0

    def get_dsem(self):
        if self.free_dsems:
            return self.free_dsems.pop()
        ds = DmaSem(self.stack.enter_context(self.nc.semaphore("ds%d" % len(self.all_dsems))))
        self.all_dsems.append(ds)
        return ds

    def release(self, bufs):
        for b in bufs:
            if b.dsem is not None:
                self.free_dsems.append(b.dsem)
                b.dsem = None

    def _dep(self, op, prev):
        if prev is None or prev is op:
            return
        if prev.eng == "pe" and op.eng == "pe" and not prev.is_dma:
            return
        if not prev.is_dma:
            prev.signal = True
        op.deps.append(prev)

    def _track(self, o, r, w, rkey):
        for b in r:
            self._dep(o, b.lastw)
        for b in w:
            self._dep(o, b.lastw)
            for rd in b.readers.values():
                self._dep(o, rd)
        for b in r:
            b.readers[rkey] = o
        for b in w:
            b.lastw = o
            b.readers = {}

    def op(self, eng, fn, r=(), w=()):
        o = Op(eng, fn)
        self._track(o, r, w, eng)
        self.ops[eng].append(o)
        self.nops += 1
        return o

    def dma(self, eng, out_ap, in_ap, sb, r=(), w=()):
        if sb.dsem is None:
            sb.dsem = self.get_dsem()
        ds = sb.dsem
        o = Op(eng, lambda e: e.dma_start(out=out_ap, in_=in_ap))
        o.is_dma = True
        o.dsem = ds
        ds.count += 16
        o.dcount = ds.count
        self._track(o, r, w, ("dma", id(ds)))
        self.ops[eng].append(o)
        self.nops += 1
        return o

    def barrier(self):
        lasts = []
        for e in ("pe", "act", "dve", "pool"):
            found = None
            for o in reversed(self.ops[e]):
                if not o.is_dma and o.fn is not None:
                    found = o
                    break
            if found is not None:
                found.signal = True
                self.last_sig[e] = found
            if e in self.last_sig:
                lasts.append(self.last_sig[e])
        dl = []
        for ds in self.all_dsems:
            if ds.count > 0:
                d = Op("sp", None)
                d.is_dma = True
                d.dsem = ds
                d.dcount = ds.count
                dl.append(d)
        for e in ENGS:
            b = Op(e, None)
            b.deps = list(lasts) + dl
            self.ops[e].append(b)

    def emit(self):
        nc = self.nc
        for e in ("pe", "act", "dve", "pool"):
            for o in self.ops[e]:
                if o.signal and not o.is_dma and o.fn is not None:
                    self.ecnt[e] += 1
                    o.cnt = self.ecnt[e]
        esem = self.esem
        oplists = self.ops
        self.ops = {e: [] for e in ENGS}

        def run(e, engobj):
            seen = {}
            for o in oplists[e]:
                for d in o.deps:
                    if d.is_dma:
                        key = id(d.dsem)
                        if seen.get(key, 0) < d.dcount:
                            seen[key] = d.dcount
                            engobj.wait_ge(d.dsem.h, d.dcount)
                    else:
                        if d.eng == e and e == "pe":
                            continue
                        if seen.get(d.eng, 0) < d.cnt:
                            seen[d.eng] = d.cnt
                            engobj.wait_ge(esem[d.eng], d.cnt)
                if o.fn is None:
                    continue
                ins = o.fn(engobj)
                if o.is_dma:
                    ins.then_inc(o.dsem.h, 16)
                elif o.signal:
                    ins.then_inc(esem[e], 1)

        with nc.Block() as block:
            @block.tensor
            def _(eng):
                run("pe", eng)

            @block.scalar
            def _(eng):
                run("act", eng)

            @block.vector
            def _(eng):
                run("dve", eng)

            @block.gpsimd
            def _(eng):
                run("pool", eng)

            @block.sync
            def _(eng):
                run("sp", eng)


def split_sizes(total, maxsz, mult=128):
    n = -(-total // maxsz)
    base = -(-(total // mult) // n) * mult
    out = []
    t = 0
    while t < total:
        s = min(base, total - t)
        out.append((t, s))
        t += s
    return out


def groups512(t0, n):
    out = []
    t = 0
    while t < n:
        s = min(512, n - t)
        out.append((t0 + t, s))
        t += s
    return out


def segs(t0, n):
    out = []
    if t0 < NCTX:
        m = min(n, NCTX - t0)
        out.append((t0, m, 1))
        if n > m:
            out.append((t0 + m, n - m, 0))
    else:
        out.append((t0, n, 0))
    return out


def build(NL, DEPTH, dbg=()):
    NT = NCTX + NL
    TT = NT // 128
    nc = bass.Bass("TRN2", target_bir_lowering=False)

    def din(name, shape):
        return nc.dram_tensor(name, list(shape), F32, kind="ExternalInput").ap()

    x_in = din("x", [NL, D])
    ctx_in = din("ctx", [NCTX, D])
    c_in = din("c2", [2, D])
    W = {}
    for name, shape in [("w_mod", [DEPTH, D, 6 * D]), ("b_mod", [DEPTH, 6 * D]), ("norm1_g", [DEPTH, D]),
                        ("norm2_g", [DEPTH, D]), ("w_in", [DEPTH, D, N_IN]), ("attn_qnorm_g", [DEPTH, 128]),
                        ("attn_knorm_g", [DEPTH, 128]), ("attn_lambda", [DEPTH, 4, 128]),
                        ("attn_subln_g", [DEPTH, 256]), ("rnn_conv_w", [DEPTH, 4, 1024]),
                        ("rnn_conv_b", [DEPTH, 1024]), ("rnn_wa", [DEPTH, 2, 8, 128, 128]),
                        ("rnn_ba", [DEPTH, 2, 1024]), ("rnn_wx", [DEPTH, 2, 8, 128, 128]),
                        ("rnn_bx", [DEPTH, 2, 1024]), ("rnn_lambda", [DEPTH, 2, 1024]),
                        ("ml_conv_w", [DEPTH, 4, 2048]), ("ml_conv_b", [DEPTH, 2048]), ("ml_gate_b", [DEPTH, 16]),
                        ("ml_norm_g", [DEPTH, 1024]), ("w_branch_attn", [DEPTH, 1024, D]),
                        ("w_branch_rnn", [DEPTH, 1024, D]), ("w_branch_ml", [DEPTH, 1024, D]),
                        ("w_out", [DEPTH, D, D]), ("w_ffn1", [DEPTH, D, DFF]), ("w_ffn3", [DEPTH, D, DFF]),
                        ("w_ffn2", [DEPTH, DFF, D])]:
        W[name] = din(name, shape)
    ident_in = din("k_ident", [128, 128])
    rot_in = din("k_rot", [128, 128])
    cos_in = din("k_cos", [128, NL])
    sin_in = din("k_sin", [128, NL])
    mask_in = din("k_mask", [2, 128, 128])
    sel_in = din("k_sel", [4, 512])
    out_d = nc.dram_tensor("out", [NL, D], F32, kind="ExternalOutput").ap()

    def dscr(name, shape, dt):
        kind = "ExternalOutput" if name in dbg else "Internal"
        return nc.dram_tensor(name, list(shape), dt, kind=kind).ap()

    hT = dscr("hT", [D, NT], F32)
    xmT = dscr("xmT", [D, NT], BF16)
    uT = dscr("uT", [N_IN, NT], F32)
    vtok = dscr("vtok", [NT, 3072], F32)
    qkT = dscr("qkT", [2048, NT], BF16)
    qkm = dscr("qkm", [2048, NT], BF16)
    aT = dscr("aT", [1024, NT], BF16)
    rT = dscr("rT", [1024, NT], BF16)
    mT = dscr("mT", [1024, NT], BF16)
    zT = dscr("zT", [D, NT], BF16)
    actT = dscr("actT", [DFF, NT], BF16)

    with ExitStack() as gst:
        S = Sched(nc, gst)

        uid = [0]

        def sbt(st, name, shape, dt=F32):
            uid[0] += 1
            name = "%s_u%d" % (name, uid[0])
            t = st.enter_context(nc.sbuf_tensor(name, list(shape), dt))
            return t, Buf(name)

        PS = []
        for i in range(8):
            t = gst.enter_context(nc.psum_tensor("ps%d" % i, [128, 512], F32))
            PS.append((t, Buf("ps%d" % i)))

        ident, Bident = sbt(gst, "ident", [128, 128])
        rotb, Brotb = sbt(gst, "rotb", [128, 128], BF16)
        onesb, Bonesb = sbt(gst, "onesb", [128, 128], BF16)
        ones32, Bones32 = sbt(gst, "ones32", [128, 128])
        maskt, Bmask = sbt(gst, "maskt", [128, 2, 128])
        sel, Bsel = sbt(gst, "sel", [4, 512])
        sT, BsT = sbt(gst, "sT", [128, KT, 2])
        epsc, Bepsc = sbt(gst, "epsc", [128, 1])
        modT, BmodT = sbt(gst, "modT", [128, 96, 2])
        G1, BG1 = sbt(gst, "G1", [128, KT, 2])
        G2, BG2 = sbt(gst, "G2", [128, KT, 2])
        gqk, Bgqk = sbt(gst, "gqk", [128, 2])
        nlam, Bnlam = sbt(gst, "nlam", [128, 1])
        gsub, Bgsub = sbt(gst, "gsub", [128, 256])
        rcw, Brcw = sbt(gst, "rcw", [128, 32])
        rcb, Brcb = sbt(gst, "rcb", [128, 8])
        rba, Brba = sbt(gst, "rba", [128, 16])
        rbx, Brbx = sbt(gst, "rbx", [128, 16])
        rnsp, Brnsp = sbt(gst, "rnsp", [128, 16])
        mcw, Bmcw = sbt(gst, "mcw", [128, 64])
        mcb, Bmcb = sbt(gst, "mcb", [128, 16])
        mgb, Bmgb = sbt(gst, "mgb", [4, 4])
        mng, Bmng = sbt(gst, "mng", [128, 1024])
        Bscr = Buf("dram_misc")

        with ExitStack() as st:
            rot32, Brot32 = sbt(st, "rot32", [128, 128])
            c2, Bc2 = sbt(st, "c2sb", [2, D])
            S.dma("sp", ident[:], ident_in, Bident, w=[Bident])
            S.dma("sp", rot32[:], rot_in, Brot32, w=[Brot32])
            S.dma("sp", maskt[:], mask_in.rearrange("a p t -> p a t"), Bmask, w=[Bmask])
            S.dma("sp", sel[:], sel_in, Bsel, w=[Bsel])
            S.dma("sp", c2[:], c_in, Bc2, w=[Bc2])
            S.op("dve", lambda e: e.tensor_copy(rotb[:], rot32[:]), r=[Brot32], w=[Brotb])
            S.op("dve", lambda e: e.memset(onesb[:], 1.0), w=[Bonesb])
            S.op("dve", lambda e: e.memset(ones32[:], 1.0), w=[Bones32])
            S.op("dve", lambda e: e.memset(epsc[:], EPS), w=[Bepsc])
            S.op("act", lambda e: e.activation(c2[:], c2[:], AF.Silu), r=[Bc2], w=[Bc2])
            pt, Bpt = PS[0]
            for kt in range(KT):
                S.op("pe", lambda e, kt=kt: e.transpose(pt[:, 2 * kt:2 * kt + 2], c2[0:2, kt * 128:(kt + 1) * 128],
                                                        ident[0:2, 0:2]), r=[Bc2, Bident], w=[Bpt])
            S.op("dve", lambda e: e.tensor_copy(sT[:].rearrange("p k r -> p (k r)"), pt[:, 0:2 * KT]), r=[Bpt], w=[BsT])
            xin = [sbt(st, "xin%d" % i, [128, D]) for i in range(2)]
            hst = [sbt(st, "hst%d" % i, [128, KT, 128]) for i in range(2)]
            for tt in range(TT):
                xt, Bxt = xin[tt % 2]
                ht, Bht = hst[tt % 2]
                src = ctx_in[tt * 128:(tt + 1) * 128, :] if tt < 2 else x_in[(tt - 2) * 128:(tt - 1) * 128, :]
                S.dma("sp", xt[:], src, Bxt, w=[Bxt])
                for q in range(4):
                    p, Bp = PS[1 + (tt * 4 + q) % 4]
                    for j in range(4):
                        kt = q * 4 + j
                        S.op("pe", lambda e, p=p, j=j, kt=kt, xt=xt: e.transpose(
                            p[:, j * 128:(j + 1) * 128], xt[:, kt * 128:(kt + 1) * 128], ident[:]),
                            r=[Bxt, Bident], w=[Bp])
                    eng = "dve" if q % 2 == 0 else "act"
                    if eng == "dve":
                        S.op("dve", lambda e, p=p, q=q, ht=ht: e.tensor_copy(
                            ht[:, q * 4:(q + 1) * 4, :], p[:].rearrange("p (j t) -> p j t", j=4)), r=[Bp], w=[Bht])
                    else:
                        S.op("act", lambda e, p=p, q=q, ht=ht: e.copy(
                            ht[:, q * 4:(q + 1) * 4, :], p[:].rearrange("p (j t) -> p j t", j=4)), r=[Bp], w=[Bht])
                S.dma("sp", hT[:, tt * 128:(tt + 1) * 128].rearrange("(k p) t -> p k t", p=128), ht[:], Bht, r=[Bht])
            S.barrier()
            S.emit()
            S.release([Bident, Brot32, Bmask, Bsel, Bc2] + [b for _, b in xin] + [b for _, b in hst])

        def load_T(st, dst_ap, Bdst, src_ap, R, tmpname):
            t, Bt = sbt(st, tmpname, [R, 128])
            S.dma("sp", t[:], src_ap, Bt, w=[Bt])
            p, Bp = PS[0]
            S.op("pe", lambda e: e.transpose(p[:, 0:R], t[:], ident[0:R, 0:R]), r=[Bt, Bident], w=[Bp])
            S.op("dve", lambda e: e.tensor_copy(dst_ap, p[:, 0:R]), r=[Bp], w=[Bdst])
            return Bt

        def load_rowbc(st, dst_ap, Bdst, src_ap, Fdim, tmpname):
            t, Bt = sbt(st, tmpname, [1, Fdim])
            S.dma("sp", t[:], src_ap, Bt, w=[Bt])
            for f0 in range(0, Fdim, 512):
                fn = min(512, Fdim - f0)
                p, Bp = PS[0]
                S.op("pe", lambda e, f0=f0, fn=fn: e.matmul(p[:, 0:fn], ones32[0:1, :], t[0:1, f0:f0 + fn], start=True,
                                                           stop=True), r=[Bt, Bones32], w=[Bp])
                S.op("dve", lambda e, f0=f0, fn=fn: e.tensor_copy(dst_ap[:, f0:f0 + fn], p[:, 0:fn]), r=[Bp], w=[Bdst])
            return Bt

        def phase_params(l):
            lam_init = 0.8 - 0.6 * math.exp(-0.3 * l)
            with ExitStack() as st:
                rel = []
                bm, Bbm = sbt(st, "bm", [128, 96])
                g1, Bg1 = sbt(st, "g1", [128, KT])
                g2, Bg2 = sbt(st, "g2", [128, KT])
                rlam, Brlam = sbt(st, "rlam", [128, 16])
                rel.append(load_T(st, bm[:], Bbm, W["b_mod"][l].rearrange("(m p) -> m p", p=128), 96, "t_bm"))
                rel.append(load_T(st, g1[:], Bg1, W["norm1_g"][l].rearrange("(m p) -> m p", p=128), KT, "t_g1"))
                rel.append(load_T(st, g2[:], Bg2, W["norm2_g"][l].rearrange("(m p) -> m p", p=128), KT, "t_g2"))
                rel.append(load_T(st, gqk[:, 0:1], Bgqk, W["attn_qnorm_g"][l].rearrange("(m p) -> m p", p=128), 1, "t_gq"))
                rel.append(load_T(st, gqk[:, 1:2], Bgqk, W["attn_knorm_g"][l].rearrange("(m p) -> m p", p=128), 1, "t_gk"))
                rel.append(load_T(st, rcw[:], Brcw, W["rnn_conv_w"][l].rearrange("a (k p) -> (a k) p", p=128), 32, "t_rcw"))
                rel.append(load_T(st, rcb[:], Brcb, W["rnn_conv_b"][l].rearrange("(k p) -> k p", p=128), 8, "t_rcb"))
                rel.append(load_T(st, rba[:], Brba, W["rnn_ba"][l].rearrange("a (k p) -> (a k) p", p=128), 16, "t_rba"))
                rel.append(load_T(st, rbx[:], Brbx, W["rnn_bx"][l].rearrange("a (k p) -> (a k) p", p=128), 16, "t_rbx"))
                rel.append(load_T(st, rlam[:], Brlam, W["rnn_lambda"][l].rearrange("a (k p) -> (a k) p", p=128), 16, "t_rl"))
                rel.append(load_T(st, mcw[:], Bmcw, W["ml_conv_w"][l].rearrange("a (k p) -> (a k) p", p=128), 64, "t_mcw"))
                rel.append(load_T(st, mcb[:], Bmcb, W["ml_conv_b"][l].rearrange("(k p) -> k p", p=128), 16, "t_mcb"))
                rel.append(load_rowbc(st, gsub[:], Bgsub, W["attn_subln_g"][l].rearrange("(o f) -> o f", o=1), 256, "t_gs"))
                rel.append(load_rowbc(st, mng[:], Bmng, W["ml_norm_g"][l].rearrange("(o f) -> o f", o=1), 1024, "t_mn"))
                S.op("dve", lambda e: e.tensor_scalar_mul(gsub[:], gsub[:], 1.0 - lam_init), r=[Bgsub], w=[Bgsub])
                for j in range(4):
                    S.dma("sp", mgb[:, j:j + 1], W["ml_gate_b"][l, j * 4:(j + 1) * 4].rearrange("(h o) -> h o", o=1),
                          Bmgb, w=[Bmgb])
                S.op("act", lambda e: e.activation(rlam[:], rlam[:], AF.Exp, scale=-1.0), r=[Brlam], w=[Brlam])
                S.op("act", lambda e: e.activation(rlam[:], rlam[:], AF.Ln, bias=1.0), r=[Brlam], w=[Brlam])
                S.op("dve", lambda e: e.tensor_scalar_mul(rnsp[:], rlam[:], -8.0), r=[Brlam], w=[Brnsp])
                lv, Blv = sbt(st, "lv", [1, 512])
                lw, Blw = sbt(st, "lw", [1, 8])
                S.dma("sp", lv[:], W["attn_lambda"][l].rearrange("(o a) f -> o (a f)", o=1), Blv, w=[Blv])
                S.op("dve", lambda e: e.tensor_tensor(lv[0:1, 0:128], lv[0:1, 0:128], lv[0:1, 128:256], ALU.mult), r=[Blv], w=[Blv])
                S.op("dve", lambda e: e.tensor_tensor(lv[0:1, 256:384], lv[0:1, 256:384], lv[0:1, 384:512], ALU.mult), r=[Blv], w=[Blv])
                S.op("dve", lambda e: e.tensor_reduce(lw[0:1, 0:1], lv[0:1, 0:128], AX.X, ALU.add), r=[Blv], w=[Blw])
                S.op("dve", lambda e: e.tensor_reduce(lw[0:1, 1:2], lv[0:1, 256:384], AX.X, ALU.add), r=[Blv], w=[Blw])
                S.op("act", lambda e: e.activation(lw[0:1, 2:4], lw[0:1, 0:2], AF.Exp), r=[Blw], w=[Blw])
                S.op("dve", lambda e: e.tensor_tensor(lw[0:1, 4:5], lw[0:1, 3:4], lw[0:1, 2:3], ALU.subtract), r=[Blw], w=[Blw])
                S.op("dve", lambda e: e.tensor_scalar_add(lw[0:1, 5:6], lw[0:1, 4:5], -lam_init), r=[Blw], w=[Blw])
                p, Bp = PS[0]
                S.op("pe", lambda e: e.matmul(p[:, 0:1], ones32[0:1, :], lw[0:1, 5:6], start=True, stop=True),
                     r=[Blw, Bones32], w=[Bp])
                S.op("dve", lambda e: e.tensor_copy(nlam[:], p[:, 0:1]), r=[Bp], w=[Bnlam])
                wm = [[sbt(st, "wm%d_%d" % (i, j), [128, 4, 512]) for j in range(4)] for i in range(2)]
                pm, Bpm = PS[1]
                wmod = W["w_mod"][l].rearrange("(k p) c -> p k c", p=128)
                for cb in range(24):
                    bufs = wm[cb % 2]
                    for j in range(4):
                        t, Bt = bufs[j]
                        S.dma("sp", t[:], wmod[:, j * 4:(j + 1) * 4, cb * 512:(cb + 1) * 512], Bt, w=[Bt])
                    for mi in range(4):
                        m = cb * 4 + mi
                        for kt in range(KT):
                            t, Bt = bufs[kt // 4]
                            S.op("pe", lambda e, t=t, kt=kt, mi=mi, m=m: e.matmul(
                                pm[:, 2 * m:2 * m + 2], t[:, kt % 4, mi * 128:(mi + 1) * 128], sT[:, kt, :],
                                start=(kt == 0), stop=(kt == KT - 1)), r=[Bt, BsT], w=[Bpm])
                for r_ in range(2):
                    S.op("dve", lambda e, r_=r_: e.tensor_tensor(
                        modT[:, :, r_], pm[:, 0:192].rearrange("p (m r) -> p m r", r=2)[:, :, r_], bm[:], ALU.add),
                        r=[Bpm, Bbm], w=[BmodT])
                    for (Gt, BG, gt_, Bg_, j) in ((G1, BG1, g1, Bg1, 1), (G2, BG2, g2, Bg2, 4)):
                        S.op("dve", lambda e, Gt=Gt, gt_=gt_, j=j, r_=r_: e.scalar_tensor_tensor(
                            Gt[:, :, r_], modT[:, j * 16:(j + 1) * 16, r_], 1.0, gt_[:], ALU.add, ALU.mult),
                            r=[BmodT, Bg_], w=[BG])
                S.barrier()
                S.emit()
                S.release(rel + [Blv, Bmgb] + [b for row in wm for _, b in row])

        def phase_norm(Gt, BG, shj):
            with ExitStack() as st:
                hin = [sbt(st, "n_h%d" % i, [128, KT, 512]) for i in range(2)]
                sq, Bsq = sbt(st, "n_sq", [128, KT, 512], BF16)
                xo = [sbt(st, "n_xo%d" % i, [128, KT, 512], BF16) for i in range(2)]
                rstd, Brstd = sbt(st, "n_rstd", [128, 512])
                tmp = [sbt(st, "n_tmp%d" % i, [128, 512]) for i in range(4)]
                for gi, (t0, n) in enumerate(groups512(0, NT)):
                    h, Bh = hin[gi % 2]
                    xo_, Bxo = xo[gi % 2]
                    S.dma("sp", h[:, :, 0:n], hT[:, t0:t0 + n].rearrange("(k p) t -> p k t", p=128), Bh, w=[Bh])
                    S.op("act", lambda e, h=h, n=n: e.activation(sq[:, :, 0:n], h[:, :, 0:n], AF.Square), r=[Bh], w=[Bsq])
                    p, Bp = PS[gi % 2]
                    for kt in range(KT):
                        S.op("pe", lambda e, p=p, kt=kt, n=n: e.matmul(p[:, 0:n], onesb[:], sq[:, kt, 0:n], start=(kt == 0),
                                                                         stop=(kt == KT - 1)), r=[Bsq, Bonesb], w=[Bp])
                    S.op("act", lambda e, p=p, n=n: e.activation(rstd[:, 0:n], p[:, 0:n], AF.Ln, bias=epsc[:], scale=1.0 / D),
                         r=[Bp, Bepsc], w=[Brstd])
                    S.op("act", lambda e, n=n: e.activation(rstd[:, 0:n], rstd[:, 0:n], AF.Exp, scale=-0.5), r=[Brstd], w=[Brstd])
                    for kt in range(KT):
                        tm, Btm = tmp[kt % 4]
                        for (s0, sn, r_) in segs(t0, n):
                            a0 = s0 - t0
                            S.op("dve", lambda e, h=h, kt=kt, a0=a0, sn=sn, r_=r_, tm=tm: e.scalar_tensor_tensor(
                                tm[:, a0:a0 + sn], h[:, kt, a0:a0 + sn], Gt[:, kt, r_:r_ + 1], rstd[:, a0:a0 + sn],
                                ALU.mult, ALU.mult), r=[Bh, BG, Brstd], w=[Btm])
                            eng = "act" if kt % 4 != 3 else "dve"
                            if eng == "act":
                                S.op("act", lambda e, kt=kt, a0=a0, sn=sn, r_=r_, tm=tm, xo_=xo_: e.activation(
                                    xo_[:, kt, a0:a0 + sn], tm[:, a0:a0 + sn], AF.Identity,
                                    bias=modT[:, shj * 16 + kt, r_:r_ + 1], scale=1.0), r=[Btm, BmodT], w=[Bxo])
                            else:
                                S.op("dve", lambda e, kt=kt, a0=a0, sn=sn, r_=r_, tm=tm, xo_=xo_: e.tensor_scalar(
                                    xo_[:, kt, a0:a0 + sn], tm[:, a0:a0 + sn], modT[:, shj * 16 + kt, r_:r_ + 1], None,
                                    ALU.add), r=[Btm, BmodT], w=[Bxo])
                    S.dma("sp", xmT[:, t0:t0 + n].rearrange("(k p) t -> p k t", p=128), xo_[:, :, 0:n], Bxo, r=[Bxo])
                S.barrier()
                S.emit()
                S.release([b for _, b in hin] + [b for _, b in xo])

        def gemm_fm(tag, inputs, streams, colblocks, sg_max, epilogue, extra_setup=None, wwidth=512):
            with ExitStack() as st:
                ns = len(streams)
                X = []
                for i, (scr, kti) in enumerate(inputs):
                    X.append(sbt(st, "%s_x%d" % (tag, i), [128, kti, sg_max], BF16))
                Wb = []
                for s_, (ii, wap) in enumerate(streams):
                    kti = inputs[ii][1]
                    nch = -(-kti // 8)
                    Wb.append([[sbt(st, "%s_w%d_%d_%d" % (tag, s_, par, ch), [128, min(8, kti - ch * 8), wwidth], BF16)
                                for ch in range(nch)] for par in range(2)])
                ctxobj = extra_setup(st) if extra_setup else None
                allb = [b for _, b in X] + [b for s_ in Wb for par in s_ for _, b in par]
                gidx = 0
                cbi = 0
                for sgi, (t0sg, nsg) in enumerate(split_sizes(NT, sg_max)):
                    for i, (scr, kti) in enumerate(inputs):
                        xt, Bx = X[i]
                        S.dma("sp", xt[:, :, 0:nsg], scr[0:kti * 128, t0sg:t0sg + nsg].rearrange("(k p) t -> p k t", p=128),
                              Bx, w=[Bx])
                    for (c0, cw) in colblocks:
                        par = cbi % 2
                        cbi += 1
                        for s_, (ii, wap) in enumerate(streams):
                            kti = inputs[ii][1]
                            wv = wap.rearrange("(k p) c -> p k c", p=128)
                            for ch, (wt, Bw) in enumerate(Wb[s_][par]):
                                k0 = ch * 8
                                k1 = min(kti, k0 + 8)
                                S.dma("pool", wt[:, 0:k1 - k0, 0:cw], wv[:, k0:k1, c0:c0 + cw], Bw, w=[Bw])
                        for mi in range(-(-cw // 128)):
                            mw = min(128, cw - mi * 128)
                            for (g0, n) in groups512(t0sg, nsg):
                                pss = []
                                for s_, (ii, wap) in enumerate(streams):
                                    kti = inputs[ii][1]
                                    xt, Bx = X[ii]
                                    p, Bp = PS[(gidx % 2) * ns + s_] if 2 * ns <= 6 else PS[s_]
                                    for kt in range(kti):
                                        wt, Bw = Wb[s_][par][kt // 8]
                                        S.op("pe", lambda e, p=p, wt=wt, kt=kt, mi=mi, mw=mw, xt=xt, x0=g0 - t0sg, n=n, kti=kti: e.matmul(
                                            p[0:mw, 0:n], wt[:, kt % 8, mi * 128:mi * 128 + mw],
                                            xt[:, kt, x0:x0 + n], start=(kt == 0), stop=(kt == kti - 1)),
                                            r=[Bw, Bx], w=[Bp])
                                    pss.append((p, Bp))
                                epilogue(ctxobj, sgi, t0sg, nsg, c0, mi, mw, g0, n, pss, gidx)
                                gidx += 1
                S.barrier()
                S.emit()
                S.release(allb + (ctxobj["bufs"] if ctxobj and "bufs" in ctxobj else []))

        def phase_inproj(l):
            win = W["w_in"][l]
            SG = 2176
            fm_ranges = [(OFF["aq"], 2048), (OFF["rx"], 4096), (OFF["bg"], 6144)]
            cbs = []
            for (c0, w_) in fm_ranges:
                for c in range(c0, c0 + w_, 512):
                    cbs.append((c, 512))

            def setup(st):
                stg = [sbt(st, "ip_stg%d" % i, [128, SG]) for i in range(3)]
                return dict(stg=stg, bufs=[b for _, b in stg], k=0)

            def epi(cx, sgi, t0sg, nsg, c0, mi, mw, g0, n, pss, gidx):
                p, Bp = pss[0]
                first = (g0 == t0sg)
                if first:
                    cx["k"] += 1
                stg, Bstg = cx["stg"][cx["k"] % 3]
                a0 = g0 - t0sg
                if gidx % 2 == 0:
                    S.op("act", lambda e: e.copy(stg[0:mw, a0:a0 + n], p[0:mw, 0:n]), r=[Bp], w=[Bstg])
                else:
                    S.op("dve", lambda e: e.tensor_copy(stg[0:mw, a0:a0 + n], p[0:mw, 0:n]), r=[Bp], w=[Bstg])
                if g0 + n == t0sg + nsg:
                    r0 = c0 + mi * 128
                    S.dma("sp", uT[r0:r0 + mw, t0sg:t0sg + nsg], stg[0:mw, 0:nsg], Bstg, r=[Bstg])

            gemm_fm("ip", [(xmT, KT)], [(0, win)], cbs, SG, epi, setup)

            def epi_g(cx, sgi, t0sg, nsg, c0, mi, mw, g0, n, pss, gidx):
                epi(cx, sgi, t0sg, nsg, c0, mi, mw, g0, n, pss, gidx)

            gemm_fm("ig", [(xmT, KT)], [(0, win)], [(OFF["mg"] + 4 * j, 4) for j in range(4)], SG, epi_g, setup)

            with ExitStack() as st:
                X, BX = sbt(st, "it_x", [128, KT, SG], BF16)
                Wb = [[sbt(st, "it_w%d_%d" % (par, ch), [128, 8, 512], BF16) for ch in range(2)] for par in range(2)]
                stg = [sbt(st, "it_s%d" % i, [128, 512]) for i in range(3)]
                cols = []
                for ci, name in enumerate(("av", "mv", "mo")):
                    for c in range(0, 1024, 512):
                        cols.append((OFF[name] + c, ci * 1024 + c))
                wv = win.rearrange("(k p) c -> p k c", p=128)
                k = 0
                cbi = 0
                for (t0sg, nsg) in split_sizes(NT, SG):
                    S.dma("sp", X[:, :, 0:nsg], xmT[:, t0sg:t0sg + nsg].rearrange("(k p) t -> p k t", p=128), BX, w=[BX])
                    for (c0, oc0) in cols:
                        par = cbi % 2
                        cbi += 1
                        for ch in range(2):
                            wt, Bw = Wb[par][ch]
                            S.dma("pool", wt[:], wv[:, ch * 8:(ch + 1) * 8, c0:c0 + 512], Bw, w=[Bw])
                        for ti in range(nsg // 128):
                            p, Bp = PS[k % 4]
                            for kt in range(KT):
                                wt, Bw = Wb[par][kt // 8]
                                S.op("pe", lambda e, p=p, kt=kt, ti=ti, wt=wt: e.matmul(
                                    p[:, :], X[:, kt, ti * 128:(ti + 1) * 128], wt[:, kt % 8, :], start=(kt == 0),
                                    stop=(kt == KT - 1)), r=[BX, Bw], w=[Bp])
                            sg_, Bsg = stg[k % 3]
                            if k % 2 == 0:
                                S.op("act", lambda e, sg_=sg_, p=p: e.copy(sg_[:], p[:]), r=[Bp], w=[Bsg])
                            else:
                                S.op("dve", lambda e, sg_=sg_, p=p: e.tensor_copy(sg_[:], p[:]), r=[Bp], w=[Bsg])
                            tk = t0sg + ti * 128
                            S.dma("sp", vtok[tk:tk + 128, oc0:oc0 + 512], sg_[:], Bsg, r=[Bsg])
                            k += 1
                S.barrier()
                S.emit()
                S.release([BX] + [b for par in Wb for _, b in par] + [b for _, b in stg])

        def phase_attn_prep():
            with ExitStack() as st:
                cosT, Bcos = sbt(st, "ap_cos", [128, NL])
                sinT, Bsin = sbt(st, "ap_sin", [128, NL])
                S.dma("sp", cosT[:], cos_in, Bcos, w=[Bcos])
                S.dma("sp", sinT[:], sin_in, Bsin, w=[Bsin])
                qin = [sbt(st, "ap_q%d" % i, [128, NT]) for i in range(2)]
                qo = [sbt(st, "ap_o%d" % i, [128, NT], BF16) for i in range(2)]
                tmps = [dict(sq=sbt(st, "ap_sq%d" % i, [128, 512], BF16), rstd=sbt(st, "ap_rstd%d" % i, [128, 512]),
                             qn=sbt(st, "ap_qn%d" % i, [128, 512]), qnb=sbt(st, "ap_qnb%d" % i, [128, 512], BF16),
                             t1=sbt(st, "ap_t1%d" % i, [128, 512]), t2=sbt(st, "ap_t2%d" % i, [128, 512])) for i in range(3)]
                gcount = [0]
                pend = []

                def first_half(idx, gi, t0, n, q, Bq, o, Bo, which):
                    tb = tmps[gcount[0] % 3]
                    gcount[0] += 1
                    (sq, Bsq), (rstd, Brstd), (qn, Bqn), (qnb, Bqnb), (t1, Bt1), (t2, Bt2) = tb["sq"], tb["rstd"], tb["qn"], tb["qnb"], tb["t1"], tb["t2"]
                    S.op("act", lambda e: e.activation(sq[:, 0:n], q[:, t0:t0 + n], AF.Square), r=[Bq], w=[Bsq])
                    p, Bp = PS[gi % 2]
                    S.op("pe", lambda e: e.matmul(p[:, 0:n], onesb[:], sq[:, 0:n], start=True, stop=True), r=[Bsq, Bonesb], w=[Bp])
                    S.op("act", lambda e: e.activation(rstd[:, 0:n], p[:, 0:n], AF.Ln, bias=epsc[:], scale=1.0 / 128), r=[Bp, Bepsc], w=[Brstd])
                    S.op("act", lambda e: e.activation(rstd[:, 0:n], rstd[:, 0:n], AF.Exp, scale=-0.5), r=[Brstd], w=[Brstd])
                    S.op("dve", lambda e: e.scalar_tensor_tensor(
                        qn[:, 0:n], q[:, t0:t0 + n], gqk[:, which:which + 1], rstd[:, 0:n], ALU.mult, ALU.mult),
                        r=[Bq, Bgqk, Brstd], w=[Bqn])
                    later = []
                    for (s0_, sn, r_) in segs(t0, n):
                        a0 = s0_ - t0
                        if r_ == 1:
                            S.op("pool", lambda e, s0_=s0_, sn=sn, a0=a0: e.tensor_copy(o[:, s0_:s0_ + sn], qn[:, a0:a0 + sn]), r=[Bqn], w=[Bo])
                        else:
                            l0 = s0_ - NCTX
                            S.op("pool", lambda e, a0=a0, sn=sn: e.tensor_copy(qnb[:, a0:a0 + sn], qn[:, a0:a0 + sn]), r=[Bqn], w=[Bqnb])
                            pr, Bpr = PS[2 + gcount[0] % 3]
                            S.op("pool", lambda e, a0=a0, sn=sn, l0=l0: e.tensor_tensor(t1[:, 0:sn], qn[:, a0:a0 + sn], cosT[:, l0:l0 + sn], ALU.mult),
                                 r=[Bqn, Bcos], w=[Bt1])
                            later.append((pr, Bpr, sn, l0, s0_, t1, Bt1, t2, Bt2, o, Bo, qnb, Bqnb, a0))
                    return later

                def second_half(later):
                    for (pr, Bpr, sn, l0, s0_, t1, Bt1, t2, Bt2, o, Bo, qnb, Bqnb, a0) in later:
                        S.op("pe", lambda e, pr=pr, a0=a0, sn=sn, qnb=qnb: e.matmul(pr[:, 0:sn], rotb[:], qnb[:, a0:a0 + sn], start=True, stop=True),
                             r=[Brotb, Bqnb], w=[Bpr])
                        S.op("dve", lambda e, pr=pr, sn=sn, l0=l0, t2=t2: e.tensor_tensor(t2[:, 0:sn], pr[:, 0:sn], sinT[:, l0:l0 + sn], ALU.mult),
                             r=[Bpr, Bsin], w=[Bt2])
                        S.op("dve", lambda e, o=o, s0_=s0_, sn=sn, t1=t1, t2=t2: e.tensor_tensor(o[:, s0_:s0_ + sn], t1[:, 0:sn], t2[:, 0:sn], ALU.add),
                             r=[Bt1, Bt2], w=[Bo])

                for idx in range(16):
                    which = idx // 8
                    q, Bq = qin[idx % 2]
                    o, Bo = qo[idx % 2]
                    row0 = (OFF["aq"] if which == 0 else OFF["ak"]) + (idx % 8) * 128
                    S.dma("sp", q[:], uT[row0:row0 + 128, :], Bq, w=[Bq])
                    prev = None
                    for gi, (t0, n) in enumerate(groups512(0, NT)):
                        cur = first_half(idx, gi, t0, n, q, Bq, o, Bo, which)
                        if prev is not None:
                            second_half(prev)
                        prev = cur
                    second_half(prev)
                    S.dma("sp", qkT[idx * 128:(idx + 1) * 128, :], o[:], Bo, r=[Bo])
                S.barrier()
                S.emit()
                S.release([Bcos, Bsin] + [b for _, b in qin] + [b for _, b in qo])

        def phase_attn():
            scale = 128 ** -0.5
            with ExitStack() as st:
                qTt, BqT = sbt(st, "at_q", [128, 2, NT], BF16)
                kTt, BkT = sbt(st, "at_k", [128, 2, NT], BF16)
                V, BV = sbt(st, "at_v", [128, TT, 257], BF16)
                ao, Bao = sbt(st, "at_ao", [128, 2, NT], BF16)
                A2, BA2 = sbt(st, "at_A2", [128, TT, 256])
                SSQ, BSSQ = sbt(st, "at_SSQ", [128, TT])
                E = [sbt(st, "at_e%d" % i, [128, 512], BF16) for i in range(4)]
                r01s = [sbt(st, "at_r%d" % i, [128, 2]) for i in range(2)]
                a1s = [sbt(st, "at_a1_%d" % i, [128, 256]) for i in range(2)]
                junk, Bjunk = sbt(st, "at_junk", [128, 256])
                STB = [PS[0], PS[1], PS[6]]
                LA = 2
                for h in range(4):
                    for sub in range(2):
                        S.dma("sp", qTt[:, sub, :], qkT[(h * 2 + sub) * 128:(h * 2 + sub + 1) * 128, :], BqT, w=[BqT])
                        S.dma("sp", kTt[:, sub, :], qkT[1024 + (h * 2 + sub) * 128:1024 + (h * 2 + sub + 1) * 128, :], BkT, w=[BkT])
                    S.dma("pool", V[:, :, 0:256], vtok[:, h * 256:(h + 1) * 256].rearrange("(t p) c -> p t c", p=128), BV, w=[BV])
                    S.op("dve", lambda e: e.memset(V[:, :, 256:257], 1.0), w=[BV])
                    iters = []
                    for qg in range(NT // 256):
                        ktiles = list(range(2)) if qg == 0 else list(range(TT))
                        for ki, kt in enumerate(ktiles):
                            iters.append((qg, ki, kt, len(ktiles)))
                    nit = len(iters)

                    def emit_st(i):
                        qg, ki, kt, nk = iters[i]
                        q0 = qg * 256
                        pst, Bpst = STB[i % 3]
                        et, Bet = E[i % 4]
                        for sub in range(2):
                            S.op("pe", lambda e, sub=sub: e.matmul(
                                pst[:, sub * 256:(sub + 1) * 256], kTt[:, sub, kt * 128:(kt + 1) * 128],
                                qTt[:, sub, q0:q0 + 256], start=True, stop=True), r=[BkT, BqT], w=[Bpst])
                        S.op("act", lambda e: e.activation(et[:], pst[:], AF.Exp, scale=scale), r=[Bpst], w=[Bet])

                    def emit_pv(i):
                        qg, ki, kt, nk = iters[i]
                        q0 = qg * 256
                        et, Bet = E[i % 4]
                        for sub in range(2):
                            for qs in range(2):
                                pa, Bpa = PS[2 + sub * 2 + qs]
                                S.op("pe", lambda e, pa=pa, sub=sub, qs=qs: e.matmul(
                                    pa[:, 0:257], et[:, sub * 256 + qs * 128:sub * 256 + (qs + 1) * 128], V[:, kt, :],
                                    start=(ki == 0), stop=(ki == nk - 1)), r=[Bet, BV], w=[Bpa])
                        if ki != nk - 1:
                            return
                        for qs in range(2):
                            p0, Bp0 = PS[2 + qs]
                            p1, Bp1 = PS[4 + qs]
                            tt = qg * 2 + qs
                            r01, Br01 = r01s[qs]
                            a1, Ba1 = a1s[qs]
                            S.op("dve", lambda e, p0=p0, r01=r01: e.reciprocal(r01[:, 0:1], p0[:, 256:257]), r=[Bp0], w=[Br01])
                            S.op("dve", lambda e, p1=p1, r01=r01: e.reciprocal(r01[:, 1:2], p1[:, 256:257]), r=[Bp1], w=[Br01])
                            S.op("dve", lambda e, r01=r01: e.tensor_tensor(r01[:, 1:2], r01[:, 1:2], nlam[:], ALU.mult), r=[Br01, Bnlam], w=[Br01])
                            S.op("dve", lambda e, p0=p0, r01=r01, a1=a1: e.tensor_scalar(a1[:], p0[:, 0:256], r01[:, 0:1], None, ALU.mult),
                                 r=[Bp0, Br01], w=[Ba1])
                            S.op("dve", lambda e, p1=p1, r01=r01, a1=a1, tt=tt: e.scalar_tensor_tensor(
                                A2[:, tt, :], p1[:, 0:256], r01[:, 1:2], a1[:], ALU.mult, ALU.add), r=[Bp1, Br01, Ba1], w=[BA2])
                            S.op("dve", lambda e, tt=tt: e.scalar_tensor_tensor(
                                junk[:], A2[:, tt, :], 1.0, A2[:, tt, :], ALU.mult, ALU.mult, accum_out=SSQ[:, tt:tt + 1]),
                                r=[BA2], w=[Bjunk, BSSQ])

                    for i in range(min(LA, nit)):
                        emit_st(i)
                    for i in range(nit):
                        if i + LA < nit:
                            emit_st(i + LA)
                        emit_pv(i)
                    S.op("act", lambda e: e.activation(SSQ[:], SSQ[:], AF.Sqrt, bias=epsc[:], scale=1.0 / 256), r=[BSSQ, Bepsc], w=[BSSQ])
                    S.op("dve", lambda e: e.reciprocal(SSQ[:], SSQ[:]), r=[BSSQ], w=[BSSQ])
                    for tt in range(TT):
                        a1, Ba1 = a1s[tt % 2]
                        S.op("dve", lambda e, tt=tt, a1=a1: e.scalar_tensor_tensor(a1[:], A2[:, tt, :], SSQ[:, tt:tt + 1], gsub[:], ALU.mult, ALU.mult),
                             r=[BA2, BSSQ, Bgsub], w=[Ba1])
                        ptr, Bptr = PS[7] if tt % 2 == 0 else PS[6]
                        for vc in range(2):
                            S.op("pe", lambda e, ptr=ptr, vc=vc, a1=a1: e.transpose(ptr[:, vc * 128:(vc + 1) * 128], a1[:, vc * 128:(vc + 1) * 128], ident[:]),
                                 r=[Ba1, Bident], w=[Bptr])
                        S.op("act", lambda e, ptr=ptr, tt=tt: e.copy(ao[:, :, tt * 128:(tt + 1) * 128], ptr[:, 0:256].rearrange("p (v t) -> p v t", v=2)),
                             r=[Bptr], w=[Bao])
                    for vc in range(2):
                        S.dma("sp", aT[h * 256 + vc * 128:h * 256 + (vc + 1) * 128, :], ao[:, vc, :], Bao, r=[Bao])
                S.barrier()
                S.emit()
                S.release([BqT, BkT, BV, Bao])

        def conv_ops(xin, Bx, y, By, wt, Bw, wcol, bt, Bb, bcol):
            for (s0, sn) in ((0, NCTX), (NCTX, NL)):
                S.op("dve", lambda e, s0=s0, sn=sn: e.tensor_scalar(y[:, s0:s0 + sn], xin[:, s0:s0 + sn], wcol(1), bcol, ALU.mult, ALU.add),
                     r=[Bx, Bw, Bb], w=[By])
                S.op("dve", lambda e, s0=s0, sn=sn: e.scalar_tensor_tensor(y[:, s0 + 1:s0 + sn], xin[:, s0:s0 + sn - 1], wcol(0), y[:, s0 + 1:s0 + sn],
                                                                          ALU.mult, ALU.add), r=[Bx, Bw, By], w=[By])
                S.op("dve", lambda e, s0=s0, sn=sn: e.scalar_tensor_tensor(y[:, s0:s0 + sn - 1], xin[:, s0 + 1:s0 + sn], wcol(2), y[:, s0:s0 + sn - 1],
                                                                          ALU.mult, ALU.add), r=[Bx, Bw, By], w=[By])
                S.op("dve", lambda e, s0=s0, sn=sn: e.scalar_tensor_tensor(y[:, s0:s0 + sn - 2], xin[:, s0 + 2:s0 + sn], wcol(3), y[:, s0:s0 + sn - 2],
                                                                          ALU.mult, ALU.add), r=[Bx, Bw, By], w=[By])

        def rev_ap(t, p0, p1, rowlen, c0, n):
            return bass.AP(t, p0 * rowlen + c0 + n - 1, [[rowlen, p1 - p0], [-1, n]])

        def phase_rglru(l):
            with ExitStack() as st:
                xs = [sbt(st, "rg_x%d" % i, [128, NT]) for i in range(2)]
                gs = [sbt(st, "rg_g%d" % i, [128, NT]) for i in range(2)]
                xc, Bxc = sbt(st, "rg_xc", [128, NT])
                xcb, Bxcb = sbt(st, "rg_xcb", [128, NT], BF16)
                Rt, BR = sbt(st, "rg_R", [128, NT])
                It, BI = sbt(st, "rg_I", [128, NT])
                hh = [sbt(st, "rg_h%d" % d, [128, NT]) for d in range(2)]
                yo, Byo = sbt(st, "rg_yo", [128, NT], BF16)
                wg = [[[sbt(st, "rg_w%d_%d_%d" % (par, d, j), [128, 128], BF16) for j in range(2)] for d in range(2)] for par in range(2)]
                for k in range(8):
                    x, Bx = xs[k % 2]
                    g, Bg = gs[k % 2]
                    S.dma("sp", x[:], uT[OFF["rx"] + k * 128:OFF["rx"] + (k + 1) * 128, :], Bx, w=[Bx])
                    S.dma("sp", g[:], uT[OFF["rg"] + k * 128:OFF["rg"] + (k + 1) * 128, :], Bg, w=[Bg])
                    wk = wg[k % 2]
                    for d in range(2):
                        S.dma("pool", wk[d][0][0][:], W["rnn_wa"][l, d, k], wk[d][0][1], w=[wk[d][0][1]])
                        S.dma("pool", wk[d][1][0][:], W["rnn_wx"][l, d, k], wk[d][1][1], w=[wk[d][1][1]])
                    conv_ops(x, Bx, xc, Bxc, rcw, Brcw, lambda tap, k=k: rcw[:, tap * 8 + k:tap * 8 + k + 1], rcb, Brcb, rcb[:, k:k + 1])
                    S.op("act", lambda e: e.copy(xcb[:], xc[:]), r=[Bxc], w=[Bxcb])
                    S.op("act", lambda e, g=g: e.activation(g[:], g[:], AF.Gelu), r=[Bg], w=[Bg])
                    for d in range(2):
                        col = d * 8 + k
                        h, Bh = hh[d]
                        for gi, (t0, n) in enumerate(groups512(0, NT)):
                            pr, Bpr = PS[(gi % 2) * 2]
                            pi, Bpi = PS[(gi % 2) * 2 + 1]
                            S.op("pe", lambda e, pr=pr, d=d, t0=t0, n=n, wk=wk: e.matmul(pr[:, 0:n], wk[d][0][0][:], xcb[:, t0:t0 + n], start=True, stop=True),
                                 r=[wk[d][0][1], Bxcb], w=[Bpr])
                            S.op("pe", lambda e, pi=pi, d=d, t0=t0, n=n, wk=wk: e.matmul(pi[:, 0:n], wk[d][1][0][:], xcb[:, t0:t0 + n], start=True, stop=True),
                                 r=[wk[d][1][1], Bxcb], w=[Bpi])
                            S.op("act", lambda e, pr=pr, t0=t0, n=n, col=col: e.activation(Rt[:, t0:t0 + n], pr[:, 0:n], AF.Sigmoid, bias=rba[:, col:col + 1], scale=1.0),
                                 r=[Bpr, Brba], w=[BR])
                            S.op("act", lambda e, pi=pi, t0=t0, n=n, col=col: e.activation(It[:, t0:t0 + n], pi[:, 0:n], AF.Sigmoid, bias=rbx[:, col:col + 1], scale=1.0),
                                 r=[Bpi, Brbx], w=[BI])
                        S.op("act", lambda e, col=col: e.activation(Rt[:], Rt[:], AF.Exp, scale=rnsp[:, col:col + 1]), r=[BR, Brnsp], w=[BR])
                        S.op("dve", lambda e, h=h: e.tensor_tensor(h[:], Rt[:], Rt[:], ALU.mult), r=[BR], w=[Bh])
                        S.op("act", lambda e, h=h: e.activation(h[:], h[:], AF.Sqrt, bias=1.0, scale=-1.0), r=[Bh], w=[Bh])
                        S.op("pool", lambda e: e.tensor_tensor(It[:], It[:], xc[:], ALU.mult), r=[BI, Bxc], w=[BI])
                        S.op("dve", lambda e, h=h: e.tensor_tensor(It[:], It[:], h[:], ALU.mult), r=[BI, Bh], w=[BI])
                        if d == 0:
                            S.op("dve", lambda e, h=h: e.tensor_tensor_scan(h[:], Rt[:], It[:], 0.0, ALU.mult, ALU.add), r=[BR, BI], w=[Bh])
                        else:
                            S.op("dve", lambda e, h=h: e.tensor_tensor_scan(rev_ap(h, 0, 128, NT, 0, NCTX), rev_ap(Rt, 0, 128, NT, 0, NCTX),
                                                                            rev_ap(It, 0, 128, NT, 0, NCTX), 0.0, ALU.mult, ALU.add), r=[BR, BI], w=[Bh])
                            S.op("dve", lambda e, h=h: e.tensor_tensor_scan(rev_ap(h, 0, 128, NT, NCTX, NL), rev_ap(Rt, 0, 128, NT, NCTX, NL),
                                                                            rev_ap(It, 0, 128, NT, NCTX, NL), h[:, 0:1], ALU.mult, ALU.add),
                                 r=[BR, BI, Bh], w=[Bh])
                    S.op("dve", lambda e: e.tensor_tensor(hh[0][0][:], hh[0][0][:], hh[1][0][:], ALU.add), r=[hh[0][1], hh[1][1]], w=[hh[0][1]])
                    S.op("dve", lambda e, g=g: e.tensor_tensor(yo[:], hh[0][0][:], g[:], ALU.mult), r=[hh[0][1], Bg], w=[Byo])
                    S.dma("sp", rT[k * 128:(k + 1) * 128, :], yo[:], Byo, r=[Byo])
                S.barrier()
                S.emit()
                S.release([b for _, b in xs] + [b for _, b in gs] + [Byo] + [wg[p][d][j][1] for p in range(2) for d in range(2) for j in range(2)])

        def phase_mlstm_prep():
            with ExitStack() as st:
                xs = [sbt(st, "mp_x%d" % i, [128, NT]) for i in range(2)]
                ys = [sbt(st, "mp_y%d" % i, [128, NT]) for i in range(2)]
                os_ = [sbt(st, "mp_o%d" % i, [128, NT], BF16) for i in range(2)]
                for j in range(16):
                    x, Bx = xs[j % 2]
                    y, By = ys[j % 2]
                    o, Bo = os_[j % 2]
                    S.dma("sp", x[:], uT[OFF["mq"] + j * 128:OFF["mq"] + (j + 1) * 128, :], Bx, w=[Bx])
                    conv_ops(x, Bx, y, By, mcw, Bmcw, lambda tap, j=j: mcw[:, tap * 16 + j:tap * 16 + j + 1], mcb, Bmcb, mcb[:, j:j + 1])
                    if j < 8:
                        S.op("act", lambda e, o=o, y=y: e.activation(o[:], y[:], AF.Silu), r=[By], w=[Bo])
                    else:
                        S.op("act", lambda e, y=y: e.activation(y[:], y[:], AF.Silu), r=[By], w=[By])
                        S.op("dve", lambda e, o=o, y=y: e.tensor_scalar(o[:], y[:], 1.0 / 16.0, None, ALU.mult), r=[By], w=[Bo])
                    S.dma("sp", qkm[j * 128:(j + 1) * 128, :], o[:], Bo, r=[Bo])
                S.barrier()
                S.emit()
                S.release([b for _, b in xs] + [b for _, b in os_])

        def phase_mlstm(l):
            with ExitStack() as st:
                nG = [sbt(st, "ml_nG%d" % d, [4, NT]) for d in range(2)]
                COL, BCOL = sbt(st, "ml_col", [128, TT, 32])
                SCB, BSCB = sbt(st, "ml_scb", [128, 8, TT])
                rows = {("nG", 0): nG[0], ("nG", 1): nG[1]}
                relg = []
                with ExitStack() as st2:
                    onesr, Bonesr = sbt(st2, "ml_onesr", [4, NT])
                    t_i = sbt(st2, "ml_ti", [4, NT])
                    t_f = sbt(st2, "ml_tf", [4, NT])
                    t_B = sbt(st2, "ml_tB", [4, NT])
                    t_A = sbt(st2, "ml_tA", [4, NT])
                    t_ws = sbt(st2, "ml_tws", [4, NT])
                    ge, Bge = sbt(st2, "ml_gend", [4, TT])
                    gp, Bgp = sbt(st2, "ml_gprev", [4, TT])
                    sct, Bsct = sbt(st2, "ml_sc", [4, TT])
                    relg = [t_i[1], t_f[1]]
                    S.op("pool", lambda e: e.memset(onesr[:], 1.0), w=[Bonesr])
                    mg0 = OFF["mg"]
                    for d in range(2):
                        it, Bi = t_i
                        ft, Bf = t_f
                        Bt_, BB = t_B
                        At, BA = t_A
                        nGt, BnG = nG[d]
                        E1t, BE1 = t_f
                        wit, Bwi = t_B
                        wst, Bws = t_ws
                        S.dma("sp", it[:], uT[mg0 + d * 8:mg0 + d * 8 + 4, :], Bi, w=[Bi])
                        S.dma("sp", ft[:], uT[mg0 + d * 8 + 4:mg0 + d * 8 + 8, :], Bf, w=[Bf])
                        S.op("dve", lambda e, d=d: e.tensor_scalar(it[:], it[:], mgb[:, d * 2:d * 2 + 1], None, ALU.add), r=[Bi, Bmgb], w=[Bi])
                        S.op("dve", lambda e, d=d: e.tensor_scalar(ft[:], ft[:], mgb[:, d * 2 + 1:d * 2 + 2], None, ALU.add), r=[Bf, Bmgb], w=[Bf])
                        S.op("act", lambda e: e.activation(ft[:], ft[:], AF.Exp, scale=-1.0), r=[Bf], w=[Bf])
                        S.op("act", lambda e: e.activation(ft[:], ft[:], AF.Ln, bias=1.0), r=[Bf], w=[Bf])
                        S.op("dve", lambda e: e.tensor_scalar(ft[:], ft[:], -1.0, None, ALU.mult), r=[Bf], w=[Bf])
                        if d == 0:
                            S.op("dve", lambda e: e.tensor_tensor_scan(Bt_[:], onesr[:], ft[:], 0.0, ALU.mult, ALU.add), r=[Bf, Bonesr], w=[BB])
                        else:
                            S.op("dve", lambda e: e.tensor_tensor_scan(rev_ap(Bt_, 0, 4, NT, 0, NCTX), rev_ap(onesr, 0, 4, NT, 0, NCTX),
                                                                       rev_ap(ft, 0, 4, NT, 0, NCTX), 0.0, ALU.mult, ALU.add), r=[Bf, Bonesr], w=[BB])
                            S.op("dve", lambda e: e.tensor_tensor_scan(rev_ap(Bt_, 0, 4, NT, NCTX, NL), rev_ap(onesr, 0, 4, NT, NCTX, NL),
                                                                       rev_ap(ft, 0, 4, NT, NCTX, NL), Bt_[:, 0:1], ALU.mult, ALU.add),
                                 r=[Bf, Bonesr, BB], w=[BB])
                        S.op("dve", lambda e: e.tensor_tensor(At[:], it[:], Bt_[:], ALU.subtract), r=[Bi, BB], w=[BA])
                        Gt, BGt = it, Bi
                        if d == 0:
                            S.op("dve", lambda e: e.tensor_tensor_scan(Gt[:], At[:], At[:], 0.0, ALU.max, ALU.max), r=[BA], w=[BGt])
                        else:
                            S.op("dve", lambda e: e.tensor_tensor_scan(rev_ap(Gt, 0, 4, NT, 0, NCTX), rev_ap(At, 0, 4, NT, 0, NCTX),
                                                                       rev_ap(At, 0, 4, NT, 0, NCTX), 0.0, ALU.max, ALU.max), r=[BA], w=[BGt])
                            S.op("dve", lambda e: e.tensor_tensor_scan(rev_ap(Gt, 0, 4, NT, NCTX, NL), rev_ap(At, 0, 4, NT, NCTX, NL),
                                                                       rev_ap(At, 0, 4, NT, NCTX, NL), Gt[:, 0:1], ALU.max, ALU.max),
                                 r=[BA, BGt], w=[BGt])
                        S.op("dve", lambda e, nGt=nGt: e.tensor_scalar(nGt[:], Gt[:], -1.0, None, ALU.mult), r=[BGt], w=[BnG])
                        S.op("dve", lambda e, nGt=nGt: e.tensor_tensor(E1t[:], nGt[:], Bt_[:], ALU.subtract), r=[BnG, BB], w=[BE1])
                        S.op("act", lambda e: e.activation(E1t[:], E1t[:], AF.Exp), r=[BE1], w=[BE1])
                        endoff = 127 if d == 0 else 0
                        S.op("dve", lambda e, endoff=endoff: e.tensor_copy(ge[:], bass.AP(Gt, endoff, [[NT, 4], [128, TT]])), r=[BGt], w=[Bge])
                        S.op("dve", lambda e: e.memset(gp[:], 0.0), w=[Bgp])
                        if d == 0:
                            S.op("dve", lambda e: e.tensor_copy(gp[:, 1:TT], ge[:, 0:TT - 1]), r=[Bge], w=[Bgp])
                        else:
                            S.op("dve", lambda e: e.tensor_copy(gp[:, 0:1], ge[:, 1:2]), r=[Bge], w=[Bgp])
                            S.op("dve", lambda e: e.tensor_copy(gp[:, TT - 1:TT], ge[:, 0:1]), r=[Bge], w=[Bgp])
                            if TT > 3:
                                S.op("dve", lambda e: e.tensor_copy(gp[:, 2:TT - 1], ge[:, 3:TT]), r=[Bge], w=[Bgp])
                        S.op("dve", lambda e, nGt=nGt: e.tensor_tensor(
                            wit[:].rearrange("p (c t) -> p c t", t=128), nGt[:].rearrange("p (c t) -> p c t", t=128),
                            bass.AP(gp, 0, [[TT, 4], [1, TT], [0, 128]]), ALU.add), r=[BnG, Bgp, BB], w=[Bwi])
                        S.op("act", lambda e: e.activation(wit[:], wit[:], AF.Exp), r=[Bwi], w=[Bwi])
                        S.op("dve", lambda e: e.tensor_tensor(
                            wst[:].rearrange("p (c t) -> p c t", t=128), At[:].rearrange("p (c t) -> p c t", t=128),
                            bass.AP(ge, 0, [[TT, 4], [1, TT], [0, 128]]), ALU.subtract), r=[BA, Bge], w=[Bws])
                        S.op("act", lambda e: e.activation(wst[:], wst[:], AF.Exp), r=[Bws], w=[Bws])
                        S.op("dve", lambda e: e.tensor_tensor(sct[:], gp[:], ge[:], ALU.subtract), r=[Bgp, Bge], w=[Bsct])
                        S.op("act", lambda e: e.activation(sct[:], sct[:], AF.Exp), r=[Bsct], w=[Bsct])
                        for h in range(4):
                            p, Bp = PS[h % 2]
                            S.op("pe", lambda e, p=p, h=h: e.matmul(p[:, 0:TT], sel[0:4, h * 128:(h + 1) * 128], sct[:], start=True, stop=True),
                                 r=[Bsel, Bsct], w=[Bp])
                            S.op("dve", lambda e, p=p, h=h, d=d: e.tensor_copy(SCB[:, d * 4 + h, :], p[:, 0:TT]), r=[Bp], w=[BSCB])
                        for c in range(TT):
                            p, Bp = PS[2 + c % 2]
                            for qi, (t, Bt) in enumerate((t_A, t_ws, t_B, t_f)):
                                o0 = qi * 4
                                S.op("pe", lambda e, p=p, t=t, c=c, o0=o0: e.transpose(p[:, o0:o0 + 4], t[0:4, c * 128:(c + 1) * 128], ident[0:4, 0:4]),
                                     r=[Bt, Bident], w=[Bp])
                            S.op("dve", lambda e, p=p, c=c, d=d: e.tensor_copy(COL[:, c, d * 16:(d + 1) * 16], p[:, 0:16]), r=[Bp], w=[BCOL])
                    S.barrier()
                    S.emit()

                qTt, BqT = sbt(st, "ml_q", [128, 2, NT], BF16)
                kTt, BkT = sbt(st, "ml_k", [128, 2, NT], BF16)
                ktok, Bktok = sbt(st, "ml_ktok", [128, TT, 256], BF16)
                V, BV = sbt(st, "ml_v", [128, TT, 257], BF16)
                hacc, Bhacc = sbt(st, "ml_hacc", [128, TT, 256])
                mos = [sbt(st, "ml_mo%d" % i, [128, 256]) for i in range(2)]
                mout, Bmout = sbt(st, "ml_mout", [128, 2, NT], BF16)
                C32, BC32 = sbt(st, "ml_c32", [128, 2, 257])
                Cb, BCb = sbt(st, "ml_cb", [128, 2, 257], BF16)
                arg, Barg = sbt(st, "ml_arg", [128, 128])
                SD, BSD = sbt(st, "ml_sd", [128, 128], BF16)
                intras = [sbt(st, "ml_intra%d" % i, [128, 257]) for i in range(3)]
                nd, Bnd = sbt(st, "ml_nd", [128, 257])
                rden, Brden = sbt(st, "ml_rden", [128, 1])
                vws = [sbt(st, "ml_vw%d" % i, [128, 257], BF16) for i in range(2)]
                ssq, Bssq = sbt(st, "ml_ssq", [128, 1])
                junk, Bjunk = sbt(st, "ml_junk", [128, 256])
                hns = [sbt(st, "ml_hn%d" % i, [128, 256]) for i in range(2)]
                sgs = [sbt(st, "ml_sg%d" % i, [128, 256]) for i in range(2)]
                SSQ, BSSQ = sbt(st, "ml_SSQ", [128, TT])
                NGB = [sbt(st, "ml_ngb%d" % i, [128, 512]) for i in range(2)]
                Bkt7 = PS[7][1]
                for h in range(4):
                    for j in range(2):
                        S.dma("sp", qTt[:, j, :], qkm[(h * 2 + j) * 128:(h * 2 + j + 1) * 128, :], BqT, w=[BqT])
                        S.dma("sp", kTt[:, j, :], qkm[1024 + (h * 2 + j) * 128:1024 + (h * 2 + j + 1) * 128, :], BkT, w=[BkT])
                    S.dma("pool", V[:, :, 0:256], vtok[:, 1024 + h * 256:1024 + (h + 1) * 256].rearrange("(t p) c -> p t c", p=128), BV, w=[BV])
                    S.op("dve", lambda e: e.memset(V[:, :, 256:257], 1.0), w=[BV])
                    def ktok_transpose(c):
                        p, Bp = PS[7]
                        pb = p[:].bitcast(BF16)
                        for j in range(2):
                            S.op("pe", lambda e, j=j: e.transpose(pb[:, 512 + j * 128:512 + (j + 1) * 128], kTt[:, j, c * 128:(c + 1) * 128], identb[:]),
                                 r=[BkT, Bidentb], w=[Bkt7])
                        S.op("act", lambda e: e.copy(ktok[:, c, :], pb[:, 512:768]), r=[Bkt7], w=[Bktok])
                    for d in range(2):
                        order = list(range(TT)) if d == 0 else [1, 0] + list(range(TT - 1, 1, -1))
                        nGt, BnG = nG[d]
                        S.op("dve", lambda e: e.memset(C32[:], 0.0), w=[BC32])
                        S.op("dve", lambda e: e.memset(Cb[:], 0.0), w=[BCb])
                        ngb_state = {"g": None, "n": 0, "t": None}
                        cb_ = d * 16

                        def st_a1(step, d=d, order=order, nGt=nGt, BnG=BnG, ngb_state=ngb_state, cb_=cb_, h=h):
                            c = order[step]
                            g4 = c // 4
                            if ngb_state["g"] != g4:
                                ngt, Bng = NGB[ngb_state["n"] % 2]
                                ngb_state["n"] += 1
                                ngb_state["g"] = g4
                                ngb_state["t"] = (ngt, Bng)
                                n4 = min(512, NT - g4 * 512)
                                p, Bp = PS[7]
                                S.op("pe", lambda e: e.matmul(p[:, 0:n4], sel[0:4, h * 128:(h + 1) * 128],
                                                              nGt[0:4, g4 * 512:g4 * 512 + n4], start=True, stop=True),
                                     r=[Bsel, BnG], w=[Bp])
                                S.op("act", lambda e: e.copy(ngt[:, 0:n4], p[:, 0:n4]), r=[Bp], w=[Bng])
                            ngt, Bng = ngb_state["t"]
                            o4 = (c % 4) * 128
                            t0 = c * 128
                            pst, Bpst = PS[0]
                            for j in range(2):
                                S.op("pe", lambda e, j=j: e.matmul(pst[:, 0:128], kTt[:, j, t0:t0 + 128], qTt[:, j, t0:t0 + 128],
                                                                   start=(j == 0), stop=(j == 1)), r=[BkT, BqT], w=[Bpst])
                            S.op("dve", lambda e: e.scalar_tensor_tensor(
                                arg[:], ngt[:, o4:o4 + 128], COL[:, c, cb_ + h:cb_ + h + 1], maskt[:, d, :], ALU.add, ALU.min),
                                r=[Bng, BCOL, Bmask], w=[Barg])
                            S.op("act", lambda e: e.activation(arg[:], arg[:], AF.Exp), r=[Barg], w=[Barg])

                        def st_a1b(step, order=order):
                            c = order[step]
                            pst, Bpst = PS[0]
                            S.op("dve", lambda e: e.tensor_tensor(SD[:], pst[:, 0:128], arg[:], ALU.mult), r=[Bpst, Barg], w=[BSD])
                            pin, Bpin = PS[1]
                            S.op("pe", lambda e: e.matmul(pin[:, 0:257], SD[:], V[:, c, :], start=True, stop=True), r=[BSD, BV], w=[Bpin])
                            it_, Bit_ = intras[step % 3]
                            S.op("act", lambda e: e.copy(it_[:], pin[:, 0:257]), r=[Bpin], w=[Bit_])

                        def st_vw(step, order=order, cb_=cb_, h=h, d=d):
                            c = order[step]
                            if d == 0:
                                ktok_transpose(c)
                            vw, Bvw = vws[step % 2]
                            S.op("act", lambda e: e.activation(vw[:], V[:, c, :], AF.Copy, scale=COL[:, c, cb_ + 4 + h:cb_ + 4 + h + 1]),
                                 r=[BV, BCOL], w=[Bvw])

                        def st_pu(step, order=order):
                            c = order[step]
                            par = step % 2
                            vw, Bvw = vws[par]
                            for j in range(2):
                                pu, Bpu = PS[3 + 2 * par + j]
                                S.op("pe", lambda e, pu=pu, j=j: e.matmul(pu[:, 0:257], ktok[:, c, j * 128:(j + 1) * 128], vw[:], start=True, stop=True),
                                     r=[Bktok, Bvw], w=[Bpu])

                        def st_pit(step, order=order):
                            c = order[step]
                            t0 = c * 128
                            pit, Bpit = PS[2]
                            for j in range(2):
                                S.op("pe", lambda e, j=j: e.matmul(pit[:, 0:257], qTt[:, j, t0:t0 + 128], Cb[:, j, :],
                                                                   start=(j == 0), stop=(j == 1)), r=[BqT, BCb], w=[Bpit])

                        def st_b(step, d=d, order=order, cb_=cb_, h=h):
                            c = order[step]
                            par = step % 2
                            pit, Bpit = PS[2]
                            for j in range(2):
                                pu, Bpu = PS[3 + 2 * par + j]
                                S.op("dve", lambda e, pu=pu, j=j: e.scalar_tensor_tensor(
                                    C32[:, j, :], C32[:, j, :], SCB[:, d * 4 + h, c:c + 1], pu[:, 0:257], ALU.mult, ALU.add),
                                    r=[BC32, BSCB, Bpu], w=[BC32])
                            S.op("act", lambda e: e.copy(Cb[:], C32[:]), r=[BC32], w=[BCb])
                            it_, Bit_ = intras[step % 3]
                            S.op("dve", lambda e: e.scalar_tensor_tensor(
                                nd[:], pit[:, 0:257], COL[:, c, cb_ + 8 + h:cb_ + 8 + h + 1], it_[:], ALU.mult, ALU.add),
                                r=[Bpit, BCOL, Bit_], w=[Bnd])
                            S.op("dve", lambda e: e.tensor_tensor(
                                rden[:], nd[:, 256:257], COL[:, c, cb_ + 12 + h:cb_ + 12 + h + 1], ALU.max),
                                r=[Bnd, BCOL], w=[Brden])
                            S.op("dve", lambda e: e.scalar_tensor_tensor(rden[:], nd[:, 256:257], -1.0, rden[:], ALU.mult, ALU.max),
                                 r=[Bnd, Brden], w=[Brden])
                            S.op("dve", lambda e: e.reciprocal(rden[:], rden[:]), r=[Brden], w=[Brden])
                            if d == 0:
                                S.op("dve", lambda e: e.tensor_scalar(hacc[:, c, :], nd[:, 0:256], rden[:, 0:1], None, ALU.mult),
                                     r=[Bnd, Brden], w=[Bhacc])
                            else:
                                S.op("dve", lambda e: e.scalar_tensor_tensor(hacc[:, c, :], nd[:, 0:256], rden[:, 0:1], hacc[:, c, :], ALU.mult, ALU.add),
                                     r=[Bnd, Brden, Bhacc], w=[Bhacc])

                        st_a1(0); st_a1b(0); st_vw(0); st_pu(0)
                        if TT > 1:
                            st_a1(1); st_a1b(1)
                        for step in range(TT):
                            st_pit(step)
                            if step + 2 < TT:
                                st_a1(step + 2)
                            if step + 1 < TT:
                                st_vw(step + 1)
                                st_pu(step + 1)
                            st_b(step)
                            if step + 2 < TT:
                                st_a1b(step + 2)
                    for c in range(TT):
                        S.op("dve", lambda e, c=c: e.scalar_tensor_tensor(junk[:], hacc[:, c, :], 1.0, hacc[:, c, :], ALU.mult, ALU.mult,
                                                                          accum_out=SSQ[:, c:c + 1]), r=[Bhacc], w=[Bjunk, BSSQ])
                    S.op("act", lambda e: e.activation(SSQ[:], SSQ[:], AF.Sqrt, bias=epsc[:], scale=1.0 / 256), r=[BSSQ, Bepsc], w=[BSSQ])
                    S.op("dve", lambda e: e.reciprocal(SSQ[:], SSQ[:]), r=[BSSQ], w=[BSSQ])
                    for c in range(TT):
                        hn, Bhn = hns[c % 2]
                        sg_, Bsg_ = sgs[c % 2]
                        S.op("dve", lambda e, c=c, h=h, hn=hn: e.scalar_tensor_tensor(hn[:], hacc[:, c, :], SSQ[:, c:c + 1], mng[:, h * 256:(h + 1) * 256], ALU.mult, ALU.mult),
                             r=[Bhacc, BSSQ, Bmng], w=[Bhn])
                        mo, Bmo = mos[c % 2]
                        S.dma("sp", mo[:], vtok[c * 128:(c + 1) * 128, 2048 + h * 256:2048 + (h + 1) * 256], Bmo, w=[Bmo])
                        S.op("act", lambda e, mo=mo, sg_=sg_: e.activation(sg_[:], mo[:], AF.Sigmoid), r=[Bmo], w=[Bsg_])
                        S.op("pool", lambda e, hn=hn, sg_=sg_: e.tensor_tensor(hn[:], hn[:], sg_[:], ALU.mult), r=[Bhn, Bsg_], w=[Bhn])
                        ptr, Bptr = PS[6 + c % 2]
                        for vc in range(2):
                            S.op("pe", lambda e, ptr=ptr, vc=vc, hn=hn: e.transpose(ptr[:, vc * 128:(vc + 1) * 128], hn[:, vc * 128:(vc + 1) * 128], ident[:]),
                                 r=[Bhn, Bident], w=[Bptr])
                        S.op("act", lambda e, ptr=ptr, c=c: e.copy(mout[:, :, c * 128:(c + 1) * 128], ptr[:, 0:256].rearrange("p (v t) -> p v t", v=2)),
                             r=[Bptr], w=[Bmout])
                    for vc in range(2):
                        S.dma("sp", mT[h * 256 + vc * 128:h * 256 + (vc + 1) * 128, :], mout[:, vc, :], Bmout, r=[Bmout])
                S.barrier()
                S.emit()
                S.release(relg + [BqT, BkT, BV, Bmout] + [b for _, b in mos])

        def resid_epilogue(gtj):
            def setup(st):
                hb = [sbt(st, "re_h%d_%d" % (gtj, i), [128, 512]) for i in range(3)]
                return dict(hb=hb, bufs=[b for _, b in hb])

            def epi(cx, sgi, t0sg, nsg, c0, mi, mw, g0, n, pss, gidx):
                p, Bp = pss[0]
                m = (c0 // 128) + mi
                hb, Bhb = cx["hb"][gidx % 3]
                S.dma("act", hb[:, 0:n], hT[m * 128:(m + 1) * 128, g0:g0 + n], Bhb, w=[Bhb])
                for (s0, sn, r_) in segs(g0, n):
                    a0 = s0 - g0
                    S.op("dve", lambda e, a0=a0, sn=sn, r_=r_: e.scalar_tensor_tensor(
                        hb[:, a0:a0 + sn], p[:, a0:a0 + sn], modT[:, gtj * 16 + m, r_:r_ + 1], hb[:, a0:a0 + sn], ALU.mult, ALU.add),
                        r=[Bp, BmodT, Bhb], w=[Bhb])
                S.dma("sp", hT[m * 128:(m + 1) * 128, g0:g0 + n], hb[:, 0:n], Bhb, r=[Bhb])
            return setup, epi

        def phase_merge(l):
            SG = 1536
            bg0 = OFF["bg"]

            def setup(st):
                gts = [[sbt(st, "mg_g%d_%d" % (i, j), [128, 512]) for j in range(3)] for i in range(2)]
                t1 = sbt(st, "mg_t1", [128, 512])
                t2 = sbt(st, "mg_t2", [128, 512])
                zo = [sbt(st, "mg_zo%d" % i, [128, SG], BF16) for i in range(2)]
                return dict(gts=gts, t1=t1, t2=t2, zo=zo, k=0,
                            bufs=[b for row in gts for _, b in row] + [b for _, b in zo])

            def epi(cx, sgi, t0sg, nsg, c0, mi, mw, g0, n, pss, gidx):
                m = (c0 // 128) + mi
                if g0 == t0sg:
                    cx["k"] += 1
                zo, Bzo = cx["zo"][cx["k"] % 2]
                gts = cx["gts"][gidx % 2]
                t1, Bt1 = cx["t1"]
                t2, Bt2 = cx["t2"]
                for j in range(3):
                    gt_, Bgt = gts[j]
                    r0 = bg0 + j * 2048 + m * 128
                    S.dma("sp", gt_[:, 0:n], uT[r0:r0 + 128, g0:g0 + n], Bgt, w=[Bgt])
                    S.op("act", lambda e, gt_=gt_: e.activation(gt_[:, 0:n], gt_[:, 0:n], AF.Sigmoid), r=[Bgt], w=[Bgt])
                a0 = g0 - t0sg
                S.op("dve", lambda e: e.tensor_tensor(t1[:, 0:n], pss[0][0][:, 0:n], gts[0][0][:, 0:n], ALU.mult), r=[pss[0][1], gts[0][1]], w=[Bt1])
                S.op("dve", lambda e: e.tensor_tensor(t2[:, 0:n], pss[1][0][:, 0:n], gts[1][0][:, 0:n], ALU.mult), r=[pss[1][1], gts[1][1]], w=[Bt2])
                S.op("dve", lambda e: e.tensor_tensor(t1[:, 0:n], t1[:, 0:n], t2[:, 0:n], ALU.add), r=[Bt1, Bt2], w=[Bt1])
                S.op("dve", lambda e: e.tensor_tensor(t2[:, 0:n], pss[2][0][:, 0:n], gts[2][0][:, 0:n], ALU.mult), r=[pss[2][1], gts[2][1]], w=[Bt2])
                S.op("dve", lambda e: e.tensor_tensor(zo[:, a0:a0 + n], t1[:, 0:n], t2[:, 0:n], ALU.add), r=[Bt1, Bt2], w=[Bzo])
                if g0 + n == t0sg + nsg:
                    S.dma("sp", zT[m * 128:(m + 1) * 128, t0sg:t0sg + nsg], zo[:, 0:nsg], Bzo, r=[Bzo])

            gemm_fm("mg", [(aT, 8), (rT, 8), (mT, 8)],
                    [(0, W["w_branch_attn"][l]), (1, W["w_branch_rnn"][l]), (2, W["w_branch_ml"][l])],
                    [(c, 512) for c in range(0, D, 512)], SG, epi, setup)
            setup2, epi2 = resid_epilogue(2)
            gemm_fm("op", [(zT, KT)], [(0, W["w_out"][l])], [(c, 512) for c in range(0, D, 512)], 2176, epi2, setup2)

        def phase_ffn(l):
            SG = 2176

            def setup(st):
                s1 = [sbt(st, "ff_s%d" % i, [128, 512]) for i in range(2)]
                ao = [sbt(st, "ff_ao%d" % i, [128, SG], BF16) for i in range(2)]
                return dict(s1=s1, ao=ao, k=0, bufs=[b for _, b in ao])

            def epi(cx, sgi, t0sg, nsg, c0, mi, mw, g0, n, pss, gidx):
                m = (c0 // 128) + mi
                if g0 == t0sg:
                    cx["k"] += 1
                ao, Bao = cx["ao"][cx["k"] % 2]
                s1, Bs1 = cx["s1"][gidx % 2]
                a0 = g0 - t0sg
                S.op("act", lambda e: e.activation(s1[:, 0:n], pss[0][0][:, 0:n], AF.Silu), r=[pss[0][1]], w=[Bs1])
                S.op("dve", lambda e: e.tensor_tensor(ao[:, a0:a0 + n], pss[1][0][:, 0:n], s1[:, 0:n], ALU.mult), r=[pss[1][1], Bs1], w=[Bao])
                if g0 + n == t0sg + nsg:
                    S.dma("sp", actT[m * 128:(m + 1) * 128, t0sg:t0sg + nsg], ao[:, 0:nsg], Bao, r=[Bao])

            gemm_fm("f1", [(xmT, KT)], [(0, W["w_ffn1"][l]), (0, W["w_ffn3"][l])], [(c, 512) for c in range(0, DFF, 512)], SG, epi, setup)
            setup2, epi2 = resid_epilogue(5)
            gemm_fm("f2", [(actT, 44)], [(0, W["w_ffn2"][l])], [(c, 256) for c in range(0, D, 256)], 1152, epi2, setup2, wwidth=256)

        def phase_out():
            with ExitStack() as st:
                hin = [sbt(st, "po_h%d" % i, [128, KT, 128]) for i in range(2)]
                ot = [sbt(st, "po_o%d" % i, [128, D]) for i in range(2)]
                for ti in range(NL // 128):
                    h, Bh = hin[ti % 2]
                    o, Bo = ot[ti % 2]
                    tk = NCTX + ti * 128
                    S.dma("sp", h[:], hT[:, tk:tk + 128].rearrange("(k p) t -> p k t", p=128), Bh, w=[Bh])
                    for q in range(4):
                        p, Bp = PS[(ti * 4 + q) % 4]
                        for j in range(4):
                            kt = q * 4 + j
                            S.op("pe", lambda e, p=p, j=j, kt=kt, h=h: e.transpose(p[:, j * 128:(j + 1) * 128], h[:, kt, :], ident[:]),
                                 r=[Bh, Bident], w=[Bp])
                        if q % 2 == 0:
                            S.op("dve", lambda e, p=p, q=q, o=o: e.tensor_copy(o[:, q * 512:(q + 1) * 512], p[:]), r=[Bp], w=[Bo])
                        else:
                            S.op("act", lambda e, p=p, q=q, o=o: e.copy(o[:, q * 512:(q + 1) * 512], p[:]), r=[Bp], w=[Bo])
                    S.dma("sp", out_d[ti * 128:(ti + 1) * 128, :], o[:], Bo, r=[Bo])
                S.barrier()
                S.emit()

        identb, Bidentb = sbt(gst, "identb", [128, 128], BF16)
        S.op("dve", lambda e: e.tensor_copy(identb[:], ident[:]), r=[Bident], w=[Bidentb])

        for l in range(DEPTH):
            phase_params(l)
            phase_norm(G1, BG1, 0)
            phase_inproj(l)
            phase_attn_prep()
            phase_attn()
            phase_rglru(l)
            phase_mlstm_prep()
            phase_mlstm(l)
            phase_merge(l)
            phase_norm(G2, BG2, 3)
            phase_ffn(l)
        phase_out()
        print("total ops recorded:", S.nops, "dma sems:", len(S.all_dsems))
    return nc


def host_tables(NL):
    ident = np.eye(128, dtype=np.float32)
    R = np.zeros((128, 128), np.float32)
    for j in range(32):
        R[j, j + 32] = -1.0
        R[j + 32, j] = 1.0
        R[j + 64, j + 96] = -1.0
        R[j + 96, j + 64] = 1.0
    rotT = np.ascontiguousarray(R.T)
    rows = NL // 64
    row = np.repeat(np.arange(rows, dtype=np.float32), 64)
    col = np.tile(np.arange(64, dtype=np.float32), rows)
    inv = (np.float32(10000.0) ** (-np.arange(32, dtype=np.float32) / np.float32(32))).astype(np.float32)
    ar = row[:, None] * inv
    ac = col[:, None] * inv
    ang = np.concatenate([ar, ar, ac, ac], axis=-1).astype(np.float32)
    cosT = np.ascontiguousarray(np.cos(ang).T.astype(np.float32))
    sinT = np.ascontiguousarray(np.sin(ang).T.astype(np.float32))
    s = np.arange(128)[:, None]
    t = np.arange(128)[None, :]
    mask = np.stack([np.where(s <= t, 0.0, NEG), np.where(s >= t, 0.0, NEG)]).astype(np.float32)
    sel = np.zeros((4, 512), np.float32)
    for j in range(4):
        sel[j, j * 128:(j + 1) * 128] = 1.0
    return dict(k_ident=ident, k_rot=rotT, k_cos=cosT, k_sin=sinT, k_mask=mask, k_sel=sel)


WNAMES = ["w_mod", "b_mod", "norm1_g", "norm2_g", "w_in", "attn_qnorm_g", "attn_knorm_g", "attn_lambda", "attn_subln_g",
          "rnn_conv_w", "rnn_conv_b", "rnn_wa", "rnn_ba", "rnn_wx", "rnn_bx", "rnn_lambda", "ml_conv_w", "ml_conv_b",
          "ml_gate_b", "ml_norm_g", "w_branch_attn", "w_branch_rnn", "w_branch_ml", "w_out", "w_ffn1", "w_ffn3", "w_ffn2"]


def run(inputs, dbg=(), n_cores=None):
    x = np.asarray(inputs["x"], np.float32)
    B, NL, _ = x.shape
    DEPTH = inputs["w_mod"].shape[0]
    nc = build(NL, DEPTH, dbg)
    tabs = host_tables(NL)
    wd = {k: np.ascontiguousarray(np.asarray(inputs[k], np.float32)) for k in WNAMES}
    in_maps = []
    ncores = B if n_cores is None else n_cores
    for b in range(ncores):
        m = dict(wd)
        m.update(tabs)
        m["x"] = np.ascontiguousarray(x[b])
        m["ctx"] = np.ascontiguousarray(np.asarray(inputs["ctx"], np.float32)[b])
        m["c2"] = np.ascontiguousarray(np.stack([np.asarray(inputs["c"], np.float32)[b], np.asarray(inputs["c_ctx"], np.float32)]))
        in_maps.append(m)
    res = run_bass_kernel_spmd(nc, in_maps, core_ids=list(range(ncores)))
    return res


def kernel(**inputs):
    res = run(inputs)
    return np.stack([r["out"] for r in res.results], axis=0).astype(np.float32)
```

```python
import math
from contextlib import ExitStack
import numpy as np
import concourse.bass as bass
import concourse.mybir as mybir
from concourse.bass_utils import run_bass_kernel_spmd

F32 = mybir.dt.float32
BF16 = mybir.dt.bfloat16
ALU = mybir.AluOpType
AF = mybir.ActivationFunctionType
AX = mybir.AxisListType
ENGS = ("pe", "act", "dve", "pool", "sp")

D = 2048
KT = 16
NCTX = 256
DFF = 5632
N_IN = 15376
OFF = dict(aq=0, ak=1024, av=2048, rx=3072, rg=4096, mq=5120, mk=6144, mv=7168, mo=8192, mg=9216, bg=9232)
EPS = 1e-6
NEG = -1.0e30


class Buf:
    __slots__ = ("name", "lastw", "readers", "dsem")

    def __init__(self, name):
        self.name = name
        self.lastw = None
        self.readers = {}
        self.dsem = None


class DmaSem:
    __slots__ = ("h", "count")

    def __init__(self, h):
        self.h = h
        self.count = 0


class Op:
    __slots__ = ("eng", "fn", "deps", "signal", "is_dma", "dsem", "dcount", "cnt")

    def __init__(self, eng, fn):
        self.eng = eng
        self.fn = fn
        self.deps = []
        self.signal = False
        self.is_dma = False
        self.dsem = None
        self.dcount = 0
        self.cnt = 0


class Sched:
    def __init__(self, nc, stack):
        self.nc = nc
        self.stack = stack
        self.ops = {e: [] for e in ENGS}
        self.esem = {e: stack.enter_context(nc.semaphore("es_" + e)) for e in ("pe", "act", "dve", "pool")}
        self.ecnt = {e: 0 for e in ("pe", "act", "dve", "pool")}
        self.free_dsems = []
        self.all_dsems = []
        self.last_sig = {}
        self.nops = 0

    def get_dsem(self):
        if self.free_dsems:
            return self.free_dsems.pop()
        ds = DmaSem(self.stack.enter_context(self.nc.semaphore("ds%d" % len(self.all_dsems))))
        self.all_dsems.append(ds)
        return ds

    def release(self, bufs):
        for b in bufs:
            if b.dsem is not None:
                self.free_dsems.append(b.dsem)
                b.dsem = None

    def _dep(self, op, prev):
        if prev is None or prev is op:
            return
        if prev.eng == "pe" and op.eng == "pe" and not prev.is_dma:
            return
        if not prev.is_dma:
            prev.signal = True
        op.deps.append(prev)

    def _track(self, o, r, w, rkey):
        for b in r:
            self._dep(o, b.lastw)
        for b in w:
            self._dep(o, b.lastw)
            for rd in b.readers.values():
                self._dep(o, rd)
        for b in r:
            b.readers[rkey] = o
        for b in w:
            b.lastw = o
            b.readers = {}

    def op(self, eng, fn, r=(), w=()):
        o = Op(eng, fn)
        self._track(o, r, w, eng)
        self.ops[eng].append(o)
        self.nops += 1
        return o

    def dma(self, eng, out_ap, in_ap, sb, r=(), w=()):
        if sb.dsem is None:
            sb.dsem = self.get_dsem()
        ds = sb.dsem
        o = Op(eng, lambda e: e.dma_start(out=out_ap, in_=in_ap))
        o.is_dma = True
        o.dsem = ds
        ds.count += 16
        o.dcount = ds.count
        self._track(o, r, w, ("dma", id(ds)))
        self.ops[eng].append(o)
        self.nops += 1
        return o

    def barrier(self):
        lasts = []
        for e in ("pe", "act", "dve", "pool"):
            found = None
            for o in reversed(self.ops[e]):
                if not o.is_dma and o.fn is not None:
                    found = o
                    break
            if found is not None:
                found.signal = True
                self.last_sig[e] = found
            if e in self.last_sig:
                lasts.append(self.last_sig[e])
        dl = []
        for ds in self.all_dsems:
            if ds.count > 0:
                d = Op("sp", None)
                d.is_dma = True
                d.dsem = ds
                d.dcount = ds.count
                dl.append(d)
        for e in ENGS:
            b = Op(e, None)
            b.deps = list(lasts) + dl
            self.ops[e].append(b)

    def emit(self):
        nc = self.nc
        for e in ("pe", "act", "dve", "pool"):
            for o in self.ops[e]:
                if o.signal and not o.is_dma and o.fn is not None:
                    self.ecnt[e] += 1
                    o.cnt = self.ecnt[e]
        esem = self.esem
        oplists = self.ops
        self.ops = {e: [] for e in ENGS}

        def run(e, engobj):
            seen = {}
            for o in oplists[e]:
                for d in o.deps:
                    if d.is_dma:
                        key = id(d.dsem)
                        if seen.get(key, 0) < d.dcount:
                            seen[key] = d.dcount
                            engobj.wait_ge(d.dsem.h, d.dcount)
                    else:
                        if d.eng == e and e == "pe":
                            continue
                        if seen.get(d.eng, 0) < d.cnt:
                            seen[d.eng] = d.cnt
                            engobj.wait_ge(esem[d.eng], d.cnt)
                if o.fn is None:
                    continue
                ins = o.fn(engobj)
                if o.is_dma:
                    ins.then_inc(o.dsem.h, 16)
                elif o.signal:
                    ins.then_inc(esem[e], 1)

        with nc.Block() as block:
            @block.tensor
            def _(eng):
                run("pe", eng)

            @block.scalar
            def _(eng):
                run("act", eng)

            @block.vector
            def _(eng):
                run("dve", eng)

            @block.gpsimd
            def _(eng):
                run("pool", eng)

            @block.sync
            def _(eng):
                run("sp", eng)


def split_sizes(total, maxsz, mult=128):
    n = -(-total // maxsz)
    base = -(-(total // mult) // n) * mult
    out = []
    t = 0
    while t < total:
        s = min(base, total - t)
        out.append((t, s))
        t += s
    return out


def groups512(t0, n):
    out = []
    t = 0
    while t < n:
        s = min(512, n - t)
        out.append((t0 + t, s))
        t += s
    return out


def segs(t0, n):
    out = []
    if t0 < NCTX:
        m = min(n, NCTX - t0)
        out.append((t0, m, 1))
        if n > m:
            out.append((t0 + m, n - m, 0))
    else:
        out.append((t0, n, 0))
    return out


def build(NL, DEPTH, dbg=()):
    NT = NCTX + NL
    TT = NT // 128
    nc = bass.Bass("TRN2", target_bir_lowering=False)

    def din(name, shape):
        return nc.dram_tensor(name, list(shape), F32, kind="ExternalInput").ap()

    x_in = din("x", [NL, D])
    ctx_in = din("ctx", [NCTX, D])
    c_in = din("c2", [2, D])
    W = {}
    for name, shape in [("w_mod", [DEPTH, D, 6 * D]), ("b_mod", [DEPTH, 6 * D]), ("norm1_g", [DEPTH, D]),
                        ("norm2_g", [DEPTH, D]), ("w_in", [DEPTH, D, N_IN]), ("attn_qnorm_g", [DEPTH, 128]),
                        ("attn_knorm_g", [DEPTH, 128]), ("attn_lambda", [DEPTH, 4, 128]),
                        ("attn_subln_g", [DEPTH, 256]), ("rnn_conv_w", [DEPTH, 4, 1024]),
                        ("rnn_conv_b", [DEPTH, 1024]), ("rnn_wa", [DEPTH, 2, 8, 128, 128]),
                        ("rnn_ba", [DEPTH, 2, 1024]), ("rnn_wx", [DEPTH, 2, 8, 128, 128]),
                        ("rnn_bx", [DEPTH, 2, 1024]), ("rnn_lambda", [DEPTH, 2, 1024]),
                        ("ml_conv_w", [DEPTH, 4, 2048]), ("ml_conv_b", [DEPTH, 2048]), ("ml_gate_b", [DEPTH, 16]),
                        ("ml_norm_g", [DEPTH, 1024]), ("w_branch_attn", [DEPTH, 1024, D]),
                        ("w_branch_rnn", [DEPTH, 1024, D]), ("w_branch_ml", [DEPTH, 1024, D]),
                        ("w_out", [DEPTH, D, D]), ("w_ffn1", [DEPTH, D, DFF]), ("w_ffn3", [DEPTH, D, DFF]),
                        ("w_ffn2", [DEPTH, DFF, D])]:
        W[name] = din(name, shape)
    ident_in = din("k_ident", [128, 128])
    rot_in = din("k_rot", [128, 128])
    cos_in = din("k_cos", [128, NL])
    sin_in = din("k_sin", [128, NL])
    mask_in = din("k_mask", [2, 128, 128])
    sel_in = din("k_sel", [4, 512])
    out_d = nc.dram_tensor("out", [NL, D], F32, kind="ExternalOutput").ap()

    def dscr(name, shape, dt):
        kind = "ExternalOutput" if name in dbg else "Internal"
        return nc.dram_tensor(name, list(shape), dt, kind=kind).ap()

    hT = dscr("hT", [D, NT], F32)
    xmT = dscr("xmT", [D, NT], BF16)
    uT = dscr("uT", [N_IN, NT], F32)
    vtok = dscr("vtok", [NT, 3072], F32)
    qkT = dscr("qkT", [2048, NT], BF16)
    qkm = dscr("qkm", [2048, NT], BF16)
    aT = dscr("aT", [1024, NT], BF16)
    rT = dscr("rT", [1024, NT], BF16)
    mT = dscr("mT", [1024, NT], BF16)
    zT = dscr("zT", [D, NT], BF16)
    actT = dscr("actT", [DFF, NT], BF16)

    with ExitStack() as gst:
        S = Sched(nc, gst)

        uid = [0]

        def sbt(st, name, shape, dt=F32):
            uid[0] += 1
            name = "%s_u%d" % (name, uid[0])
            t = st.enter_context(nc.sbuf_tensor(name, list(shape), dt))
            return t, Buf(name)

        PS = []
        for i in range(8):
            t = gst.enter_context(nc.psum_tensor("ps%d" % i, [128, 512], F32))
            PS.append((t, Buf("ps%d" % i)))

        ident, Bident = sbt(gst, "ident", [128, 128])
        rotb, Brotb = sbt(gst, "rotb", [128, 128], BF16)
        onesb, Bonesb = sbt(gst, "onesb", [128, 128], BF16)
        ones32, Bones32 = sbt(gst, "ones32", [128, 128])
        maskt, Bmask = sbt(gst, "maskt", [128, 2, 128])
        sel, Bsel = sbt(gst, "sel", [4, 512])
        sT, BsT = sbt(gst, "sT", [128, KT, 2])
        epsc, Bepsc = sbt(gst, "epsc", [128, 1])
        modT, BmodT = sbt(gst, "modT", [128, 96, 2])
        G1, BG1 = sbt(gst, "G1", [128, KT, 2])
        G2, BG2 = sbt(gst, "G2", [128, KT, 2])
        gqk, Bgqk = sbt(gst, "gqk", [128, 2])
        nlam, Bnlam = sbt(gst, "nlam", [128, 1])
        gsub, Bgsub = sbt(gst, "gsub", [128, 256])
        rcw, Brcw = sbt(gst, "rcw", [128, 32])
        rcb, Brcb = sbt(gst, "rcb", [128, 8])
        rba, Brba = sbt(gst, "rba", [128, 16])
        rbx, Brbx = sbt(gst, "rbx", [128, 16])
        rnsp, Brnsp = sbt(gst, "rnsp", [128, 16])
        mcw, Bmcw = sbt(gst, "mcw", [128, 64])
        mcb, Bmcb = sbt(gst, "mcb", [128, 16])
        mgb, Bmgb = sbt(gst, "mgb", [4, 4])
        mng, Bmng = sbt(gst, "mng", [128, 1024])
        Bscr = Buf("dram_misc")

        with ExitStack() as st:
            rot32, Brot32 = sbt(st, "rot32", [128, 128])
            c2, Bc2 = sbt(st, "c2sb", [2, D])
            S.dma("sp", ident[:], ident_in, Bident, w=[Bident])
            S.dma("sp", rot32[:], rot_in, Brot32, w=[Brot32])
            S.dma("sp", maskt[:], mask_in.rearrange("a p t -> p a t"), Bmask, w=[Bmask])
            S.dma("sp", sel[:], sel_in, Bsel, w=[Bsel])
            S.dma("sp", c2[:], c_in, Bc2, w=[Bc2])
            S.op("dve", lambda e: e.tensor_copy(rotb[:], rot32[:]), r=[Brot32], w=[Brotb])
            S.op("dve", lambda e: e.memset(onesb[:], 1.0), w=[Bonesb])
            S.op("dve", lambda e: e.memset(ones32[:], 1.0), w=[Bones32])
            S.op("dve", lambda e: e.memset(epsc[:], EPS), w=[Bepsc])
            S.op("act", lambda e: e.activation(c2[:], c2[:], AF.Silu), r=[Bc2], w=[Bc2])
            pt, Bpt = PS[0]
            for kt in range(KT):
                S.op("pe", lambda e, kt=kt: e.transpose(pt[:, 2 * kt:2 * kt + 2], c2[0:2, kt * 128:(kt + 1) * 128],
                                                        ident[0:2, 0:2]), r=[Bc2, Bident], w=[Bpt])
            S.op("dve", lambda e: e.tensor_copy(sT[:].rearrange("p k r -> p (k r)"), pt[:, 0:2 * KT]), r=[Bpt], w=[BsT])
            xin = [sbt(st, "xin%d" % i, [128, D]) for i in range(2)]
            hst = [sbt(st, "hst%d" % i, [128, KT, 128]) for i in range(2)]
            for tt in range(TT):
                xt, Bxt = xin[tt % 2]
                ht, Bht = hst[tt % 2]
                src = ctx_in[tt * 128:(tt + 1) * 128, :] if tt < 2 else x_in[(tt - 2) * 128:(tt - 1) * 128, :]
                S.dma("sp", xt[:], src, Bxt, w=[Bxt])
                for q in range(4):
                    p, Bp = PS[1 + (tt * 4 + q) % 4]
                    for j in range(4):
                        kt = q * 4 + j
                        S.op("pe", lambda e, p=p, j=j, kt=kt, xt=xt: e.transpose(
                            p[:, j * 128:(j + 1) * 128], xt[:, kt * 128:(kt + 1) * 128], ident[:]),
                            r=[Bxt, Bident], w=[Bp])
                    eng = "dve" if q % 2 == 0 else "act"
                    if eng == "dve":
                        S.op("dve", lambda e, p=p, q=q, ht=ht: e.tensor_copy(
                            ht[:, q * 4:(q + 1) * 4, :], p[:].rearrange("p (j t) -> p j t", j=4)), r=[Bp], w=[Bht])
                    else:
                        S.op("act", lambda e, p=p, q=q, ht=ht: e.copy(
                            ht[:, q * 4:(q + 1) * 4, :], p[:].rearrange("p (j t) -> p j t", j=4)), r=[Bp], w=[Bht])
                S.dma("sp", hT[:, tt * 128:(tt + 1) * 128].rearrange("(k p) t -> p k t", p=128), ht[:], Bht, r=[Bht])
            S.barrier()
            S.emit()
            S.release([Bident, Brot32, Bmask, Bsel, Bc2] + [b for _, b in xin] + [b for _, b in hst])

        def load_T(st, dst_ap, Bdst, src_ap, R, tmpname):
            t, Bt = sbt(st, tmpname, [R, 128])
            S.dma("sp", t[:], src_ap, Bt, w=[Bt])
            p, Bp = PS[0]
            S.op("pe", lambda e: e.transpose(p[:, 0:R], t[:], ident[0:R, 0:R]), r=[Bt, Bident], w=[Bp])
            S.op("dve", lambda e: e.tensor_copy(dst_ap, p[:, 0:R]), r=[Bp], w=[Bdst])
            return Bt

        def load_rowbc(st, dst_ap, Bdst, src_ap, Fdim, tmpname):
            t, Bt = sbt(st, tmpname, [1, Fdim])
            S.dma("sp", t[:], src_ap, Bt, w=[Bt])
            for f0 in range(0, Fdim, 512):
                fn = min(512, Fdim - f0)
                p, Bp = PS[0]
                S.op("pe", lambda e, f0=f0, fn=fn: e.matmul(p[:, 0:fn], ones32[0:1, :], t[0:1, f0:f0 + fn], start=True,
                                                           stop=True), r=[Bt, Bones32], w=[Bp])
                S.op("dve", lambda e, f0=f0, fn=fn: e.tensor_copy(dst_ap[:, f0:f0 + fn], p[:, 0:fn]), r=[Bp], w=[Bdst])
            return Bt

        def phase_params(l):
            lam_init = 0.8 - 0.6 * math.exp(-0.3 * l)
            with ExitStack() as st:
                rel = []
                bm, Bbm = sbt(st, "bm", [128, 96])
                g1, Bg1 = sbt(st, "g1", [128, KT])
                g2, Bg2 = sbt(st, "g2", [128, KT])
                rlam, Brlam = sbt(st, "rlam", [128, 16])
                rel.append(load_T(st, bm[:], Bbm, W["b_mod"][l].rearrange("(m p) -> m p", p=128), 96, "t_bm"))
                rel.append(load_T(st, g1[:], Bg1, W["norm1_g"][l].rearrange("(m p) -> m p", p=128), KT, "t_g1"))
                rel.append(load_T(st, g2[:], Bg2, W["norm2_g"][l].rearrange("(m p) -> m p", p=128), KT, "t_g2"))
                rel.append(load_T(st, gqk[:, 0:1], Bgqk, W["attn_qnorm_g"][l].rearrange("(m p) -> m p", p=128), 1, "t_gq"))
                rel.append(load_T(st, gqk[:, 1:2], Bgqk, W["attn_knorm_g"][l].rearrange("(m p) -> m p", p=128), 1, "t_gk"))
                rel.append(load_T(st, rcw[:], Brcw, W["rnn_conv_w"][l].rearrange("a (k p) -> (a k) p", p=128), 32, "t_rcw"))
                rel.append(load_T(st, rcb[:], Brcb, W["rnn_conv_b"][l].rearrange("(k p) -> k p", p=128), 8, "t_rcb"))
                rel.append(load_T(st, rba[:], Brba, W["rnn_ba"][l].rearrange("a (k p) -> (a k) p", p=128), 16, "t_rba"))
                rel.append(load_T(st, rbx[:], Brbx, W["rnn_bx"][l].rearrange("a (k p) -> (a k) p", p=128), 16, "t_rbx"))
                rel.append(load_T(st, rlam[:], Brlam, W["rnn_lambda"][l].rearrange("a (k p) -> (a k) p", p=128), 16, "t_rl"))
                rel.append(load_T(st, mcw[:], Bmcw, W["ml_conv_w"][l].rearrange("a (k p) -> (a k) p", p=128), 64, "t_mcw"))
                rel.append(load_T(st, mcb[:], Bmcb, W["ml_conv_b"][l].rearrange("(k p) -> k p", p=128), 16, "t_mcb"))
                rel.append(load_rowbc(st, gsub[:], Bgsub, W["attn_subln_g"][l].rearrange("(o f) -> o f", o=1), 256, "t_gs"))
                rel.append(load_rowbc(st, mng[:], Bmng, W["ml_norm_g"][l].rearrange("(o f) -> o f", o=1), 1024, "t_mn"))
                S.op("dve", lambda e: e.tensor_scalar_mul(gsub[:], gsub[:], 1.0 - lam_init), r=[Bgsub], w=[Bgsub])
                for j in range(4):
                    S.dma("sp", mgb[:, j:j + 1], W["ml_gate_b"][l, j * 4:(j + 1) * 4].rearrange("(h o) -> h o", o=1),
                          Bmgb, w=[Bmgb])
                S.op("act", lambda e: e.activation(rlam[:], rlam[:], AF.Exp, scale=-1.0), r=[Brlam], w=[Brlam])
                S.op("act", lambda e: e.activation(rlam[:], rlam[:], AF.Ln, bias=1.0), r=[Brlam], w=[Brlam])
                S.op("dve", lambda e: e.tensor_scalar_mul(rnsp[:], rlam[:], -8.0), r=[Brlam], w=[Brnsp])
                lv, Blv = sbt(st, "lv", [1, 512])
                lw, Blw = sbt(st, "lw", [1, 8])
                S.dma("sp", lv[:], W["attn_lambda"][l].rearrange("(o a) f -> o (a f)", o=1), Blv, w=[Blv])
                S.op("dve", lambda e: e.tensor_tensor(lv[0:1, 0:128], lv[0:1, 0:128], lv[0:1, 128:256], ALU.mult), r=[Blv], w=[Blv])
                S.op("dve", lambda e: e.tensor_tensor(lv[0:1, 256:384], lv[0:1, 256:384], lv[0:1, 384:512], ALU.mult), r=[Blv], w=[Blv])
                S.op("dve", lambda e: e.tensor_reduce(lw[0:1, 0:1], lv[0:1, 0:128], AX.X, ALU.add), r=[Blv], w=[Blw])
                S.op("dve", lambda e: e.tensor_reduce(lw[0:1, 1:2], lv[0:1, 256:384], AX.X, ALU.add), r=[Blv], w=[Blw])
                S.op("act", lambda e: e.activation(lw[0:1, 2:4], lw[0:1, 0:2], AF.Exp), r=[Blw], w=[Blw])
                S.op("dve", lambda e: e.tensor_tensor(lw[0:1, 4:5], lw[0:1, 3:4], lw[0:1, 2:3], ALU.subtract), r=[Blw], w=[Blw])
                S.op("dve", lambda e: e.tensor_scalar_add(lw[0:1, 5:6], lw[0:1, 4:5], -lam_init), r=[Blw], w=[Blw])
                p, Bp = PS[0]
                S.op("pe", lambda e: e.matmul(p[:, 0:1], ones32[0:1, :], lw[0:1, 5:6], start=True, stop=True),
                     r=[Blw, Bones32], w=[Bp])
                S.op("dve", lambda e: e.tensor_copy(nlam[:], p[:, 0:1]), r=[Bp], w=[Bnlam])
                wm = [[sbt(st, "wm%d_%d" % (i, j), [128, 4, 512]) for j in range(4)] for i in range(2)]
                pm, Bpm = PS[1]
                wmod = W["w_mod"][l].rearrange("(k p) c -> p k c", p=128)
                for cb in range(24):
                    bufs = wm[cb % 2]
                    for j in range(4):
                        t, Bt = bufs[j]
                        S.dma("sp", t[:], wmod[:, j * 4:(j + 1) * 4, cb * 512:(cb + 1) * 512], Bt, w=[Bt])
                    for mi in range(4):
                        m = cb * 4 + mi
                        for kt in range(KT):
                            t, Bt = bufs[kt // 4]
                            S.op("pe", lambda e, t=t, kt=kt, mi=mi, m=m: e.matmul(
                                pm[:, 2 * m:2 * m + 2], t[:, kt % 4, mi * 128:(mi + 1) * 128], sT[:, kt, :],
                                start=(kt == 0), stop=(kt == KT - 1)), r=[Bt, BsT], w=[Bpm])
                for r_ in range(2):
                    S.op("dve", lambda e, r_=r_: e.tensor_tensor(
                        modT[:, :, r_], pm[:, 0:192].rearrange("p (m r) -> p m r", r=2)[:, :, r_], bm[:], ALU.add),
                        r=[Bpm, Bbm], w=[BmodT])
                    for (Gt, BG, gt_, Bg_, j) in ((G1, BG1, g1, Bg1, 1), (G2, BG2, g2, Bg2, 4)):
                        S.op("dve", lambda e, Gt=Gt, gt_=gt_, j=j, r_=r_: e.scalar_tensor_tensor(
                            Gt[:, :, r_], modT[:, j * 16:(j + 1) * 16, r_], 1.0, gt_[:], ALU.add, ALU.mult),
                            r=[BmodT, Bg_], w=[BG])
                S.barrier()
                S.emit()
                S.release(rel + [Blv, Bmgb] + [b for row in wm for _, b in row])

        def phase_norm(Gt, BG, shj):
            with ExitStack() as st:
                hin = [sbt(st, "n_h%d" % i, [128, KT, 512]) for i in range(2)]
                sq, Bsq = sbt(st, "n_sq", [128, KT, 512], BF16)
                xo = [sbt(st, "n_xo%d" % i, [128, KT, 512], BF16) for i in range(2)]
                rstd, Brstd = sbt(st, "n_rstd", [128, 512])
                tmp = [sbt(st, "n_tmp%d" % i, [128, 512]) for i in range(4)]
                for gi, (t0, n) in enumerate(groups512(0, NT)):
                    h, Bh = hin[gi % 2]
                    xo_, Bxo = xo[gi % 2]
                    S.dma("sp", h[:, :, 0:n], hT[:, t0:t0 + n].rearrange("(k p) t -> p k t", p=128), Bh, w=[Bh])
                    S.op("act", lambda e, h=h, n=n: e.activation(sq[:, :, 0:n], h[:, :, 0:n], AF.Square), r=[Bh], w=[Bsq])
                    p, Bp = PS[gi % 2]
                    for kt in range(KT):
                        S.op("pe", lambda e, p=p, kt=kt, n=n: e.matmul(p[:, 0:n], onesb[:], sq[:, kt, 0:n], start=(kt == 0),
                                                                         stop=(kt == KT - 1)), r=[Bsq, Bonesb], w=[Bp])
                    S.op("act", lambda e, p=p, n=n: e.activation(rstd[:, 0:n], p[:, 0:n], AF.Ln, bias=epsc[:], scale=1.0 / D),
                         r=[Bp, Bepsc], w=[Brstd])
                    S.op("act", lambda e, n=n: e.activation(rstd[:, 0:n], rstd[:, 0:n], AF.Exp, scale=-0.5), r=[Brstd], w=[Brstd])
                    for kt in range(KT):
                        tm, Btm = tmp[kt % 4]
                        for (s0, sn, r_) in segs(t0, n):
                            a0 = s0 - t0
                            S.op("dve", lambda e, h=h, kt=kt, a0=a0, sn=sn, r_=r_, tm=tm: e.scalar_tensor_tensor(
                                tm[:, a0:a0 + sn], h[:, kt, a0:a0 + sn], Gt[:, kt, r_:r_ + 1], rstd[:, a0:a0 + sn],
                                ALU.mult, ALU.mult), r=[Bh, BG, Brstd], w=[Btm])
                            eng = "act" if kt % 4 != 3 else "dve"
                            if eng == "act":
                                S.op("act", lambda e, kt=kt, a0=a0, sn=sn, r_=r_, tm=tm, xo_=xo_: e.activation(
                                    xo_[:, kt, a0:a0 + sn], tm[:, a0:a0 + sn], AF.Identity,
                                    bias=modT[:, shj * 16 + kt, r_:r_ + 1], scale=1.0), r=[Btm, BmodT], w=[Bxo])
                            else:
                                S.op("dve", lambda e, kt=kt, a0=a0, sn=sn, r_=r_, tm=tm, xo_=xo_: e.tensor_scalar(
                                    xo_[:, kt, a0:a0 + sn], tm[:, a0:a0 + sn], modT[:, shj * 16 + kt, r_:r_ + 1], None,
                                    ALU.add), r=[Btm, BmodT], w=[Bxo])
                    S.dma("sp", xmT[:, t0:t0 + n].rearrange("(k p) t -> p k t", p=128), xo_[:, :, 0:n], Bxo, r=[Bxo])
                S.barrier()
                S.emit()
                S.release([b for _, b in hin] + [b for _, b in xo])

        def gemm_fm(tag, inputs, streams, colblocks, sg_max, epilogue, extra_setup=None, wwidth=512):
            with ExitStack() as st:
                ns = len(streams)
                X = []
                for i, (scr, kti) in enumerate(inputs):
                    X.append(sbt(st, "%s_x%d" % (tag, i), [128, kti, sg_max], BF16))
                Wb = []
                for s_, (ii, wap) in enumerate(streams):
                    kti = inputs[ii][1]
                    nch = -(-kti // 8)
                    Wb.append([[sbt(st, "%s_w%d_%d_%d" % (tag, s_, par, ch), [128, min(8, kti - ch * 8), wwidth], BF16)
                                for ch in range(nch)] for par in range(2)])
                ctxobj = extra_setup(st) if extra_setup else None
                allb = [b for _, b in X] + [b for s_ in Wb for par in s_ for _, b in par]
                gidx = 0
                cbi = 0
                for sgi, (t0sg, nsg) in enumerate(split_sizes(NT, sg_max)):
                    for i, (scr, kti) in enumerate(inputs):
                        xt, Bx = X[i]
                        S.dma("sp", xt[:, :, 0:nsg], scr[0:kti * 128, t0sg:t0sg + nsg].rearrange("(k p) t -> p k t", p=128),
                              Bx, w=[Bx])
                    for (c0, cw) in colblocks:
                        par = cbi % 2
                        cbi += 1
                        for s_, (ii, wap) in enumerate(streams):
                            kti = inputs[ii][1]
                            wv = wap.rearrange("(k p) c -> p k c", p=128)
                            for ch, (wt, Bw) in enumerate(Wb[s_][par]):
                                k0 = ch * 8
                                k1 = min(kti, k0 + 8)
                                S.dma("pool", wt[:, 0:k1 - k0, 0:cw], wv[:, k0:k1, c0:c0 + cw], Bw, w=[Bw])
                        for mi in range(-(-cw // 128)):
                            mw = min(128, cw - mi * 128)
                            for (g0, n) in groups512(t0sg, nsg):
                                pss = []
                                for s_, (ii, wap) in enumerate(streams):
                                    kti = inputs[ii][1]
                                    xt, Bx = X[ii]
                                    p, Bp = PS[(gidx % 2) * ns + s_] if 2 * ns <= 6 else PS[s_]
                                    for kt in range(kti):
                                        wt, Bw = Wb[s_][par][kt // 8]
                                        S.op("pe", lambda e, p=p, wt=wt, kt=kt, mi=mi, mw=mw, xt=xt, x0=g0 - t0sg, n=n, kti=kti: e.matmul(
                                            p[0:mw, 0:n], wt[:, kt % 8, mi * 128:mi * 128 + mw],
                                            xt[:, kt, x0:x0 + n], start=(kt == 0), stop=(kt == kti - 1)),
                                            r=[Bw, Bx], w=[Bp])
                                    pss.append((p, Bp))
                                epilogue(ctxobj, sgi, t0sg, nsg, c0, mi, mw, g0, n, pss, gidx)
                                gidx += 1
                S.barrier()
                S.emit()
                S.release(allb + (ctxobj["bufs"] if ctxobj and "bufs" in ctxobj else []))

        def phase_inproj(l):
            win = W["w_in"][l]
            SG = 2176
            fm_ranges = [(OFF["aq"], 2048), (OFF["rx"], 4096), (OFF["bg"], 6144)]
            cbs = []
            for (c0, w_) in fm_ranges:
                for c in range(c0, c0 + w_, 512):
                    cbs.append((c, 512))

            def setup(st):
                stg = [sbt(st, "ip_stg%d" % i, [128, SG]) for i in range(3)]
                return dict(stg=stg, bufs=[b for _, b in stg], k=0)

            def epi(cx, sgi, t0sg, nsg, c0, mi, mw, g0, n, pss, gidx):
                p, Bp = pss[0]
                first = (g0 == t0sg)
                if first:
                    cx["k"] += 1
                stg, Bstg = cx["stg"][cx["k"] % 3]
                a0 = g0 - t0sg
                if gidx % 2 == 0:
                    S.op("act", lambda e: e.copy(stg[0:mw, a0:a0 + n], p[0:mw, 0:n]), r=[Bp], w=[Bstg])
                else:
                    S.op("dve", lambda e: e.tensor_copy(stg[0:mw, a0:a0 + n], p[0:mw, 0:n]), r=[Bp], w=[Bstg])
                if g0 + n == t0sg + nsg:
                    r0 = c0 + mi * 128
                    S.dma("sp", uT[r0:r0 + mw, t0sg:t0sg + nsg], stg[0:mw, 0:nsg], Bstg, r=[Bstg])

            gemm_fm("ip", [(xmT, KT)], [(0, win)], cbs, SG, epi, setup)

            def epi_g(cx, sgi, t0sg, nsg, c0, mi, mw, g0, n, pss, gidx):
                epi(cx, sgi, t0sg, nsg, c0, mi, mw, g0, n, pss, gidx)

            gemm_fm("ig", [(xmT, KT)], [(0, win)], [(OFF["mg"] + 4 * j, 4) for j in range(4)], SG, epi_g, setup)

            with ExitStack() as st:
                X, BX = sbt(st, "it_x", [128, KT, SG], BF16)
                Wb = [[sbt(st, "it_w%d_%d" % (par, ch), [128, 8, 512], BF16) for ch in range(2)] for par in range(2)]
                stg = [sbt(st, "it_s%d" % i, [128, 512]) for i in range(3)]
                cols = []
                for ci, name in enumerate(("av", "mv", "mo")):
                    for c in range(0, 1024, 512):
                        cols.append((OFF[name] + c, ci * 1024 + c))
                wv = win.rearrange("(k p) c -> p k c", p=128)
                k = 0
                cbi = 0
                for (t0sg, nsg) in split_sizes(NT, SG):
                    S.dma("sp", X[:, :, 0:nsg], xmT[:, t0sg:t0sg + nsg].rearrange("(k p) t -> p k t", p=128), BX, w=[BX])
                    for (c0, oc0) in cols:
                        par = cbi % 2
                        cbi += 1
                        for ch in range(2):
                            wt, Bw = Wb[par][ch]
                            S.dma("pool", wt[:], wv[:, ch * 8:(ch + 1) * 8, c0:c0 + 512], Bw, w=[Bw])
                        for ti in range(nsg // 128):
                            p, Bp = PS[k % 4]
                            for kt in range(KT):
                                wt, Bw = Wb[par][kt // 8]
                                S.op("pe", lambda e, p=p, kt=kt, ti=ti, wt=wt: e.matmul(
                                    p[:, :], X[:, kt, ti * 128:(ti + 1) * 128], wt[:, kt % 8, :], start=(kt == 0),
                                    stop=(kt == KT - 1)), r=[BX, Bw], w=[Bp])
                            sg_, Bsg = stg[k % 3]
                            if k % 2 == 0:
                                S.op("act", lambda e, sg_=sg_, p=p: e.copy(sg_[:], p[:]), r=[Bp], w=[Bsg])
                            else:
                                S.op("dve", lambda e, sg_=sg_, p=p: e.tensor_copy(sg_[:], p[:]), r=[Bp], w=[Bsg])
                            tk = t0sg + ti * 128
                            S.dma("sp", vtok[tk:tk + 128, oc0:oc0 + 512], sg_[:], Bsg, r=[Bsg])
                            k += 1
                S.barrier()
                S.emit()
                S.release([BX] + [b for par in Wb for _, b in par] + [b for _, b in stg])

        def phase_attn_prep():
            with ExitStack() as st:
                cosT, Bcos = sbt(st, "ap_cos", [128, NL])
                sinT, Bsin = sbt(st, "ap_sin", [128, NL])
                S.dma("sp", cosT[:], cos_in, Bcos, w=[Bcos])
                S.dma("sp", sinT[:], sin_in, Bsin, w=[Bsin])
                qin = [sbt(st, "ap_q%d" % i, [128, NT]) for i in range(2)]
                qo = [sbt(st, "ap_o%d" % i, [128, NT], BF16) for i in range(2)]
                tmps = [dict(sq=sbt(st, "ap_sq%d" % i, [128, 512], BF16), rstd=sbt(st, "ap_rstd%d" % i, [128, 512]),
                             qn=sbt(st, "ap_qn%d" % i, [128, 512]), qnb=sbt(st, "ap_qnb%d" % i, [128, 512], BF16),
                             t1=sbt(st, "ap_t1%d" % i, [128, 512]), t2=sbt(st, "ap_t2%d" % i, [128, 512])) for i in range(3)]
                gcount = [0]
                pend = []

                def first_half(idx, gi, t0, n, q, Bq, o, Bo, which):
                    tb = tmps[gcount[0] % 3]
                    gcount[0] += 1
                    (sq, Bsq), (rstd, Brstd), (qn, Bqn), (qnb, Bqnb), (t1, Bt1), (t2, Bt2) = tb["sq"], tb["rstd"], tb["qn"], tb["qnb"], tb["t1"], tb["t2"]
                    S.op("act", lambda e: e.activation(sq[:, 0:n], q[:, t0:t0 + n], AF.Square), r=[Bq], w=[Bsq])
                    p, Bp = PS[gi % 2]
                    S.op("pe", lambda e: e.matmul(p[:, 0:n], onesb[:], sq[:, 0:n], start=True, stop=True), r=[Bsq, Bonesb], w=[Bp])
                    S.op("act", lambda e: e.activation(rstd[:, 0:n], p[:, 0:n], AF.Ln, bias=epsc[:], scale=1.0 / 128), r=[Bp, Bepsc], w=[Brstd])
                    S.op("act", lambda e: e.activation(rstd[:, 0:n], rstd[:, 0:n], AF.Exp, scale=-0.5), r=[Brstd], w=[Brstd])
                    S.op("dve", lambda e: e.scalar_tensor_tensor(
                        qn[:, 0:n], q[:, t0:t0 + n], gqk[:, which:which + 1], rstd[:, 0:n], ALU.mult, ALU.mult),
                        r=[Bq, Bgqk, Brstd], w=[Bqn])
                    later = []
                    for (s0_, sn, r_) in segs(t0, n):
                        a0 = s0_ - t0
                        if r_ == 1:
                            S.op("pool", lambda e, s0_=s0_, sn=sn, a0=a0: e.tensor_copy(o[:, s0_:s0_ + sn], qn[:, a0:a0 + sn]), r=[Bqn], w=[Bo])
                        else:
                            l0 = s0_ - NCTX
                            S.op("pool", lambda e, a0=a0, sn=sn: e.tensor_copy(qnb[:, a0:a0 + sn], qn[:, a0:a0 + sn]), r=[Bqn], w=[Bqnb])
                            pr, Bpr = PS[2 + gcount[0] % 3]
                            S.op("pool", lambda e, a0=a0, sn=sn, l0=l0: e.tensor_tensor(t1[:, 0:sn], qn[:, a0:a0 + sn], cosT[:, l0:l0 + sn], ALU.mult),
                                 r=[Bqn, Bcos], w=[Bt1])
                            later.append((pr, Bpr, sn, l0, s0_, t1, Bt1, t2, Bt2, o, Bo, qnb, Bqnb, a0))
                    return later

                def second_half(later):
                    for (pr, Bpr, sn, l0, s0_, t1, Bt1, t2, Bt2, o, Bo, qnb, Bqnb, a0) in later:
                        S.op("pe", lambda e, pr=pr, a0=a0, sn=sn, qnb=qnb: e.matmul(pr[:, 0:sn], rotb[:], qnb[:, a0:a0 + sn], start=True, stop=True),
                             r=[Brotb, Bqnb], w=[Bpr])
                        S.op("dve", lambda e, pr=pr, sn=sn, l0=l0, t2=t2: e.tensor_tensor(t2[:, 0:sn], pr[:, 0:sn], sinT[:, l0:l0 + sn], ALU.mult),
                             r=[Bpr, Bsin], w=[Bt2])
                        S.op("dve", lambda e, o=o, s0_=s0_, sn=sn, t1=t1, t2=t2: e.tensor_tensor(o[:, s0_:s0_ + sn], t1[:, 0:sn], t2[:, 0:sn], ALU.add),
                             r=[Bt1, Bt2], w=[Bo])

                for idx in range(16):
                    which = idx // 8
                    q, Bq = qin[idx % 2]
                    o, Bo = qo[idx % 2]
                    row0 = (OFF["aq"] if which == 0 else OFF["ak"]) + (idx % 8) * 128
                    S.dma("sp", q[:], uT[row0:row0 + 128, :], Bq, w=[Bq])
                    prev = None
                    for gi, (t0, n) in enumerate(groups512(0, NT)):
                        cur = first_half(idx, gi, t0, n, q, Bq, o, Bo, which)
                        if prev is not None:
                            second_half(prev)
                        prev = cur
                    second_half(prev)
                    S.dma("sp", qkT[idx * 128:(idx + 1) * 128, :], o[:], Bo, r=[Bo])
                S.barrier()
                S.emit()
                S.release([Bcos, Bsin] + [b for _, b in qin] + [b for _, b in qo])

        def phase_attn():
            scale = 128 ** -0.5
            with ExitStack() as st:
                qTt, BqT = sbt(st, "at_q", [128, 2, NT], BF16)
                kTt, BkT = sbt(st, "at_k", [128, 2, NT], BF16)
                V, BV = sbt(st, "at_v", [128, TT, 257], BF16)
                ao, Bao = sbt(st, "at_ao", [128, 2, NT], BF16)
                A2, BA2 = sbt(st, "at_A2", [128, TT, 256])
                SSQ, BSSQ = sbt(st, "at_SSQ", [128, TT])
                E = [sbt(st, "at_e%d" % i, [128, 512], BF16) for i in range(4)]
                r01s = [sbt(st, "at_r%d" % i, [128, 2]) for i in range(2)]
                a1s = [sbt(st, "at_a1_%d" % i, [128, 256]) for i in range(2)]
                junk, Bjunk = sbt(st, "at_junk", [128, 256])
                STB = [PS[0], PS[1], PS[6]]
                LA = 2
                def load_head(h):
                    for sub in range(2):
                        S.dma("sp", qTt[:, sub, :], qkT[(h * 2 + sub) * 128:(h * 2 + sub + 1) * 128, :], BqT, w=[BqT])
                        S.dma("sp", kTt[:, sub, :], qkT[1024 + (h * 2 + sub) * 128:1024 + (h * 2 + sub + 1) * 128, :], BkT, w=[BkT])
                    S.dma("pool", V[:, :, 0:256], vtok[:, h * 256:(h + 1) * 256].rearrange("(t p) c -> p t c", p=128), BV, w=[BV])
                    S.op("dve", lambda e: e.memset(V[:, :, 256:257], 1.0), w=[BV])

                load_head(0)
                for h in range(4):
                    iters = []
                    for qg in range(NT // 256):
                        ktiles = list(range(2)) if qg == 0 else list(range(TT))
                        for ki, kt in enumerate(ktiles):
                            iters.append((qg, ki, kt, len(ktiles)))
                    nit = len(iters)

                    def emit_st(i):
                        qg, ki, kt, nk = iters[i]
                        q0 = qg * 256
                        pst, Bpst = STB[i % 3]
                        et, Bet = E[i % 4]
                        for sub in range(2):
                            S.op("pe", lambda e, sub=sub: e.matmul(
                                pst[:, sub * 256:(sub + 1) * 256], kTt[:, sub, kt * 128:(kt + 1) * 128],
                                qTt[:, sub, q0:q0 + 256], start=True, stop=True), r=[BkT, BqT], w=[Bpst])
                        S.op("act", lambda e: e.activation(et[:], pst[:], AF.Exp, scale=scale), r=[Bpst], w=[Bet])

                    def emit_pv(i):
                        qg, ki, kt, nk = iters[i]
                        q0 = qg * 256
                        et, Bet = E[i % 4]
                        for sub in range(2):
                            for qs in range(2):
                                pa, Bpa = PS[2 + sub * 2 + qs]
                                S.op("pe", lambda e, pa=pa, sub=sub, qs=qs: e.matmul(
                                    pa[:, 0:257], et[:, sub * 256 + qs * 128:sub * 256 + (qs + 1) * 128], V[:, kt, :],
                                    start=(ki == 0), stop=(ki == nk - 1)), r=[Bet, BV], w=[Bpa])
                        if ki != nk - 1:
                            return
                        for qs in range(2):
                            p0, Bp0 = PS[2 + qs]
                            p1, Bp1 = PS[4 + qs]
                            tt = qg * 2 + qs
                            r01, Br01 = r01s[qs]
                            a1, Ba1 = a1s[qs]
                            S.op("dve", lambda e, p0=p0, r01=r01: e.reciprocal(r01[:, 0:1], p0[:, 256:257]), r=[Bp0], w=[Br01])
                            S.op("dve", lambda e, p1=p1, r01=r01: e.reciprocal(r01[:, 1:2], p1[:, 256:257]), r=[Bp1], w=[Br01])
                            S.op("dve", lambda e, r01=r01: e.tensor_tensor(r01[:, 1:2], r01[:, 1:2], nlam[:], ALU.mult), r=[Br01, Bnlam], w=[Br01])
                            S.op("dve", lambda e, p0=p0, r01=r01, a1=a1: e.tensor_scalar(a1[:], p0[:, 0:256], r01[:, 0:1], None, ALU.mult),
                                 r=[Bp0, Br01], w=[Ba1])
                            S.op("dve", lambda e, p1=p1, r01=r01, a1=a1, tt=tt: e.scalar_tensor_tensor(
                                A2[:, tt, :], p1[:, 0:256], r01[:, 1:2], a1[:], ALU.mult, ALU.add), r=[Bp1, Br01, Ba1], w=[BA2])
                            S.op("dve", lambda e, tt=tt: e.scalar_tensor_tensor(
                                junk[:], A2[:, tt, :], 1.0, A2[:, tt, :], ALU.mult, ALU.mult, accum_out=SSQ[:, tt:tt + 1]),
                                r=[BA2], w=[Bjunk, BSSQ])

                    for i in range(min(LA, nit)):
                        emit_st(i)
                    for i in range(nit):
                        if i + LA < nit:
                            emit_st(i + LA)
                        emit_pv(i)
                    if h + 1 < 4:
                        load_head(h + 1)
                    S.op("act", lambda e: e.activation(SSQ[:], SSQ[:], AF.Sqrt, bias=epsc[:], scale=1.0 / 256), r=[BSSQ, Bepsc], w=[BSSQ])
                    S.op("dve", lambda e: e.reciprocal(SSQ[:], SSQ[:]), r=[BSSQ], w=[BSSQ])
                    for tt in range(TT):
                        a1, Ba1 = a1s[tt % 2]
                        S.op("dve", lambda e, tt=tt, a1=a1: e.scalar_tensor_tensor(a1[:], A2[:, tt, :], SSQ[:, tt:tt + 1], gsub[:], ALU.mult, ALU.mult),
                             r=[BA2, BSSQ, Bgsub], w=[Ba1])
                        ptr, Bptr = PS[7] if tt % 2 == 0 else PS[6]
                        for vc in range(2):
                            S.op("pe", lambda e, ptr=ptr, vc=vc, a1=a1: e.transpose(ptr[:, vc * 128:(vc + 1) * 128], a1[:, vc * 128:(vc + 1) * 128], ident[:]),
                                 r=[Ba1, Bident], w=[Bptr])
                        S.op("act", lambda e, ptr=ptr, tt=tt: e.copy(ao[:, :, tt * 128:(tt + 1) * 128], ptr[:, 0:256].rearrange("p (v t) -> p v t", v=2)),
                             r=[Bptr], w=[Bao])
                    for vc in range(2):
                        S.dma("sp", aT[h * 256 + vc * 128:h * 256 + (vc + 1) * 128, :], ao[:, vc, :], Bao, r=[Bao])
                S.barrier()
                S.emit()
                S.release([BqT, BkT, BV, Bao])

        def conv_ops(xin, Bx, y, By, wt, Bw, wcol, bt, Bb, bcol):
            for (s0, sn) in ((0, NCTX), (NCTX, NL)):
                S.op("dve", lambda e, s0=s0, sn=sn: e.tensor_scalar(y[:, s0:s0 + sn], xin[:, s0:s0 + sn], wcol(1), bcol, ALU.mult, ALU.add),
                     r=[Bx, Bw, Bb], w=[By])
                S.op("dve", lambda e, s0=s0, sn=sn: e.scalar_tensor_tensor(y[:, s0 + 1:s0 + sn], xin[:, s0:s0 + sn - 1], wcol(0), y[:, s0 + 1:s0 + sn],
                                                                          ALU.mult, ALU.add), r=[Bx, Bw, By], w=[By])
                S.op("dve", lambda e, s0=s0, sn=sn: e.scalar_tensor_tensor(y[:, s0:s0 + sn - 1], xin[:, s0 + 1:s0 + sn], wcol(2), y[:, s0:s0 + sn - 1],
                                                                          ALU.mult, ALU.add), r=[Bx, Bw, By], w=[By])
                S.op("dve", lambda e, s0=s0, sn=sn: e.scalar_tensor_tensor(y[:, s0:s0 + sn - 2], xin[:, s0 + 2:s0 + sn], wcol(3), y[:, s0:s0 + sn - 2],
                                                                          ALU.mult, ALU.add), r=[Bx, Bw, By], w=[By])

        def rev_ap(t, p0, p1, rowlen, c0, n):
            return bass.AP(t, p0 * rowlen + c0 + n - 1, [[rowlen, p1 - p0], [-1, n]])

        def phase_rglru(l):
            with ExitStack() as st:
                xs = [sbt(st, "rg_x%d" % i, [128, NT]) for i in range(2)]
                gs = [sbt(st, "rg_g%d" % i, [128, NT]) for i in range(2)]
                xc, Bxc = sbt(st, "rg_xc", [128, NT])
                xcb, Bxcb = sbt(st, "rg_xcb", [128, NT], BF16)
                Rt, BR = sbt(st, "rg_R", [128, NT])
                It, BI = sbt(st, "rg_I", [128, NT])
                hh = [sbt(st, "rg_h%d" % d, [128, NT]) for d in range(2)]
                yo, Byo = sbt(st, "rg_yo", [128, NT], BF16)
                wg = [[[sbt(st, "rg_w%d_%d_%d" % (par, d, j), [128, 128], BF16) for j in range(2)] for d in range(2)] for par in range(2)]
                for k in range(8):
                    x, Bx = xs[k % 2]
                    g, Bg = gs[k % 2]
                    S.dma("sp", x[:], uT[OFF["rx"] + k * 128:OFF["rx"] + (k + 1) * 128, :], Bx, w=[Bx])
                    S.dma("sp", g[:], uT[OFF["rg"] + k * 128:OFF["rg"] + (k + 1) * 128, :], Bg, w=[Bg])
                    wk = wg[k % 2]
                    for d in range(2):
                        S.dma("pool", wk[d][0][0][:], W["rnn_wa"][l, d, k], wk[d][0][1], w=[wk[d][0][1]])
                        S.dma("pool", wk[d][1][0][:], W["rnn_wx"][l, d, k], wk[d][1][1], w=[wk[d][1][1]])
                    conv_ops(x, Bx, xc, Bxc, rcw, Brcw, lambda tap, k=k: rcw[:, tap * 8 + k:tap * 8 + k + 1], rcb, Brcb, rcb[:, k:k + 1])
                    S.op("act", lambda e: e.copy(xcb[:], xc[:]), r=[Bxc], w=[Bxcb])
                    S.op("act", lambda e, g=g: e.activation(g[:], g[:], AF.Gelu), r=[Bg], w=[Bg])
                    for d in range(2):
                        col = d * 8 + k
                        h, Bh = hh[d]
                        for gi, (t0, n) in enumerate(groups512(0, NT)):
                            pr, Bpr = PS[(gi % 2) * 2]
                            pi, Bpi = PS[(gi % 2) * 2 + 1]
                            S.op("pe", lambda e, pr=pr, d=d, t0=t0, n=n, wk=wk: e.matmul(pr[:, 0:n], wk[d][0][0][:], xcb[:, t0:t0 + n], start=True, stop=True),
                                 r=[wk[d][0][1], Bxcb], w=[Bpr])
                            S.op("pe", lambda e, pi=pi, d=d, t0=t0, n=n, wk=wk: e.matmul(pi[:, 0:n], wk[d][1][0][:], xcb[:, t0:t0 + n], start=True, stop=True),
                                 r=[wk[d][1][1], Bxcb], w=[Bpi])
                            S.op("act", lambda e, pr=pr, t0=t0, n=n, col=col: e.activation(Rt[:, t0:t0 + n], pr[:, 0:n], AF.Sigmoid, bias=rba[:, col:col + 1], scale=1.0),
                                 r=[Bpr, Brba], w=[BR])
                            S.op("act", lambda e, pi=pi, t0=t0, n=n, col=col: e.activation(It[:, t0:t0 + n], pi[:, 0:n], AF.Sigmoid, bias=rbx[:, col:col + 1], scale=1.0),
                                 r=[Bpi, Brbx], w=[BI])
                        S.op("act", lambda e, col=col: e.activation(Rt[:], Rt[:], AF.Exp, scale=rnsp[:, col:col + 1]), r=[BR, Brnsp], w=[BR])
                        S.op("dve", lambda e, h=h: e.tensor_tensor(h[:], Rt[:], Rt[:], ALU.mult), r=[BR], w=[Bh])
                        S.op("act", lambda e, h=h: e.activation(h[:], h[:], AF.Sqrt, bias=1.0, scale=-1.0), r=[Bh], w=[Bh])
                        S.op("pool", lambda e: e.tensor_tensor(It[:], It[:], xc[:], ALU.mult), r=[BI, Bxc], w=[BI])
                        S.op("dve", lambda e, h=h: e.tensor_tensor(It[:], It[:], h[:], ALU.mult), r=[BI, Bh], w=[BI])
                        if d == 0:
                            S.op("dve", lambda e, h=h: e.tensor_tensor_scan(h[:], Rt[:], It[:], 0.0, ALU.mult, ALU.add), r=[BR, BI], w=[Bh])
                        else:
                            S.op("dve", lambda e, h=h: e.tensor_tensor_scan(rev_ap(h, 0, 128, NT, 0, NCTX), rev_ap(Rt, 0, 128, NT, 0, NCTX),
                                                                            rev_ap(It, 0, 128, NT, 0, NCTX), 0.0, ALU.mult, ALU.add), r=[BR, BI], w=[Bh])
                            S.op("dve", lambda e, h=h: e.tensor_tensor_scan(rev_ap(h, 0, 128, NT, NCTX, NL), rev_ap(Rt, 0, 128, NT, NCTX, NL),
                                                                            rev_ap(It, 0, 128, NT, NCTX, NL), h[:, 0:1], ALU.mult, ALU.add),
                                 r=[BR, BI, Bh], w=[Bh])
                    S.op("dve", lambda e: e.tensor_tensor(hh[0][0][:], hh[0][0][:], hh[1][0][:], ALU.add), r=[hh[0][1], hh[1][1]], w=[hh[0][1]])
                    S.op("dve", lambda e, g=g: e.tensor_tensor(yo[:], hh[0][0][:], g[:], ALU.mult), r=[hh[0][1], Bg], w=[Byo])
                    S.dma("sp", rT[k * 128:(k + 1) * 128, :], yo[:], Byo, r=[Byo])
                S.barrier()
                S.emit()
                S.release([b for _, b in xs] + [b for _, b in gs] + [Byo] + [wg[p][d][j][1] for p in range(2) for d in range(2) for j in range(2)])

        def phase_mlstm_prep():
            with ExitStack() as st:
                xs = [sbt(st, "mp_x%d" % i, [128, NT]) for i in range(2)]
                ys = [sbt(st, "mp_y%d" % i, [128, NT]) for i in range(2)]
                os_ = [sbt(st, "mp_o%d" % i, [128, NT], BF16) for i in range(2)]
                for j in range(16):
                    x, Bx = xs[j % 2]
                    y, By = ys[j % 2]
                    o, Bo = os_[j % 2]
                    S.dma("sp", x[:], uT[OFF["mq"] + j * 128:OFF["mq"] + (j + 1) * 128, :], Bx, w=[Bx])
                    conv_ops(x, Bx, y, By, mcw, Bmcw, lambda tap, j=j: mcw[:, tap * 16 + j:tap * 16 + j + 1], mcb, Bmcb, mcb[:, j:j + 1])
                    if j < 8:
                        S.op("act", lambda e, o=o, y=y: e.activation(o[:], y[:], AF.Silu), r=[By], w=[Bo])
                    else:
                        S.op("act", lambda e, y=y: e.activation(y[:], y[:], AF.Silu), r=[By], w=[By])
                        S.op("dve", lambda e, o=o, y=y: e.tensor_scalar(o[:], y[:], 1.0 / 16.0, None, ALU.mult), r=[By], w=[Bo])
                    S.dma("sp", qkm[j * 128:(j + 1) * 128, :], o[:], Bo, r=[Bo])
                S.barrier()
                S.emit()
                S.release([b for _, b in xs] + [b for _, b in os_])

        def phase_mlstm(l):
            with ExitStack() as st:
                nG = [sbt(st, "ml_nG%d" % d, [4, NT]) for d in range(2)]
                COL, BCOL = sbt(st, "ml_col", [128, TT, 32])
                SCB, BSCB = sbt(st, "ml_scb", [128, 8, TT])
                rows = {("nG", 0): nG[0], ("nG", 1): nG[1]}
                relg = []
                with ExitStack() as st2:
                    onesr, Bonesr = sbt(st2, "ml_onesr", [4, NT])
                    t_i = sbt(st2, "ml_ti", [4, NT])
                    t_f = sbt(st2, "ml_tf", [4, NT])
                    t_B = sbt(st2, "ml_tB", [4, NT])
                    t_A = sbt(st2, "ml_tA", [4, NT])
                    t_ws = sbt(st2, "ml_tws", [4, NT])
                    ge, Bge = sbt(st2, "ml_gend", [4, TT])
                    gp, Bgp = sbt(st2, "ml_gprev", [4, TT])
                    sct, Bsct = sbt(st2, "ml_sc", [4, TT])
                    relg = [t_i[1], t_f[1]]
                    S.op("pool", lambda e: e.memset(onesr[:], 1.0), w=[Bonesr])
                    mg0 = OFF["mg"]
                    for d in range(2):
                        it, Bi = t_i
                        ft, Bf = t_f
                        Bt_, BB = t_B
                        At, BA = t_A
                        nGt, BnG = nG[d]
                        E1t, BE1 = t_f
                        wit, Bwi = t_B
                        wst, Bws = t_ws
                        S.dma("sp", it[:], uT[mg0 + d * 8:mg0 + d * 8 + 4, :], Bi, w=[Bi])
                        S.dma("sp", ft[:], uT[mg0 + d * 8 + 4:mg0 + d * 8 + 8, :], Bf, w=[Bf])
                        S.op("dve", lambda e, d=d: e.tensor_scalar(it[:], it[:], mgb[:, d * 2:d * 2 + 1], None, ALU.add), r=[Bi, Bmgb], w=[Bi])
                        S.op("dve", lambda e, d=d: e.tensor_scalar(ft[:], ft[:], mgb[:, d * 2 + 1:d * 2 + 2], None, ALU.add), r=[Bf, Bmgb], w=[Bf])
                        S.op("act", lambda e: e.activation(ft[:], ft[:], AF.Exp, scale=-1.0), r=[Bf], w=[Bf])
                        S.op("act", lambda e: e.activation(ft[:], ft[:], AF.Ln, bias=1.0), r=[Bf], w=[Bf])
                        S.op("dve", lambda e: e.tensor_scalar(ft[:], ft[:], -1.0, None, ALU.mult), r=[Bf], w=[Bf])
                        if d == 0:
                            S.op("dve", lambda e: e.tensor_tensor_scan(Bt_[:], onesr[:], ft[:], 0.0, ALU.mult, ALU.add), r=[Bf, Bonesr], w=[BB])
                        else:
                            S.op("dve", lambda e: e.tensor_tensor_scan(rev_ap(Bt_, 0, 4, NT, 0, NCTX), rev_ap(onesr, 0, 4, NT, 0, NCTX),
                                                                       rev_ap(ft, 0, 4, NT, 0, NCTX), 0.0, ALU.mult, ALU.add), r=[Bf, Bonesr], w=[BB])
                            S.op("dve", lambda e: e.tensor_tensor_scan(rev_ap(Bt_, 0, 4, NT, NCTX, NL), rev_ap(onesr, 0, 4, NT, NCTX, NL),
                                                                       rev_ap(ft, 0, 4, NT, NCTX, NL), Bt_[:, 0:1], ALU.mult, ALU.add),
                                 r=[Bf, Bonesr, BB], w=[BB])
                        S.op("dve", lambda e: e.tensor_tensor(At[:], it[:], Bt_[:], ALU.subtract), r=[Bi, BB], w=[BA])
                        Gt, BGt = it, Bi
                        if d == 0:
                            S.op("dve", lambda e: e.tensor_tensor_scan(Gt[:], At[:], At[:], 0.0, ALU.max, ALU.max), r=[BA], w=[BGt])
                        else:
                            S.op("dve", lambda e: e.tensor_tensor_scan(rev_ap(Gt, 0, 4, NT, 0, NCTX), rev_ap(At, 0, 4, NT, 0, NCTX),
                                                                       rev_ap(At, 0, 4, NT, 0, NCTX), 0.0, ALU.max, ALU.max), r=[BA], w=[BGt])
                            S.op("dve", lambda e: e.tensor_tensor_scan(rev_ap(Gt, 0, 4, NT, NCTX, NL), rev_ap(At, 0, 4, NT, NCTX, NL),
                                                                       rev_ap(At, 0, 4, NT, NCTX, NL), Gt[:, 0:1], ALU.max, ALU.max),
                                 r=[BA, BGt], w=[BGt])
                        S.op("dve", lambda e, nGt=nGt: e.tensor_scalar(nGt[:], Gt[:], -1.0, None, ALU.mult), r=[BGt], w=[BnG])
                        S.op("dve", lambda e, nGt=nGt: e.tensor_tensor(E1t[:], nGt[:], Bt_[:], ALU.subtract), r=[BnG, BB], w=[BE1])
                        S.op("act", lambda e: e.activation(E1t[:], E1t[:], AF.Exp), r=[BE1], w=[BE1])
                        endoff = 127 if d == 0 else 0
                        S.op("dve", lambda e, endoff=endoff: e.tensor_copy(ge[:], bass.AP(Gt, endoff, [[NT, 4], [128, TT]])), r=[BGt], w=[Bge])
                        S.op("dve", lambda e: e.memset(gp[:], 0.0), w=[Bgp])
                        if d == 0:
                            S.op("dve", lambda e: e.tensor_copy(gp[:, 1:TT], ge[:, 0:TT - 1]), r=[Bge], w=[Bgp])
                        else:
                            S.op("dve", lambda e: e.tensor_copy(gp[:, 0:1], ge[:, 1:2]), r=[Bge], w=[Bgp])
                            S.op("dve", lambda e: e.tensor_copy(gp[:, TT - 1:TT], ge[:, 0:1]), r=[Bge], w=[Bgp])
                            if TT > 3:
                                S.op("dve", lambda e: e.tensor_copy(gp[:, 2:TT - 1], ge[:, 3:TT]), r=[Bge], w=[Bgp])
                        S.op("dve", lambda e, nGt=nGt: e.tensor_tensor(
                            wit[:].rearrange("p (c t) -> p c t", t=128), nGt[:].rearrange("p (c t) -> p c t", t=128),
                            bass.AP(gp, 0, [[TT, 4], [1, TT], [0, 128]]), ALU.add), r=[BnG, Bgp, BB], w=[Bwi])
                        S.op("act", lambda e: e.activation(wit[:], wit[:], AF.Exp), r=[Bwi], w=[Bwi])
                        S.op("dve", lambda e: e.tensor_tensor(
                            wst[:].rearrange("p (c t) -> p c t", t=128), At[:].rearrange("p (c t) -> p c t", t=128),
                            bass.AP(ge, 0, [[TT, 4], [1, TT], [0, 128]]), ALU.subtract), r=[BA, Bge], w=[Bws])
                        S.op("act", lambda e: e.activation(wst[:], wst[:], AF.Exp), r=[Bws], w=[Bws])
                        S.op("dve", lambda e: e.tensor_tensor(sct[:], gp[:], ge[:], ALU.subtract), r=[Bgp, Bge], w=[Bsct])
                        S.op("act", lambda e: e.activation(sct[:], sct[:], AF.Exp), r=[Bsct], w=[Bsct])
                        for h in range(4):
                            p, Bp = PS[h % 2]
                            S.op("pe", lambda e, p=p, h=h: e.matmul(p[:, 0:TT], sel[0:4, h * 128:(h + 1) * 128], sct[:], start=True, stop=True),
                                 r=[Bsel, Bsct], w=[Bp])
                            S.op("dve", lambda e, p=p, h=h, d=d: e.tensor_copy(SCB[:, d * 4 + h, :], p[:, 0:TT]), r=[Bp], w=[BSCB])
                        for c in range(TT):
                            p, Bp = PS[2 + c % 2]
                            for qi, (t, Bt) in enumerate((t_A, t_ws, t_B, t_f)):
                                o0 = qi * 4
                                S.op("pe", lambda e, p=p, t=t, c=c, o0=o0: e.transpose(p[:, o0:o0 + 4], t[0:4, c * 128:(c + 1) * 128], ident[0:4, 0:4]),
                                     r=[Bt, Bident], w=[Bp])
                            S.op("dve", lambda e, p=p, c=c, d=d: e.tensor_copy(COL[:, c, d * 16:(d + 1) * 16], p[:, 0:16]), r=[Bp], w=[BCOL])
                    S.barrier()
                    S.emit()

                qTt, BqT = sbt(st, "ml_q", [128, 2, NT], BF16)
                kTt, BkT = sbt(st, "ml_k", [128, 2, NT], BF16)
                ktok, Bktok = sbt(st, "ml_ktok", [128, TT, 256], BF16)
                V, BV = sbt(st, "ml_v", [128, TT, 257], BF16)
                hacc, Bhacc = sbt(st, "ml_hacc", [128, TT, 256])
                mos = [sbt(st, "ml_mo%d" % i, [128, 256]) for i in range(2)]
                mout, Bmout = sbt(st, "ml_mout", [128, 2, NT], BF16)
                C32, BC32 = sbt(st, "ml_c32", [128, 2, 257])
                Cb, BCb = sbt(st, "ml_cb", [128, 2, 257], BF16)
                arg, Barg = sbt(st, "ml_arg", [128, 128])
                SD, BSD = sbt(st, "ml_sd", [128, 128], BF16)
                intras = [sbt(st, "ml_intra%d" % i, [128, 257]) for i in range(3)]
                nd, Bnd = sbt(st, "ml_nd", [128, 257])
                rden, Brden = sbt(st, "ml_rden", [128, 1])
                vws = [sbt(st, "ml_vw%d" % i, [128, 257], BF16) for i in range(2)]
                ssq, Bssq = sbt(st, "ml_ssq", [128, 1])
                junk, Bjunk = sbt(st, "ml_junk", [128, 256])
                hns = [sbt(st, "ml_hn%d" % i, [128, 256]) for i in range(2)]
                sgs = [sbt(st, "ml_sg%d" % i, [128, 256]) for i in range(2)]
                SSQ, BSSQ = sbt(st, "ml_SSQ", [128, TT])
                NGB = [sbt(st, "ml_ngb%d" % i, [128, 512]) for i in range(2)]
                Bkt7 = PS[7][1]
                def ml_load_head(h):
                    for j in range(2):
                        S.dma("sp", qTt[:, j, :], qkm[(h * 2 + j) * 128:(h * 2 + j + 1) * 128, :], BqT, w=[BqT])
                        S.dma("sp", kTt[:, j, :], qkm[1024 + (h * 2 + j) * 128:1024 + (h * 2 + j + 1) * 128, :], BkT, w=[BkT])
                    S.dma("pool", V[:, :, 0:256], vtok[:, 1024 + h * 256:1024 + (h + 1) * 256].rearrange("(t p) c -> p t c", p=128), BV, w=[BV])
                    S.op("dve", lambda e: e.memset(V[:, :, 256:257], 1.0), w=[BV])

                ml_load_head(0)
                for h in range(4):
                    def ktok_transpose(c):
                        p, Bp = PS[7]
                        pb = p[:].bitcast(BF16)
                        for j in range(2):
                            S.op("pe", lambda e, j=j: e.transpose(pb[:, 512 + j * 128:512 + (j + 1) * 128], kTt[:, j, c * 128:(c + 1) * 128], identb[:]),
                                 r=[BkT, Bidentb], w=[Bkt7])
                        S.op("act", lambda e: e.copy(ktok[:, c, :], pb[:, 512:768]), r=[Bkt7], w=[Bktok])
                    for d in range(2):
                        order = list(range(TT)) if d == 0 else [1, 0] + list(range(TT - 1, 1, -1))
                        nGt, BnG = nG[d]
                        S.op("dve", lambda e: e.memset(C32[:], 0.0), w=[BC32])
                        S.op("dve", lambda e: e.memset(Cb[:], 0.0), w=[BCb])
                        ngb_state = {"g": None, "n": 0, "t": None}
                        cb_ = d * 16

                        def st_a1(step, d=d, order=order, nGt=nGt, BnG=BnG, ngb_state=ngb_state, cb_=cb_, h=h):
                            c = order[step]
                            g4 = c // 4
                            if ngb_state["g"] != g4:
                                ngt, Bng = NGB[ngb_state["n"] % 2]
                                ngb_state["n"] += 1
                                ngb_state["g"] = g4
                                ngb_state["t"] = (ngt, Bng)
                                n4 = min(512, NT - g4 * 512)
                                p, Bp = PS[7]
                                S.op("pe", lambda e: e.matmul(p[:, 0:n4], sel[0:4, h * 128:(h + 1) * 128],
                                                              nGt[0:4, g4 * 512:g4 * 512 + n4], start=True, stop=True),
                                     r=[Bsel, BnG], w=[Bp])
                                S.op("act", lambda e: e.copy(ngt[:, 0:n4], p[:, 0:n4]), r=[Bp], w=[Bng])
                            ngt, Bng = ngb_state["t"]
                            o4 = (c % 4) * 128
                            t0 = c * 128
                            pst, Bpst = PS[0]
                            for j in range(2):
                                S.op("pe", lambda e, j=j: e.matmul(pst[:, 0:128], kTt[:, j, t0:t0 + 128], qTt[:, j, t0:t0 + 128],
                                                                   start=(j == 0), stop=(j == 1)), r=[BkT, BqT], w=[Bpst])
                            S.op("dve", lambda e: e.scalar_tensor_tensor(
                                arg[:], ngt[:, o4:o4 + 128], COL[:, c, cb_ + h:cb_ + h + 1], maskt[:, d, :], ALU.add, ALU.min),
                                r=[Bng, BCOL, Bmask], w=[Barg])
                            S.op("act", lambda e: e.activation(arg[:], arg[:], AF.Exp), r=[Barg], w=[Barg])

                        def st_a1b(step, order=order):
                            c = order[step]
                            pst, Bpst = PS[0]
                            S.op("dve", lambda e: e.tensor_tensor(SD[:], pst[:, 0:128], arg[:], ALU.mult), r=[Bpst, Barg], w=[BSD])
                            pin, Bpin = PS[1]
                            S.op("pe", lambda e: e.matmul(pin[:, 0:257], SD[:], V[:, c, :], start=True, stop=True), r=[BSD, BV], w=[Bpin])
                            it_, Bit_ = intras[step % 3]
                            S.op("act", lambda e: e.copy(it_[:], pin[:, 0:257]), r=[Bpin], w=[Bit_])

                        def st_vw(step, order=order, cb_=cb_, h=h, d=d):
                            c = order[step]
                            if d == 0:
                                ktok_transpose(c)
                            vw, Bvw = vws[step % 2]
                            S.op("act", lambda e: e.activation(vw[:], V[:, c, :], AF.Copy, scale=COL[:, c, cb_ + 4 + h:cb_ + 4 + h + 1]),
                                 r=[BV, BCOL], w=[Bvw])

                        def st_pu(step, order=order):
                            c = order[step]
                            par = step % 2
                            vw, Bvw = vws[par]
                            for j in range(2):
                                pu, Bpu = PS[3 + 2 * par + j]
                                S.op("pe", lambda e, pu=pu, j=j: e.matmul(pu[:, 0:257], ktok[:, c, j * 128:(j + 1) * 128], vw[:], start=True, stop=True),
                                     r=[Bktok, Bvw], w=[Bpu])

                        def st_pit(step, order=order):
                            c = order[step]
                            t0 = c * 128
                            pit, Bpit = PS[2]
                            for j in range(2):
                                S.op("pe", lambda e, j=j: e.matmul(pit[:, 0:257], qTt[:, j, t0:t0 + 128], Cb[:, j, :],
                                                                   start=(j == 0), stop=(j == 1)), r=[BqT, BCb], w=[Bpit])

                        def st_b(step, d=d, order=order, cb_=cb_, h=h):
                            c = order[step]
                            par = step % 2
                            pit, Bpit = PS[2]
                            for j in range(2):
                                pu, Bpu = PS[3 + 2 * par + j]
                                S.op("dve", lambda e, pu=pu, j=j: e.scalar_tensor_tensor(
                                    C32[:, j, :], C32[:, j, :], SCB[:, d * 4 + h, c:c + 1], pu[:, 0:257], ALU.mult, ALU.add),
                                    r=[BC32, BSCB, Bpu], w=[BC32])
                            S.op("act", lambda e: e.copy(Cb[:], C32[:]), r=[BC32], w=[BCb])
                            it_, Bit_ = intras[step % 3]
                            S.op("dve", lambda e: e.scalar_tensor_tensor(
                                nd[:], pit[:, 0:257], COL[:, c, cb_ + 8 + h:cb_ + 8 + h + 1], it_[:], ALU.mult, ALU.add),
                                r=[Bpit, BCOL, Bit_], w=[Bnd])
                            S.op("dve", lambda e: e.tensor_tensor(
                                rden[:], nd[:, 256:257], COL[:, c, cb_ + 12 + h:cb_ + 12 + h + 1], ALU.max),
                                r=[Bnd, BCOL], w=[Brden])
                            S.op("dve", lambda e: e.scalar_tensor_tensor(rden[:], nd[:, 256:257], -1.0, rden[:], ALU.mult, ALU.max),
                                 r=[Bnd, Brden], w=[Brden])
                            S.op("dve", lambda e: e.reciprocal(rden[:], rden[:]), r=[Brden], w=[Brden])
                            if d == 0:
                                S.op("act", lambda e: e.activation(hacc[:, c, :], nd[:, 0:256], AF.Copy, scale=rden[:, 0:1]),
                                     r=[Bnd, Brden], w=[Bhacc])
                            else:
                                S.op("dve", lambda e: e.scalar_tensor_tensor(hacc[:, c, :], nd[:, 0:256], rden[:, 0:1], hacc[:, c, :], ALU.mult, ALU.add),
                                     r=[Bnd, Brden, Bhacc], w=[Bhacc])

                        st_a1(0); st_a1b(0); st_vw(0); st_pu(0)
                        if TT > 1:
                            st_a1(1); st_a1b(1)
                        for step in range(TT):
                            st_pit(step)
                            if step + 2 < TT:
                                st_a1(step + 2)
                            if step + 1 < TT:
                                st_vw(step + 1)
                                st_pu(step + 1)
                            st_b(step)
                            if step + 2 < TT:
                                st_a1b(step + 2)
                    if h + 1 < 4:
                        ml_load_head(h + 1)
                    for c in range(TT):
                        S.op("dve", lambda e, c=c: e.scalar_tensor_tensor(junk[:], hacc[:, c, :], 1.0, hacc[:, c, :], ALU.mult, ALU.mult,
                                                                          accum_out=SSQ[:, c:c + 1]), r=[Bhacc], w=[Bjunk, BSSQ])
                    S.op("act", lambda e: e.activation(SSQ[:], SSQ[:], AF.Sqrt, bias=epsc[:], scale=1.0 / 256), r=[BSSQ, Bepsc], w=[BSSQ])
                    S.op("dve", lambda e: e.reciprocal(SSQ[:], SSQ[:]), r=[BSSQ], w=[BSSQ])
                    for c in range(TT):
                        hn, Bhn = hns[c % 2]
                        sg_, Bsg_ = sgs[c % 2]
                        S.op("dve", lambda e, c=c, h=h, hn=hn: e.scalar_tensor_tensor(hn[:], hacc[:, c, :], SSQ[:, c:c + 1], mng[:, h * 256:(h + 1) * 256], ALU.mult, ALU.mult),
                             r=[Bhacc, BSSQ, Bmng], w=[Bhn])
                        mo, Bmo = mos[c % 2]
                        S.dma("sp", mo[:], vtok[c * 128:(c + 1) * 128, 2048 + h * 256:2048 + (h + 1) * 256], Bmo, w=[Bmo])
                        S.op("act", lambda e, mo=mo, sg_=sg_: e.activation(sg_[:], mo[:], AF.Sigmoid), r=[Bmo], w=[Bsg_])
                        S.op("pool", lambda e, hn=hn, sg_=sg_: e.tensor_tensor(hn[:], hn[:], sg_[:], ALU.mult), r=[Bhn, Bsg_], w=[Bhn])
                        ptr, Bptr = PS[6 + c % 2]
                        for vc in range(2):
                            S.op("pe", lambda e, ptr=ptr, vc=vc, hn=hn: e.transpose(ptr[:, vc * 128:(vc + 1) * 128], hn[:, vc * 128:(vc + 1) * 128], ident[:]),
                                 r=[Bhn, Bident], w=[Bptr])
                        S.op("act", lambda e, ptr=ptr, c=c: e.copy(mout[:, :, c * 128:(c + 1) * 128], ptr[:, 0:256].rearrange("p (v t) -> p v t", v=2)),
                             r=[Bptr], w=[Bmout])
                    for vc in range(2):
                        S.dma("sp", mT[h * 256 + vc * 128:h * 256 + (vc + 1) * 128, :], mout[:, vc, :], Bmout, r=[Bmout])
                S.barrier()
                S.emit()
                S.release(relg + [BqT, BkT, BV, Bmout] + [b for _, b in mos])

        def resid_epilogue(gtj):
            def setup(st):
                hb = [sbt(st, "re_h%d_%d" % (gtj, i), [128, 512]) for i in range(3)]
                return dict(hb=hb, bufs=[b for _, b in hb])

            def epi(cx, sgi, t0sg, nsg, c0, mi, mw, g0, n, pss, gidx):
                p, Bp = pss[0]
                m = (c0 // 128) + mi
                hb, Bhb = cx["hb"][gidx % 3]
                S.dma("act", hb[:, 0:n], hT[m * 128:(m + 1) * 128, g0:g0 + n], Bhb, w=[Bhb])
                for (s0, sn, r_) in segs(g0, n):
                    a0 = s0 - g0
                    S.op("dve", lambda e, a0=a0, sn=sn, r_=r_: e.scalar_tensor_tensor(
                        hb[:, a0:a0 + sn], p[:, a0:a0 + sn], modT[:, gtj * 16 + m, r_:r_ + 1], hb[:, a0:a0 + sn], ALU.mult, ALU.add),
                        r=[Bp, BmodT, Bhb], w=[Bhb])
                S.dma("sp", hT[m * 128:(m + 1) * 128, g0:g0 + n], hb[:, 0:n], Bhb, r=[Bhb])
            return setup, epi

        def phase_merge(l):
            SG = 1536
            bg0 = OFF["bg"]

            def setup(st):
                gts = [[sbt(st, "mg_g%d_%d" % (i, j), [128, 512]) for j in range(3)] for i in range(2)]
                t1 = sbt(st, "mg_t1", [128, 512])
                t2 = sbt(st, "mg_t2", [128, 512])
                zo = [sbt(st, "mg_zo%d" % i, [128, SG], BF16) for i in range(2)]
                return dict(gts=gts, t1=t1, t2=t2, zo=zo, k=0,
                            bufs=[b for row in gts for _, b in row] + [b for _, b in zo])

            def epi(cx, sgi, t0sg, nsg, c0, mi, mw, g0, n, pss, gidx):
                m = (c0 // 128) + mi
                if g0 == t0sg:
                    cx["k"] += 1
                zo, Bzo = cx["zo"][cx["k"] % 2]
                gts = cx["gts"][gidx % 2]
                t1, Bt1 = cx["t1"]
                t2, Bt2 = cx["t2"]
                for j in range(3):
                    gt_, Bgt = gts[j]
                    r0 = bg0 + j * 2048 + m * 128
                    S.dma("sp", gt_[:, 0:n], uT[r0:r0 + 128, g0:g0 + n], Bgt, w=[Bgt])
                    S.op("act", lambda e, gt_=gt_: e.activation(gt_[:, 0:n], gt_[:, 0:n], AF.Sigmoid), r=[Bgt], w=[Bgt])
                a0 = g0 - t0sg
                S.op("dve", lambda e: e.tensor_tensor(t1[:, 0:n], pss[0][0][:, 0:n], gts[0][0][:, 0:n], ALU.mult), r=[pss[0][1], gts[0][1]], w=[Bt1])
                S.op("dve", lambda e: e.tensor_tensor(t2[:, 0:n], pss[1][0][:, 0:n], gts[1][0][:, 0:n], ALU.mult), r=[pss[1][1], gts[1][1]], w=[Bt2])
                S.op("dve", lambda e: e.tensor_tensor(t1[:, 0:n], t1[:, 0:n], t2[:, 0:n], ALU.add), r=[Bt1, Bt2], w=[Bt1])
                S.op("dve", lambda e: e.tensor_tensor(t2[:, 0:n], pss[2][0][:, 0:n], gts[2][0][:, 0:n], ALU.mult), r=[pss[2][1], gts[2][1]], w=[Bt2])
                S.op("dve", lambda e: e.tensor_tensor(zo[:, a0:a0 + n], t1[:, 0:n], t2[:, 0:n], ALU.add), r=[Bt1, Bt2], w=[Bzo])
                if g0 + n == t0sg + nsg:
                    S.dma("sp", zT[m * 128:(m + 1) * 128, t0sg:t0sg + nsg], zo[:, 0:nsg], Bzo, r=[Bzo])

            gemm_fm("mg", [(aT, 8), (rT, 8), (mT, 8)],
                    [(0, W["w_branch_attn"][l]), (1, W["w_branch_rnn"][l]), (2, W["w_branch_ml"][l])],
                    [(c, 512) for c in range(0, D, 512)], SG, epi, setup)
            setup2, epi2 = resid_epilogue(2)
            gemm_fm("op", [(zT, KT)], [(0, W["w_out"][l])], [(c, 512) for c in range(0, D, 512)], 2176, epi2, setup2)

        def phase_ffn(l):
            SG = 2176

            def setup(st):
                s1 = [sbt(st, "ff_s%d" % i, [128, 512]) for i in range(2)]
                ao = [sbt(st, "ff_ao%d" % i, [128, SG], BF16) for i in range(2)]
                return dict(s1=s1, ao=ao, k=0, bufs=[b for _, b in ao])

            def epi(cx, sgi, t0sg, nsg, c0, mi, mw, g0, n, pss, gidx):
                m = (c0 // 128) + mi
                if g0 == t0sg:
                    cx["k"] += 1
                ao, Bao = cx["ao"][cx["k"] % 2]
                s1, Bs1 = cx["s1"][gidx % 2]
                a0 = g0 - t0sg
                S.op("act", lambda e: e.activation(s1[:, 0:n], pss[0][0][:, 0:n], AF.Silu), r=[pss[0][1]], w=[Bs1])
                S.op("dve", lambda e: e.tensor_tensor(ao[:, a0:a0 + n], pss[1][0][:, 0:n], s1[:, 0:n], ALU.mult), r=[pss[1][1], Bs1], w=[Bao])
                if g0 + n == t0sg + nsg:
                    S.dma("sp", actT[m * 128:(m + 1) * 128, t0sg:t0sg + nsg], ao[:, 0:nsg], Bao, r=[Bao])

            gemm_fm("f1", [(xmT, KT)], [(0, W["w_ffn1"][l]), (0, W["w_ffn3"][l])], [(c, 512) for c in range(0, DFF, 512)], SG, epi, setup)
            setup2, epi2 = resid_epilogue(5)
            gemm_fm("f2", [(actT, 44)], [(0, W["w_ffn2"][l])], [(c, 256) for c in range(0, D, 256)], 1152, epi2, setup2, wwidth=256)

        def phase_out():
            with ExitStack() as st:
                hin = [sbt(st, "po_h%d" % i, [128, KT, 128]) for i in range(2)]
                ot = [sbt(st, "po_o%d" % i, [128, D]) for i in range(2)]
                for ti in range(NL // 128):
                    h, Bh = hin[ti % 2]
                    o, Bo = ot[ti % 2]
                    tk = NCTX + ti * 128
                    S.dma("sp", h[:], hT[:, tk:tk + 128].rearrange("(k p) t -> p k t", p=128), Bh, w=[Bh])
                    for q in range(4):
                        p, Bp = PS[(ti * 4 + q) % 4]
                        for j in range(4):
                            kt = q * 4 + j
                            S.op("pe", lambda e, p=p, j=j, kt=kt, h=h: e.transpose(p[:, j * 128:(j + 1) * 128], h[:, kt, :], ident[:]),
                                 r=[Bh, Bident], w=[Bp])
                        if q % 2 == 0:
                            S.op("dve", lambda e, p=p, q=q, o=o: e.tensor_copy(o[:, q * 512:(q + 1) * 512], p[:]), r=[Bp], w=[Bo])
                        else:
                            S.op("act", lambda e, p=p, q=q, o=o: e.copy(o[:, q * 512:(q + 1) * 512], p[:]), r=[Bp], w=[Bo])
                    S.dma("sp", out_d[ti * 128:(ti + 1) * 128, :], o[:], Bo, r=[Bo])
                S.barrier()
                S.emit()

        identb, Bidentb = sbt(gst, "identb", [128, 128], BF16)
        S.op("dve", lambda e: e.tensor_copy(identb[:], ident[:]), r=[Bident], w=[Bidentb])

        for l in range(DEPTH):
            phase_params(l)
            phase_norm(G1, BG1, 0)
            phase_inproj(l)
            phase_attn_prep()
            phase_attn()
            phase_rglru(l)
            phase_mlstm_prep()
            phase_mlstm(l)
            phase_merge(l)
            phase_norm(G2, BG2, 3)
            phase_ffn(l)
        phase_out()
        print("total ops recorded:", S.nops, "dma sems:", len(S.all_dsems))
    return nc


def host_tables(NL):
    ident = np.eye(128, dtype=np.float32)
    R = np.zeros((128, 128), np.float32)
    for j in range(32):
        R[j, j + 32] = -1.0
        R[j + 32, j] = 1.0
        R[j + 64, j + 96] = -1.0
        R[j + 96, j + 64] = 1.0
    rotT = np.ascontiguousarray(R.T)
    rows = NL // 64
    row = np.repeat(np.arange(rows, dtype=np.float32), 64)
    col = np.tile(np.arange(64, dtype=np.float32), rows)
    inv = (np.float32(10000.0) ** (-np.arange(32, dtype=np.float32) / np.float32(32))).astype(np.float32)
    ar = row[:, None] * inv
    ac = col[:, None] * inv
    ang = np.concatenate([ar, ar, ac, ac], axis=-1).astype(np.float32)
    cosT = np.ascontiguousarray(np.cos(ang).T.astype(np.float32))
    sinT = np.ascontiguousarray(np.sin(ang).T.astype(np.float32))
    s = np.arange(128)[:, None]
    t = np.arange(128)[None, :]
    mask = np.stack([np.where(s <= t, 0.0, NEG), np.where(s >= t, 0.0, NEG)]).astype(np.float32)
    sel = np.zeros((4, 512), np.float32)
    for j in range(4):
        sel[j, j * 128:(j + 1) * 128] = 1.0
    return dict(k_ident=ident, k_rot=rotT, k_cos=cosT, k_sin=sinT, k_mask=mask, k_sel=sel)


WNAMES = ["w_mod", "b_mod", "norm1_g", "norm2_g", "w_in", "attn_qnorm_g", "attn_knorm_g", "attn_lambda", "attn_subln_g",
          "rnn_conv_w", "rnn_conv_b", "rnn_wa", "rnn_ba", "rnn_wx", "rnn_bx", "rnn_lambda", "ml_conv_w", "ml_conv_b",
          "ml_gate_b", "ml_norm_g", "w_branch_attn", "w_branch_rnn", "w_branch_ml", "w_out", "w_ffn1", "w_ffn3", "w_ffn2"]


def run(inputs, dbg=(), n_cores=None):
    x = np.asarray(inputs["x"], np.float32)
    B, NL, _ = x.shape
    DEPTH = inputs["w_mod"].shape[0]
    nc = build(NL, DEPTH, dbg)
    tabs = host_tables(NL)
    wd = {k: np.ascontiguousarray(np.asarray(inputs[k], np.float32)) for k in WNAMES}
    in_maps = []
    ncores = B if n_cores is None else n_cores
    for b in range(ncores):
        m = dict(wd)
        m.update(tabs)
        m["x"] = np.ascontiguousarray(x[b])
        m["ctx"] = np.ascontiguousarray(np.asarray(inputs["ctx"], np.float32)[b])
        m["c2"] = np.ascontiguousarray(np.stack([np.asarray(inputs["c"], np.float32)[b], np.asarray(inputs["c_ctx"], np.float32)]))
        in_maps.append(m)
    res = run_bass_kernel_spmd(nc, in_maps, core_ids=list(range(ncores)))
    return res


def kernel(**inputs):
    res = run(inputs)
    return np.stack([r["out"] for r in res.results], axis=0).astype(np.float32)
```
